# Optimizing a Trainium2 kernel written in Bass

```python
import math
import jax, jax.numpy as jnp
from jax import lax
import numpy as np

D_MODEL = 1024
BATCH = 16
SEQ = 2048
DEPTH = 2

GRID_W = 64
CTX_LEN = 256
N_EVEN = (DEPTH + 1) // 2
N_ODD = DEPTH // 2
MIX_WIDTH = D_MODEL
DA_HEADS = 4
DA_DK = 64
DA_DV = 2 * DA_DK
MLA_HEADS = 4
MLA_NOPE = 128
MLA_ROPE = 64
MLA_V = 128
MLA_Q_LORA = 256
MLA_KV_LORA = 128
EV_IN = 3 * DA_HEADS * 2 * DA_DK + MLA_Q_LORA + MLA_KV_LORA + MLA_ROPE
NA_HEADS = 16
NA_DH = 64
NA_KH = 8
NA_KW = 16
D_FF = -(-8 * D_MODEL // (3 * 256)) * 256
ROPE_THETA = 10000.0
Q_BLOCK = 128
EPS = 1e-6
NEG_INF = -1e30

kernel_name = "hybrid_diffattn_mla_natten_prefix_block"


def rms_norm(x, g):
    xf = x.astype(jnp.float32)
    y = xf * lax.rsqrt(jnp.mean(xf * xf, axis=-1, keepdims=True) + EPS)
    return (y * g.astype(jnp.float32)).astype(x.dtype)


def softmax_f32(s):
    return jax.nn.softmax(s.astype(jnp.float32), axis=-1)


def adaln(cond, mod_w, mod_b):
    return jnp.split(jax.nn.silu(cond) @ mod_w + mod_b, 6, axis=-1)


def modulate(h, shift, scale):
    return h * (1 + scale) + shift


def swiglu(h, w_in, w_out):
    g, u = jnp.split(h @ w_in, 2, axis=-1)
    return (jax.nn.silu(g) * u) @ w_out


def axial_rope_tables(S, dim):
    t = jnp.arange(S)
    row = (t // GRID_W).astype(jnp.float32)
    col = (t % GRID_W).astype(jnp.float32)
    half = dim // 2
    inv = ROPE_THETA ** (-jnp.arange(0, half, 2, dtype=jnp.float32) / half)
    ar = row[:, None] * inv
    ac = col[:, None] * inv
    return (jnp.cos(ar), jnp.sin(ar), jnp.cos(ac), jnp.sin(ac))


def _rotate(x, cos, sin):
    x1, x2 = jnp.split(x, 2, axis=-1)
    return jnp.concatenate([x1 * cos - x2 * sin, x1 * sin + x2 * cos], axis=-1)


def apply_axial_rope(x, tables):
    cr, sr, cc, sc = tables
    xr, xc = jnp.split(x, 2, axis=-1)
    return jnp.concatenate([_rotate(xr, cr, sr), _rotate(xc, cc, sc)], axis=-1).astype(x.dtype)


def sweep_query_blocks(fn, *qs):
    S = qs[0].shape[-2]
    nb = S // Q_BLOCK
    blocks = tuple(jnp.moveaxis(q.reshape(q.shape[:-2] + (nb, Q_BLOCK, q.shape[-1])), -3, 0) for q in qs)
    out = lax.map(lambda a: fn(*a), blocks)
    out = jnp.moveaxis(out, 0, -3)
    return out.reshape(out.shape[:-3] + (S, out.shape[-1]))


def merge_heads(*outs):
    return jnp.concatenate([o.transpose(0, 2, 1, 3).reshape(o.shape[0], o.shape[2], -1) for o in outs], axis=-1)


def even_project(h, w_in, da_q_g, da_k_g, mla_q_a_g, mla_w_uq, mla_kv_a_g, mla_w_ukv, mla_q_g, mla_k_g, mla_kr_g):
    B, T, _ = h.shape
    n_dq = DA_HEADS * 2 * DA_DK
    idx = [n_dq, 2 * n_dq, 3 * n_dq, 3 * n_dq + MLA_Q_LORA, 3 * n_dq + MLA_Q_LORA + MLA_KV_LORA]
    dq, dk, dv, cq, ckv, kr = jnp.split(h @ w_in, idx, axis=-1)
    dq = rms_norm(dq.reshape(B, T, DA_HEADS, 2, DA_DK), da_q_g).transpose(0, 2, 3, 1, 4)
    dk = rms_norm(dk.reshape(B, T, DA_HEADS, 2, DA_DK), da_k_g).transpose(0, 2, 3, 1, 4)
    dv = dv.reshape(B, T, DA_HEADS, DA_DV).transpose(0, 2, 1, 3)
    q = (rms_norm(cq, mla_q_a_g) @ mla_w_uq).reshape(B, T, MLA_HEADS, MLA_NOPE + MLA_ROPE)
    q = rms_norm(q, mla_q_g).transpose(0, 2, 1, 3)
    qn, qr = q[..., :MLA_NOPE], q[..., MLA_NOPE:]
    kv = (rms_norm(ckv, mla_kv_a_g) @ mla_w_ukv).reshape(B, T, MLA_HEADS, MLA_NOPE + MLA_V)
    kn = rms_norm(kv[..., :MLA_NOPE], mla_k_g).transpose(0, 2, 1, 3)
    mv = kv[..., MLA_NOPE:].transpose(0, 2, 1, 3)
    kr = rms_norm(kr, mla_kr_g)
    return dq, dk, dv, qn, qr, kn, kr, mv


def diff_attend(q, k, v, lam, lam_init, out_g):
    s = jnp.einsum('bhiqd,bhikd->bhiqk', q, k) * (DA_DK ** -0.5)
    p = softmax_f32(s)
    a = p[:, :, 0] - lam * p[:, :, 1]
    o = jnp.einsum('bhqk,bhkd->bhqd', a.astype(v.dtype), v)
    return rms_norm(o, out_g) * (1.0 - lam_init)


def mla_attend(qn, qr, kn, kr, v):
    s = (jnp.einsum('bhqd,bhkd->bhqk', qn, kn) + jnp.einsum('bhqd,bkd->bhqk', qr, kr)) * ((MLA_NOPE + MLA_ROPE) ** -0.5)
    p = softmax_f32(s)
    return jnp.einsum('bhqk,bhkd->bhqd', p.astype(v.dtype), v)


def even_mixer(hx, hc, need_ctx, w_in, da_q_g, da_k_g, lam, lam_init, da_out_g,
               mla_q_a_g, mla_w_uq, mla_kv_a_g, mla_w_ukv, mla_q_g, mla_k_g, mla_kr_g):
    proj = lambda h: even_project(h, w_in, da_q_g, da_k_g, mla_q_a_g, mla_w_uq, mla_kv_a_g, mla_w_ukv,
                                  mla_q_g, mla_k_g, mla_kr_g)
    dq_x, dk_x, dv_x, qn_x, qr_x, kn_x, kr_x, mv_x = proj(hx)
    dq_c, dk_c, dv_c, qn_c, qr_c, kn_c, kr_c, mv_c = proj(hc)
    S = hx.shape[1]
    tab_da = axial_rope_tables(S, DA_DK)
    tab_mla = axial_rope_tables(S, MLA_ROPE)
    dq_x = apply_axial_rope(dq_x, tab_da)
    dk_x = apply_axial_rope(dk_x, tab_da)
    qr_x = apply_axial_rope(qr_x, tab_mla)
    kr_x = apply_axial_rope(kr_x, tab_mla)
    dk_all = jnp.concatenate([dk_c, dk_x], axis=-2)
    dv_all = jnp.concatenate([dv_c, dv_x], axis=-2)
    kn_all = jnp.concatenate([kn_c, kn_x], axis=-2)
    kr_all = jnp.concatenate([kr_c, kr_x], axis=-2)
    mv_all = jnp.concatenate([mv_c, mv_x], axis=-2)
    da_x = sweep_query_blocks(lambda q: diff_attend(q, dk_all, dv_all, lam, lam_init, da_out_g), dq_x)
    mla_x = sweep_query_blocks(lambda a, b: mla_attend(a, b, kn_all, kr_all, mv_all), qn_x, qr_x)
    y_x = merge_heads(da_x, mla_x)
    y_c = None
    if need_ctx:
        da_c = diff_attend(dq_c, dk_c, dv_c, lam, lam_init, da_out_g)
        mla_c = mla_attend(qn_c, qr_c, kn_c, kr_c, mv_c)
        y_c = merge_heads(da_c, mla_c)
    return y_x, y_c


def odd_project(h, w_in, q_g, k_g):
    B, T, _ = h.shape
    q, k, v = jnp.split((h @ w_in).reshape(B, T, 3 * NA_HEADS, NA_DH), 3, axis=2)
    q = rms_norm(q, q_g).transpose(0, 2, 1, 3)
    k = rms_norm(k, k_g).transpose(0, 2, 1, 3)
    return q, k, v.transpose(0, 2, 1, 3)


def dense_attend(q, k, v):
    p = softmax_f32(jnp.einsum('bhqd,bhkd->bhqk', q, k) * (NA_DH ** -0.5))
    return jnp.einsum('bhqk,bhkd->bhqd', p.astype(v.dtype), v)


def na_latent(q, k, v, kc, vc, rpb):
    B, H, S, d = q.shape
    rows = S // GRID_W
    kh = min(NA_KH, rows)
    kg = k.reshape(B, H, rows, GRID_W, d)
    vg = v.reshape(B, H, rows, GRID_W, d)
    qg = q.reshape(B, H, rows, GRID_W, d)
    cols = jnp.arange(GRID_W)
    col_start = jnp.clip(cols - NA_KW // 2, 0, GRID_W - NA_KW)
    col_valid = (cols[None, :] >= col_start[:, None]) & (cols[None, :] < col_start[:, None] + NA_KW)
    dc_idx = jnp.clip(cols[None, :] - cols[:, None], -(NA_KW - 1), NA_KW - 1) + NA_KW - 1
    rpb_c = rpb[:, :, dc_idx]
    scale = NA_DH ** -0.5

    def row_step(args):
        r, q_row = args
        rs = jnp.clip(r - kh // 2, 0, rows - kh)
        kb = lax.dynamic_slice_in_dim(kg, rs, kh, axis=2)
        vb = lax.dynamic_slice_in_dim(vg, rs, kh, axis=2)
        dr_idx = rs + jnp.arange(kh) - r + NA_KH - 1
        bias = jnp.take(rpb_c, dr_idx, axis=1).transpose(0, 2, 1, 3)
        s_loc = jnp.einsum('bhqd,bhiwd->bhqiw', q_row, kb).astype(jnp.float32) * scale + bias
        s_loc = jnp.where(col_valid[:, None, :], s_loc, NEG_INF).reshape(B, H, GRID_W, kh * GRID_W)
        s_ctx = jnp.einsum('bhqd,bhkd->bhqk', q_row, kc).astype(jnp.float32) * scale
        p = softmax_f32(jnp.concatenate([s_loc, s_ctx], axis=-1))
        p_loc = p[..., :kh * GRID_W].reshape(B, H, GRID_W, kh, GRID_W).astype(v.dtype)
        p_ctx = p[..., kh * GRID_W:].astype(v.dtype)
        return jnp.einsum('bhqiw,bhiwd->bhqd', p_loc, vb) + jnp.einsum('bhqk,bhkd->bhqd', p_ctx, vc)

    out = lax.map(row_step, (jnp.arange(rows), jnp.moveaxis(qg, 2, 0)))
    return jnp.moveaxis(out, 0, 2).reshape(B, H, S, d)


def odd_mixer(hx, hc, need_ctx, w_in, q_g, k_g, rpb):
    q_x, k_x, v_x = odd_project(hx, w_in, q_g, k_g)
    q_c, k_c, v_c = odd_project(hc, w_in, q_g, k_g)
    y_x = merge_heads(na_latent(q_x, k_x, v_x, k_c, v_c, rpb))
    y_c = merge_heads(dense_attend(q_c, k_c, v_c)) if need_ctx else None
    return y_x, y_c


def setup_inputs(seed: int = 0) -> dict:
    key = jax.random.key(seed)
    ks = iter(jax.random.split(key, 40))
    nrm = lambda shape, s: jax.random.normal(next(ks), shape, jnp.float32) * s
    gain = lambda shape: 1.0 + 0.1 * jax.random.normal(next(ks), shape, jnp.float32)
    D = D_MODEL
    return {
        "x": nrm((BATCH, SEQ, D), 1.0),
        "c": nrm((BATCH, D), 1.0),
        "ctx": nrm((BATCH, CTX_LEN, D), 1.0),
        "c_ctx": nrm((D,), 1.0),
        "mod_w": nrm((DEPTH, D, 6 * D), 0.5 * D ** -0.5),
        "mod_b": nrm((DEPTH, 6 * D), 0.02),
        "norm_mix_g": gain((DEPTH, D)),
        "norm_ffn_g": gain((DEPTH, D)),
        "w_out": nrm((DEPTH, MIX_WIDTH, D), MIX_WIDTH ** -0.5),
        "ffn_w_in": nrm((DEPTH, D, 2 * D_FF), D ** -0.5),
        "ffn_w_out": nrm((DEPTH, D_FF, D), D_FF ** -0.5),
        "ev_w_in": nrm((N_EVEN, D, EV_IN), D ** -0.5),
        "da_q_g": gain((N_EVEN, DA_DK)),
        "da_k_g": gain((N_EVEN, DA_DK)),
        "da_lq1": nrm((N_EVEN, DA_DK), 0.1),
        "da_lk1": nrm((N_EVEN, DA_DK), 0.1),
        "da_lq2": nrm((N_EVEN, DA_DK), 0.1),
        "da_lk2": nrm((N_EVEN, DA_DK), 0.1),
        "da_out_g": gain((N_EVEN, DA_DV)),
        "mla_q_a_g": gain((N_EVEN, MLA_Q_LORA)),
        "mla_w_uq": nrm((N_EVEN, MLA_Q_LORA, MLA_HEADS * (MLA_NOPE + MLA_ROPE)), MLA_Q_LORA ** -0.5),
        "mla_kv_a_g": gain((N_EVEN, MLA_KV_LORA)),
        "mla_w_ukv": nrm((N_EVEN, MLA_KV_LORA, MLA_HEADS * (MLA_NOPE + MLA_V)), MLA_KV_LORA ** -0.5),
        "mla_q_g": gain((N_EVEN, MLA_NOPE + MLA_ROPE)),
        "mla_k_g": gain((N_EVEN, MLA_NOPE)),
        "mla_kr_g": gain((N_EVEN, MLA_ROPE)),
        "od_w_in": nrm((N_ODD, D, 3 * NA_HEADS * NA_DH), D ** -0.5),
        "na_q_g": gain((N_ODD, NA_DH)),
        "na_k_g": gain((N_ODD, NA_DH)),
        "na_rpb": nrm((N_ODD, NA_HEADS, 2 * NA_KH - 1, 2 * NA_KW - 1), 0.5),
    }


def reference(x, c, ctx, c_ctx, mod_w, mod_b, norm_mix_g, norm_ffn_g, w_out, ffn_w_in, ffn_w_out,
              ev_w_in, da_q_g, da_k_g, da_lq1, da_lk1, da_lq2, da_lk2, da_out_g,
              mla_q_a_g, mla_w_uq, mla_kv_a_g, mla_w_ukv, mla_q_g, mla_k_g, mla_kr_g,
              od_w_in, na_q_g, na_k_g, na_rpb):
    for l in range(DEPTH):
        need_ctx = l < DEPTH - 1
        sh_a, sc_a, g_a, sh_f, sc_f, g_f = [m[:, None, :] for m in adaln(c, mod_w[l], mod_b[l])]
        csh_a, csc_a, cg_a, csh_f, csc_f, cg_f = adaln(c_ctx, mod_w[l], mod_b[l])
        hx = modulate(rms_norm(x, norm_mix_g[l]), sh_a, sc_a)
        hc = modulate(rms_norm(ctx, norm_mix_g[l]), csh_a, csc_a)
        e = l // 2
        if l % 2 == 0:
            lam_init = 0.8 - 0.6 * math.exp(-0.3 * l)
            lam = (jnp.exp(jnp.sum(da_lq1[e] * da_lk1[e]).astype(jnp.float32))
                   - jnp.exp(jnp.sum(da_lq2[e] * da_lk2[e]).astype(jnp.float32)) + lam_init)
            y_x, y_c = even_mixer(hx, hc, need_ctx, ev_w_in[e], da_q_g[e], da_k_g[e], lam, lam_init, da_out_g[e],
                                  mla_q_a_g[e], mla_w_uq[e], mla_kv_a_g[e], mla_w_ukv[e],
                                  mla_q_g[e], mla_k_g[e], mla_kr_g[e])
        else:
            y_x, y_c = odd_mixer(hx, hc, need_ctx, od_w_in[e], na_q_g[e], na_k_g[e], na_rpb[e])
        x = x + g_a * (y_x @ w_out[l])
        x = x + g_f * swiglu(modulate(rms_norm(x, norm_ffn_g[l]), sh_f, sc_f), ffn_w_in[l], ffn_w_out[l])
        if need_ctx:
            ctx = ctx + cg_a * (y_c @ w_out[l])
            ctx = ctx + cg_f * swiglu(modulate(rms_norm(ctx, norm_ffn_g[l]), csh_f, csc_f), ffn_w_in[l], ffn_w_out[l])
    return x
```

```python
import contextlib
import numpy as np
import concourse.bass as bass
import concourse.mybir as mybir
from concourse.bass_utils import run_bass_kernel_spmd

F32 = mybir.dt.float32
BF16 = mybir.dt.bfloat16
ALU = mybir.AluOpType
AF = mybir.ActivationFunctionType
AX = mybir.AxisListType

NCORES = 8
NB = 2
D = 1024
T = 2304
TL = 2048
TBLK = [(0, 512), (512, 512), (1024, 512), (1536, 512), (2048, 256)]
DFF = 2816
EPS = 1e-6
NEG = -1e30

ENGINES = ("tensor", "vector", "scalar", "gpsimd", "sync")
DMA_POOL = {"sync": (0, 14), "gpsimd": (14, 8), "scalar": (22, 2)}
N_DMA_SEMS = 24
SAME_ENGINE_SYNC = True


class Buf:
    __slots__ = ("name", "writer", "readers", "gen")

    def __init__(self, name):
        self.name = name
        self.writer = None
        self.readers = []
        self.gen = 0


class Tile:
    __slots__ = ("ap", "buf", "gen")

    def __init__(self, ap, buf):
        self.ap = ap
        self.buf = buf
        self.gen = buf.gen

    def __getitem__(self, k):
        return self.ap[k]


class Ring:
    def __init__(self, aps, name):
        self.items = [(ap, Buf("%s%d" % (name, i))) for i, ap in enumerate(aps)]
        self.i = 0

    def next(self):
        ap, buf = self.items[self.i % len(self.items)]
        self.i += 1
        buf.gen += 1
        return Tile(ap, buf)


def _unwrap(lst):
    out = []
    for b in lst:
        if isinstance(b, Tile):
            assert b.gen == b.buf.gen, "stale ring tile %s" % b.buf.name
            out.append(b.buf)
        else:
            out.append(b)
    return out


class Op:
    __slots__ = ("eng", "fn", "is_dma", "cdeps", "ddeps", "idx", "flag", "count", "dsem", "dval", "dprev")

    def __init__(self, eng, fn, is_dma):
        self.eng = eng
        self.fn = fn
        self.is_dma = is_dma
        self.cdeps = {}
        self.ddeps = set()
        self.flag = False
        self.count = 0
        self.dsem = None
        self.dval = 0
        self.dprev = None


class Sched:
    def __init__(self, nc):
        self.nc = nc
        self.ops = []
        self.per_eng = {e: [] for e in ENGINES}
        self.n_dma_e = {}

    def _adddep(self, o, d):
        if d is None or d is o:
            return
        if d.is_dma:
            o.ddeps.add(d)
        else:
            if d.eng == o.eng and not o.is_dma:
                if o.eng == "tensor" or not SAME_ENGINE_SYNC:
                    return
            cur = o.cdeps.get(d.eng)
            if cur is None or cur.idx < d.idx:
                o.cdeps[d.eng] = d

    def op(self, eng, fn, reads=(), writes=(), dma=False):
        reads = _unwrap(reads)
        writes = _unwrap(writes)
        o = Op(eng, fn, dma)
        o.idx = len(self.ops)
        for b in reads:
            self._adddep(o, b.writer)
        for b in writes:
            self._adddep(o, b.writer)
            for r in b.readers:
                self._adddep(o, r)
        for b in reads:
            b.readers.append(o)
        for b in writes:
            b.writer = o
            b.readers = []
        if dma:
            base, cnt = DMA_POOL[eng]
            k = self.n_dma_e.get(eng, 0)
            o.dsem = base + k % cnt
            o.dval = 16 * (k // cnt + 1)
            self.n_dma_e[eng] = k + 1
        self.ops.append(o)
        self.per_eng[eng].append(o)
        return o

    def alias(self, new_bufs, old_bufs):
        users = []
        for b in old_bufs:
            if b.writer is not None:
                users.append(b.writer)
            users.extend(b.readers)
        for b in new_bufs:
            b.writer = None
            b.readers = list(users)

    def emit(self, final_wait_ops):
        nc = self.nc
        for o in self.ops:
            for d in o.cdeps.values():
                d.flag = True
        for o in final_wait_ops:
            if not o.is_dma:
                o.flag = True
        for e in ENGINES:
            c = 0
            for o in self.per_eng[e]:
                if o.flag and not o.is_dma:
                    c += 1
                o.count = c
        last_on_sem = {}
        for o in self.ops:
            if o.is_dma:
                o.dprev = last_on_sem.get(o.dsem)
                last_on_sem[o.dsem] = o
        with contextlib.ExitStack() as es:
            esem = {e: es.enter_context(nc.semaphore("s_" + e)) for e in ENGINES}
            dsems = [es.enter_context(nc.semaphore("d_%d" % i)) for i in range(N_DMA_SEMS)]
            block = es.enter_context(nc.Block())

            def run_engine(e, eng):
                known = {x: 0 for x in ENGINES}
                dknown = [0] * N_DMA_SEMS
                for o in self.per_eng[e]:
                    dw = list(o.ddeps)
                    if o.is_dma and o.dprev is not None:
                        dw.append(o.dprev)
                    for d in dw:
                        if dknown[d.dsem] < d.dval:
                            eng.wait_ge(dsems[d.dsem], d.dval)
                            dknown[d.dsem] = d.dval
                    for d in o.cdeps.values():
                        if known[d.eng] < d.count:
                            eng.wait_ge(esem[d.eng], d.count)
                            known[d.eng] = d.count
                    ins = o.fn(eng)
                    if o.is_dma:
                        ins.then_inc(dsems[o.dsem], 16)
                    elif o.flag:
                        ins.then_inc(esem[e], 1)
                if e == "sync":
                    for d in final_wait_ops:
                        if d.is_dma:
                            eng.wait_ge(dsems[d.dsem], d.dval)
                        else:
                            eng.wait_ge(esem[d.eng], d.count)

            @block.tensor
            def _(eng):
                run_engine("tensor", eng)

            @block.vector
            def _(eng):
                run_engine("vector", eng)

            @block.scalar
            def _(eng):
                run_engine("scalar", eng)

            @block.gpsimd
            def _(eng):
                run_engine("gpsimd", eng)

            @block.sync
            def _(eng):
                run_engine("sync", eng)


def _rope_tables():
    t = np.arange(TL)
    row = (t // 64).astype(np.float32)
    col = (t % 64).astype(np.float32)
    inv = (np.float32(10000.0) ** (-np.arange(0, 32, 2, dtype=np.float32) / np.float32(32))).astype(np.float32)
    ar = (row[:, None] * inv).astype(np.float32)
    ac = (col[:, None] * inv).astype(np.float32)
    cr, sr, cc, sc = np.cos(ar), np.sin(ar), np.cos(ac), np.sin(ac)
    C = np.zeros((64, TL), np.float32)
    Sg = np.zeros((64, TL), np.float32)
    for d in range(64):
        i = d % 16
        first = (d % 32) < 16
        if d < 32:
            C[d] = cr[:, i]
            Sg[d] = -sr[:, i] if first else sr[:, i]
        else:
            C[d] = cc[:, i]
            Sg[d] = -sc[:, i] if first else sc[:, i]
    cs = np.zeros((128, 2, TL), np.float32)
    cs[:64, 0] = C
    cs[64:, 0] = C
    cs[:64, 1] = Sg
    cs[64:, 1] = Sg
    return cs


def _const_mats():
    m = np.zeros((128, 4, 128), np.float32)
    m[:, 0, :] = np.eye(128, dtype=np.float32)
    m[:, 1, :] = 1.0
    m[:64, 2, :64] = 1.0
    m[64:, 2, 64:] = 1.0
    for mm_ in range(128):
        d = mm_ % 64
        p = d + 16 if (d % 32) < 16 else d - 16
        m[(mm_ // 64) * 64 + p, 3, mm_] = 1.0
    return m


NAB_F = 36 * 64


def _na_bias_tables(rpb):
    kc = np.arange(64)[:, None]
    qc = np.arange(64)[None, :]
    cs = np.clip(qc - 8, 0, 48)
    colv = (kc >= cs) & (kc < cs + 16)
    dc = np.clip(kc - qc, -15, 15) + 15
    out = np.full((16, 128, 36, 64), NEG, np.float32)
    for kr2 in range(2):
        rows = slice(kr2 * 64, (kr2 + 1) * 64)
        for j in range(1, 15):
            dr = kr2 + 7 - j
            if -7 <= dr <= 7:
                out[:, rows, j - 1, :] = np.where(colv[None], rpb[:, dr + 7][:, dc], np.float32(NEG))
        for j in range(22):
            dr = kr2 + 10 - j
            if -4 <= dr <= 3:
                out[:, rows, 14 + j, :] = np.where(colv[None], rpb[:, dr + 7][:, dc], np.float32(NEG))
    return out.reshape(16, 128, NAB_F)


R_C = 0
R_DAQ, R_DAK, R_DAO, R_QA0, R_QA1, R_KVA, R_MQN, R_MQR, R_MK, R_MKR, R_NAQ, R_NAK = range(24, 36)
NPM2 = 64


def _pack_params(inp, b0):
    pm1 = np.zeros((128, 128), np.float32)
    pm1[0:96] = inp["mod_b"].reshape(96, 128)
    pm1[96:112] = inp["norm_mix_g"].reshape(16, 128)
    pm1[112:128] = inp["norm_ffn_g"].reshape(16, 128)
    pm2 = np.zeros((NPM2, 128), np.float32)
    pm2[0:8] = inp["c"][b0].reshape(8, 128)
    pm2[8:16] = inp["c"][b0 + 1].reshape(8, 128)
    pm2[16:24] = inp["c_ctx"].reshape(8, 128)
    rep = lambda v: np.concatenate([v, v])
    pm2[R_DAQ] = rep(inp["da_q_g"][0])
    pm2[R_DAK] = rep(inp["da_k_g"][0])
    pm2[R_DAO] = inp["da_out_g"][0]
    pm2[R_QA0] = inp["mla_q_a_g"][0][:128]
    pm2[R_QA1] = inp["mla_q_a_g"][0][128:]
    pm2[R_KVA] = inp["mla_kv_a_g"][0]
    pm2[R_MQN] = inp["mla_q_g"][0][:128]
    pm2[R_MQR] = rep(inp["mla_q_g"][0][128:])
    pm2[R_MK] = inp["mla_k_g"][0]
    pm2[R_MKR] = rep(inp["mla_kr_g"][0])
    pm2[R_NAQ] = rep(inp["na_q_g"][0])
    pm2[R_NAK] = rep(inp["na_k_g"][0])
    return pm1, pm2


def build_nc(nb=NB, nlayers=2, dbg=None):
    nc = bass.Bass("TRN2", target_bir_lowering=False)
    dt_in = lambda name, shape: nc.dram_tensor(name, shape, F32, kind="ExternalInput").ap()
    x_d = dt_in("x", [nb, TL, D])
    ctx_d = dt_in("ctx", [nb, 256, D])
    modw_d = dt_in("mod_w", [2, D, 6 * D])
    wout_d = dt_in("w_out", [2, D, D])
    fwi_d = dt_in("ffn_w_in", [2, D, 2 * DFF])
    fwo_d = dt_in("ffn_w_out", [2, DFF, D])
    ev_d = dt_in("ev_w_in", [D, 1984])
    uq_d = dt_in("mla_w_uq", [256, 768])
    ukv_d = dt_in("mla_w_ukv", [128, 1024])
    od_d = dt_in("od_w_in", [D, 3072])
    lam_d = dt_in("lamv", [4, 64])
    pm1_d = dt_in("pm1", [128, 128])
    pm2_d = dt_in("pm2", [NPM2, 128])
    cmat_d = dt_in("cmat", [128, 4, 128])
    rope_d = dt_in("ropecs", [128, 2, TL])
    nab_d = dt_in("nabias", [16, 128, NAB_F])
    out_d = nc.dram_tensor("out", [nb, TL, D], F32, kind="ExternalOutput").ap()
    dbg_d = None
    if dbg is not None:
        dbg_d = nc.dram_tensor("dbg", [128, 8, T], F32, kind="ExternalOutput").ap()

    sc = lambda name, shape: nc.dram_tensor(name, shape, BF16, kind="Internal").ap()
    ev_s = sc("ev_s", [16, 128, 8, 128])
    od_s = sc("od_s", [24, 128, 8, 128])
    wout_s = sc("wout_s", [2, D, D])
    wi_s = sc("wi_s", [2, 22, 128, 8, 256])
    wo_s = sc("wo_s", [2, 8, 128, 22, 128])

    S = Sched(nc)
    sb = nc.alloc_sbuf_tensor
    xT = sb("xT", [128, 8, T], F32)
    hT = sb("hT", [128, 8, T], BF16)
    ARENA = 36352
    arena = sb("arena", [128, ARENA], BF16)
    NSLOT = 8
    slots = arena[:, 0:NSLOT * T].rearrange("p (s t) -> p s t", s=NSLOT)
    off = NSLOT * T
    ropecs = arena[:, off:off + 2 * TL * 2].bitcast(F32).rearrange("p (a t) -> p a t", a=2)
    nabt = arena[:, 4 * T:4 * T + 2 * 2 * NAB_F].rearrange("p (u h f) -> p u h f", u=2, h=2)
    off += 2 * TL * 2
    ptiles = arena[:, off:off + 4 * 512].rearrange("p (s n) -> p s n", s=4)
    off += 4 * 512
    ytiles = arena[:, off:off + 2 * 512].rearrange("p (s n) -> p s n", s=2)
    off += 2 * 512
    uq_sb = arena[:, off:off + 2 * 768].rearrange("p (k n) -> p k n", k=2)
    off += 2 * 768
    ukv_sb = arena[:, off:off + 1024]
    off += 1024
    wproj = arena[:, off:off + 4 * 1024].rearrange("p (s n) -> p s n", s=4)
    off += 4 * 1024
    assert off <= ARENA, off
    a_sb = arena[:, 0:22 * 512].rearrange("p (j n) -> p j n", j=22)
    foff = 22 * 512
    wA = arena[:, foff:foff + 4 * 2048].rearrange("p (s n) -> p s n", s=4)
    foff += 4 * 2048
    wB = arena[:, foff:foff + 2 * 22 * 128].rearrange("p (s n) -> p s n", s=2)
    foff += 2 * 22 * 128
    assert foff <= ARENA, foff
    modw_t = arena[:, 0:2 * 8192].bitcast(F32).rearrange("p (s k n) -> p s k n", s=2, k=8)

    sqt = sb("sqt", [128, 4, 512], BF16)
    ftm = sb("ftm", [128, 6, 512], F32)
    ddt = sb("ddt", [128, 2, 512], F32)
    rst = sb("rst", [128, 2, 512], F32)
    stg = arena[:, 0:4096].bitcast(F32).rearrange("p (s n) -> p s n", s=2)
    cmf = sb("cmf", [128, 128], F32)
    cmb = sb("cmb", [128, 4, 128], BF16)
    pv1 = sb("pv1", [128, 128], F32)
    pv2 = sb("pv2", [128, NPM2], F32)
    modv = sb("modv", [128, 2, 48, 3], F32)
    Av = sb("Av", [128, 2, 2, 3, 8], F32)
    siluT = sb("siluT", [128, 3, 8], F32)
    sm = sb("sm", [128, 16], F32)

    identf = cmf
    identb = cmb[:, 0, :]
    onesb = cmb[:, 1, :]
    blk64 = cmb[:, 2, :]
    permb = cmb[:, 3, :]
    epsb = sm[:, 0:1]
    neglam = sm[:, 1:2]
    og08 = sm[:, 2:3]
    naq8 = sm[:, 3:4]

    pst = [nc.alloc_psum_tensor("ps%d" % i, [128, 512], F32) for i in range(8)]
    psS = Ring(pst[0:4], "psS")
    psA = Ring(pst[4:8], "psA")
    sq_r = Ring([sqt[:, i, :] for i in range(4)], "sq")
    ft_r = Ring([ftm[:, i, :] for i in range(6)], "ft")
    dd_r = Ring([ddt[:, i, :] for i in range(2)], "dd")
    rs_r = Ring([rst[:, i, :] for i in range(2)], "rs")
    stg_r = Ring([stg[:, i, :] for i in range(2)], "stg")
    p_r = Ring([ptiles[:, i, :] for i in range(4)], "pt")
    y_r = Ring([ytiles[:, i, :] for i in range(2)], "yt")
    wproj_r = Ring([wproj[:, i, :] for i in range(4)], "wp")
    wA_r = Ring([wA[:, i, :] for i in range(4)], "wA")
    wB_r = Ring([wB[:, i, :] for i in range(2)], "wB")
    modw_r = Ring([modw_t[:, i] for i in range(2)], "mw")
    nab_r = Ring([nabt[:, i] for i in range(2)], "nab")

    xB = [[Buf("x%d_%d" % (c, tb)) for tb in range(5)] for c in range(8)]
    hB = [[Buf("h%d_%d" % (c, tb)) for tb in range(5)] for c in range(8)]
    slB = [[Buf("sl%d_%d" % (s, tb)) for tb in range(5)] for s in range(NSLOT)]
    aB = [Buf("a%d" % j) for j in range(22)]
    cB = Buf("consts")
    ropeB = Buf("rope")
    uqB = Buf("uq")
    ukvB = Buf("ukv")
    evB = [Buf("ev%d" % i) for i in range(16)]
    odB = [Buf("od%d" % i) for i in range(24)]
    woutB = [Buf("wout%d" % l) for l in range(2)]
    wiB = [[Buf("wi%d_%d" % (l, j)) for j in range(22)] for l in range(2)]
    woB = [[Buf("wo%d_%d" % (l, n)) for n in range(8)] for l in range(2)]
    modB = Buf("modv")
    outB = Buf("outd")
    arena_attn_bufs = [b for r in slB for b in r] + [ropeB, uqB, ukvB] + [it[1] for rr in (p_r, y_r, wproj_r, nab_r) for it in rr.items]
    arena_ffn_bufs = aB + [it[1] for rr in (wA_r, wB_r) for it in rr.items]
    arena_pro_bufs = [it[1] for it in modw_r.items]
    stg_bufs = [it[1] for it in stg_r.items]
    nab_bufs = [it[1] for it in nab_r.items]

    def mm(out, lhsT, rhs, start, stop, reads, writes):
        return S.op("tensor", lambda e: e.matmul(out, lhsT=lhsT, rhs=rhs, start=start, stop=stop), reads, writes)

    def act(out, in_, func, reads, writes, bias=None, scale=None):
        kw = {}
        if bias is not None:
            kw["bias"] = bias
        if scale is not None:
            kw["scale"] = scale
        return S.op("scalar", lambda e: e.activation(out=out, in_=in_, func=func, **kw), reads, writes)

    def dve(fn, reads, writes, eng="vector"):
        return S.op(eng, fn, reads, writes)

    def dma(eng, out, in_, reads, writes):
        return S.op(eng, lambda e: e.dma_start(out=out, in_=in_), reads, writes, dma=True)

    dma("sync", cmf[:], cmat_d[:, 0, :], [], [cB])
    dma("gpsimd", cmb[:], cmat_d, [], [cB])
    pmr_t = ft_r.next()
    pmr = pmr_t[:, 0:128]
    dma("sync", pmr, pm1_d, [], [pmr_t])
    lam_t = ft_r.next()
    lamt = lam_t[:, 0:256].rearrange("p (a n) -> p a n", a=4)
    dve(lambda e: e.memset(sm[:], 0.0), [], [cB])
    dve(lambda e: e.memset(epsb, EPS), [], [cB])
    for i in range(4):
        dma("sync", lamt[:, i, :], lam_d[i, :].partition_broadcast(128), [], [lam_t])
    t_ps = psA.next()
    mm(t_ps[:, 0:128], pmr, identf[:], True, True, [cB, pmr_t], [t_ps])
    dve(lambda e: e.tensor_copy(out=pv1[:], in_=t_ps[:, 0:128]), [t_ps], [cB])
    pm2r = ft_r.next()
    dma("sync", pm2r[0:NPM2, 0:128], pm2_d, [], [pm2r])
    t_ps2 = psA.next()
    mm(t_ps2[:, 0:NPM2], pm2r[0:NPM2, 0:128], identf[0:NPM2, 0:NPM2], True, True, [pm2r, cB], [t_ps2])
    dve(lambda e: e.tensor_copy(out=pv2[:], in_=t_ps2[:, 0:NPM2]), [t_ps2], [cB])
    act(siluT[:].rearrange("p s k -> p (s k)"), pv2[:, 0:24], AF.Silu, [cB], [cB])
    dve(lambda e: e.tensor_tensor(out=lamt[:, 0, :], in0=lamt[:, 0, :], in1=lamt[:, 1, :], op=ALU.mult), [cB, lam_t], [lam_t])
    dve(lambda e: e.tensor_tensor(out=lamt[:, 2, :], in0=lamt[:, 2, :], in1=lamt[:, 3, :], op=ALU.mult), [cB, lam_t], [lam_t])
    dve(lambda e: e.tensor_reduce(out=sm[:, 4:5], in_=lamt[:, 0, :], axis=AX.X, op=ALU.add), [cB, lam_t], [cB])
    dve(lambda e: e.tensor_reduce(out=sm[:, 5:6], in_=lamt[:, 2, :], axis=AX.X, op=ALU.add), [cB, lam_t], [cB])
    act(sm[:, 4:6], sm[:, 4:6], AF.Exp, [cB], [cB])
    dve(lambda e: e.tensor_tensor(out=sm[:, 6:7], in0=sm[:, 5:6], in1=sm[:, 4:5], op=ALU.subtract), [cB], [cB])
    dve(lambda e: e.tensor_scalar(out=neglam, in0=sm[:, 6:7], scalar1=-0.2, scalar2=None, op0=ALU.add), [cB], [cB])
    dve(lambda e: e.tensor_scalar(out=og08, in0=pv2[:, R_DAO:R_DAO + 1], scalar1=0.8, scalar2=None, op0=ALU.mult), [cB], [cB])
    dve(lambda e: e.tensor_scalar(out=naq8, in0=pv2[:, R_NAQ:R_NAQ + 1], scalar1=0.125, scalar2=None, op0=ALU.mult), [cB], [cB])

    for l in range(nlayers):
        mps = psA.next()
        mview = mps[:, 0:144].rearrange("p (n s) -> p n s", s=3)
        for g in range(12):
            wt = modw_r.next()
            dma("sync", wt[:], modw_d[l].rearrange("(k p) n -> p k n", p=128)[:, :, g * 512:(g + 1) * 512], [], [wt])
            for nn in range(4):
                n = g * 4 + nn
                for kc in range(8):
                    mm(mview[:, n, :], wt[:, kc, nn * 128:(nn + 1) * 128], siluT[:, :, kc], kc == 0, kc == 7, [wt, cB], [mps])
        for s in range(3):
            dve(lambda e, s=s, l=l, mview=mview: e.tensor_tensor(out=modv[:, l, :, s], in0=mview[:, :, s], in1=pv1[:, l * 48:(l + 1) * 48], op=ALU.add), [mps, cB], [modB])
        for w_, part, grow in ((0, 1, 96), (1, 4, 112)):
            for s in range(3):
                dve(lambda e, l=l, w_=w_, part=part, grow=grow, s=s: e.scalar_tensor_tensor(
                    out=Av[:, l, w_, s, :], in0=modv[:, l, part * 8:(part + 1) * 8, s], scalar=1.0,
                    in1=pv1[:, grow + l * 8:grow + (l + 1) * 8], op0=ALU.add, op1=ALU.mult), [modB, cB], [modB])

    def Acol(l, w_, s, c):
        return Av[:, l, w_, s, c:c + 1]

    def Mcol(l, part, s, c):
        return modv[:, l, part * 8 + c, s:s + 1]

    S.alias(arena_attn_bufs, arena_pro_bufs)
    ev_src = ev_d.rearrange("(k p) n -> p k n", p=128)
    for nbk in range(16):
        w = 128 if nbk < 15 else 64
        dma("gpsimd", ev_s[nbk, :, :, 0:w], ev_src[:, :, nbk * 128:nbk * 128 + w], [], [evB[nbk]])
    dma("gpsimd", uq_sb, uq_d.rearrange("(k p) n -> p k n", p=128), [], [uqB])
    dma("gpsimd", ukv_sb, ukv_d, [], [ukvB])

    def precast_layer(l):
        dma("gpsimd", wout_s[l], wout_d[l], [], [woutB[l]])
        src = fwi_d[l].rearrange("(k p) n -> p k n", p=128)
        for j in range(22):
            dma("gpsimd", wi_s[l, j, :, :, 0:128], src[:, :, j * 128:(j + 1) * 128], [], [wiB[l][j]])
            dma("gpsimd", wi_s[l, j, :, :, 128:256], src[:, :, DFF + j * 128:DFF + (j + 1) * 128], [], [wiB[l][j]])
        srco = fwo_d[l].rearrange("(j p) n -> p j n", p=128)
        for n in range(8):
            dma("gpsimd", wo_s[l, n], srco[:, :, n * 128:(n + 1) * 128], [], [woB[l][n]])

    precast_layer(0)
    if nlayers > 1:
        od_src = od_d.rearrange("(k p) n -> p k n", p=128)
        for nbk in range(24):
            dma("gpsimd", od_s[nbk], od_src[:, :, nbk * 128:(nbk + 1) * 128], [], [odB[nbk]])
        precast_layer(1)

    def load_x(b):
        S.alias(stg_bufs, arena_attn_bufs)
        for t in range(18):
            st = stg_r.next()
            src = x_d[b, t * 128:(t + 1) * 128, :] if t < 16 else ctx_d[b, (t - 16) * 128:(t - 15) * 128, :]
            dma("sync", st[:], src, [], [st])
            tb = t // 4
            for half in range(2):
                ps = psA.next()
                for cc in range(4):
                    c = half * 4 + cc
                    mm(ps[:, cc * 128:(cc + 1) * 128], st[:, c * 128:(c + 1) * 128], identf[:], True, True, [st, cB], [ps])
                o = xT[:, half * 4:(half + 1) * 4, t * 128:(t + 1) * 128]
                i_ = ps[:, :].rearrange("p (c n) -> p c n", c=4)
                wr = [xB[half * 4 + cc][tb] for cc in range(4)]
                if half == 0:
                    dve(lambda e, o=o, i_=i_: e.tensor_copy(out=o, in_=i_), [ps], wr)
                else:
                    act(o, i_, AF.Copy, [ps], wr)
        S.alias(arena_attn_bufs, stg_bufs)

    def store_out(b):
        ops = []
        S.alias(stg_bufs, arena_attn_bufs)
        for t in range(16):
            st = stg_r.next()
            tb = t // 4
            for half in range(2):
                ps = psA.next()
                for cc in range(4):
                    c = half * 4 + cc
                    mm(ps[:, cc * 128:(cc + 1) * 128], xT[:, c, t * 128:(t + 1) * 128], identf[:], True, True, [xB[c][tb], cB], [ps])
                o = st[:, half * 512:(half + 1) * 512]
                if half == 0:
                    dve(lambda e, o=o, ps=ps: e.tensor_copy(out=o, in_=ps[:, :]), [ps], [st])
                else:
                    act(o, ps[:, :], AF.Copy, [ps], [st])
            ops.append(dma("sync", out_d[b, t * 128:(t + 1) * 128, :], st[:], [st], [outB]))
        S.alias(arena_attn_bufs, stg_bufs)
        return ops

    def norm_mod(l, w_, b, tbs):
        for tb in tbs:
            t0, n = TBLK[tb]
            s = 2 if tb == 4 else b
            ssP = psS.next()
            for c in range(8):
                sq = sq_r.next()
                act(sq[:, 0:n], xT[:, c, t0:t0 + n], AF.Square, [xB[c][tb]], [sq])
                mm(ssP[:, 0:n], onesb, sq[:, 0:n], c == 0, c == 7, [sq, cB], [ssP])
            sd = ft_r.next()
            act(sd[:, 0:n], ssP[:, 0:n], AF.Ln, [ssP, cB], [sd], bias=epsb, scale=1.0 / D)
            rs = rs_r.next()
            act(rs[:, 0:n], sd[:, 0:n], AF.Exp, [sd], [rs], scale=-0.5)
            for c in range(8):
                tmp = ft_r.next()
                dve(lambda e, tmp=tmp, c=c, rs=rs, t0=t0, n=n, s=s: e.scalar_tensor_tensor(
                    out=tmp[:, 0:n], in0=xT[:, c, t0:t0 + n], scalar=Acol(l, w_, s, c), in1=rs[:, 0:n],
                    op0=ALU.mult, op1=ALU.mult), [xB[c][tb], rs, modB], [tmp])
                act(hT[:, c, t0:t0 + n], tmp[:, 0:n], AF.Identity, [tmp, modB], [hB[c][tb]],
                    bias=Mcol(l, 0 if w_ == 0 else 3, s, c))

    def group_norm(tiles, nfeat, tb, rope):
        t0, n = TBLK[tb]
        ssP = psS.next()
        for i, (ps, M, G, gain, dfn, dbufs) in enumerate(tiles):
            sq = sq_r.next()
            act(sq[0:M, 0:n], ps[0:M, 0:n], AF.Square, [ps], [sq])
            mm(ssP[:, 0:n], G, sq[0:M, 0:n], i == 0, i == len(tiles) - 1, [sq, cB], [ssP])
        yield
        sd = ft_r.next()
        act(sd[:, 0:n], ssP[:, 0:n], AF.Ln, [ssP, cB], [sd], bias=epsb, scale=1.0 / nfeat)
        yield
        rs = rs_r.next()
        act(rs[:, 0:n], sd[:, 0:n], AF.Exp, [sd], [rs], scale=-0.5)
        yield
        todo = []
        for (ps, M, G, gain, dfn, dbufs), rp in zip(tiles, rope):
            if not rp:
                dve(lambda e, ps=ps, M=M, gain=gain, dfn=dfn: e.scalar_tensor_tensor(
                    out=dfn(t0, n), in0=ps[0:M, 0:n], scalar=gain, in1=rs[0:M, 0:n], op0=ALU.mult, op1=ALU.mult),
                    [ps, rs, cB], dbufs)
                continue
            xn = ft_r.next()
            dve(lambda e, ps=ps, M=M, gain=gain, xn=xn: e.scalar_tensor_tensor(
                out=xn[0:M, 0:n], in0=ps[0:M, 0:n], scalar=gain, in1=rs[0:M, 0:n], op0=ALU.mult, op1=ALU.mult),
                [ps, rs, cB], [xn])
            todo.append((M, dfn, dbufs, xn))
        if not todo:
            return
        yield
        st2 = []
        for (M, dfn, dbufs, xn) in todo:
            xb = sq_r.next()
            act(xb[0:M, 0:n], xn[0:M, 0:n], AF.Copy, [xn], [xb])
            pp = psS.next()
            mm(pp[0:M, 0:n], permb[0:M, 0:M], xb[0:M, 0:n], True, True, [xb, cB], [pp])
            st2.append((M, dfn, dbufs, xn, pp))
        yield
        st3 = []
        for (M, dfn, dbufs, xn, pp) in st2:
            dve(lambda e, xn=xn, M=M: e.tensor_tensor(out=xn[0:M, 0:n], in0=xn[0:M, 0:n], in1=ropecs[0:M, 0, t0:t0 + n], op=ALU.mult),
                [xn, ropeB], [xn])
            t2 = ft_r.next()
            dve(lambda e, pp=pp, t2=t2, M=M: e.tensor_tensor(out=t2[0:M, 0:n], in0=pp[0:M, 0:n], in1=ropecs[0:M, 1, t0:t0 + n], op=ALU.mult),
                [pp, ropeB], [t2])
            st3.append((M, dfn, dbufs, xn, t2))
        yield
        for (M, dfn, dbufs, xn, t2) in st3:
            dve(lambda e, xn=xn, t2=t2, M=M, dfn=dfn: e.tensor_tensor(out=dfn(t0, n), in0=xn[0:M, 0:n], in1=t2[0:M, 0:n], op=ALU.add),
                [xn, t2], dbufs)

    def run_chains(*gens):
        gens = list(gens)
        while gens:
            for g in list(gens):
                try:
                    next(g)
                except StopIteration:
                    gens.remove(g)

    def load_wtile(scr, scrB, ring=None):
        wt = (ring or wproj_r).next()
        dma("sync", wt[:], scr.rearrange("p k n -> p (k n)"), [scrB], [wt])
        return wt

    def proj_fm(wt, M, tb, col0=0, wstride=128):
        t0, n = TBLK[tb]
        ps = psA.next()
        for kc in range(8):
            mm(ps[0:M, 0:n], wt[:, kc * wstride + col0:kc * wstride + col0 + M], hT[:, kc, t0:t0 + n], kc == 0, kc == 7,
               [wt, hB[kc][tb]], [ps])
        return ps

    def proj_v_tm(wt, slot, ntiles=18):
        for t in range(ntiles):
            tb = t // 4
            ps = psS.next()
            for kc in range(8):
                mm(ps[:, 0:128], hT[:, kc, t * 128:(t + 1) * 128], wt[:, kc * 128:(kc + 1) * 128], kc == 0, kc == 7,
                   [wt, hB[kc][tb]], [ps])
            o = slots[:, slot, t * 128:(t + 1) * 128]
            if t % 2 == 0:
                dve(lambda e, o=o, ps=ps: e.tensor_copy(out=o, in_=ps[:, 0:128]), [ps], [slB[slot][tb]])
            else:
                act(o, ps[:, 0:128], AF.Copy, [ps], [slB[slot][tb]])

    def attn_core(units, scale, group=2, need_sm=True):
        O = psA.next()
        Sm = psA.next() if need_sm else None
        groups = [units[i:i + group] for i in range(0, len(units), group)]
        sts = {}

        def emit_qk(gi):
            for ui, u in enumerate(groups[gi]):
                st = psS.next()
                u[1](st)
                sts[(gi, ui)] = st
        emit_qk(0)
        if len(groups) > 1:
            emit_qk(1)
        n_units = len(units)
        done = 0
        for gi, g in enumerate(groups):
            ps = []
            for ui, u in enumerate(g):
                st = sts.pop((gi, ui))
                p = p_r.next()
                act(p[:, 0:u[0]], st[:, 0:u[0]], AF.Exp, [st], [p], scale=scale)
                ps.append(p)
            for ui in reversed(range(len(g))):
                first = done == 0
                done += 1
                g[ui][2](ps[ui], O, Sm, first, done == n_units)
            if gi + 2 < len(groups):
                emit_qk(gi + 2)
        return O, Sm

    def tbs_of(t0, n):
        return sorted(set([t0 // 512, (t0 + n - 1) // 512]))

    def wout_accum(l, wt, yb, tb, b, q0=None, nq=None):
        t0, n = TBLK[tb]
        if q0 is not None:
            t0, n = q0, nq
        s = 2 if tb == 4 else b
        tbl_ = tbs_of(t0, n)
        for nchk in range(8):
            ps = psS.next()
            mm(ps[:, 0:n], wt[:, nchk * 128:(nchk + 1) * 128], yb[:, 0:n], True, True, [wt, yb], [ps])
            xbs = [xB[nchk][t_] for t_ in tbl_]
            dve(lambda e, ps=ps, nchk=nchk, t0=t0, n=n, s=s: e.scalar_tensor_tensor(
                out=xT[:, nchk, t0:t0 + n], in0=ps[:, 0:n], scalar=Mcol(l, 2, s, nchk), in1=xT[:, nchk, t0:t0 + n],
                op0=ALU.mult, op1=ALU.add), [ps, modB] + xbs, xbs)

    def ffn(l, b, tbs):
        S.alias(arena_ffn_bufs, arena_attn_bufs)
        for tb in tbs:
            t0, n = TBLK[tb]
            s = 2 if tb == 4 else b
            for j in range(22):
                wt = load_wtile(wi_s[l, j], wiB[l][j], wA_r)
                G = psS.next()
                U = psS.next()
                for kc in range(8):
                    mm(G[:, 0:n], wt[:, kc * 256:kc * 256 + 128], hT[:, kc, t0:t0 + n], kc == 0, kc == 7, [wt, hB[kc][tb]], [G])
                for kc in range(8):
                    mm(U[:, 0:n], wt[:, kc * 256 + 128:kc * 256 + 256], hT[:, kc, t0:t0 + n], kc == 0, kc == 7, [wt, hB[kc][tb]], [U])
                sg = ft_r.next()
                act(sg[:, 0:n], G[:, 0:n], AF.Silu, [G], [sg])
                dve(lambda e, sg=sg, U=U, j=j, n=n: e.tensor_tensor(out=a_sb[:, j, 0:n], in0=sg[:, 0:n], in1=U[:, 0:n], op=ALU.mult),
                    [sg, U], [aB[j]])
            for nchk in range(8):
                wt = load_wtile(wo_s[l, nchk], woB[l][nchk], wB_r)
                ps = psA.next()
                for j in range(22):
                    mm(ps[:, 0:n], wt[:, j * 128:(j + 1) * 128], a_sb[:, j, 0:n], j == 0, j == 21, [wt, aB[j]], [ps])
                dve(lambda e, ps=ps, nchk=nchk, t0=t0, n=n, s=s: e.scalar_tensor_tensor(
                    out=xT[:, nchk, t0:t0 + n], in0=ps[:, 0:n], scalar=Mcol(l, 5, s, nchk), in1=xT[:, nchk, t0:t0 + n],
                    op0=ALU.mult, op1=ALU.add), [ps, modB, xB[nchk][tb]], [xB[nchk][tb]])
        S.alias(arena_attn_bufs, arena_ffn_bufs)

    ALLCH = list(range(18))
    CTXCH = [16, 17]

    def da_qblock(tb, wo_t, b):
        l = 0
        SQ, SK, SV = 0, 1, 2
        t0, nq = TBLK[tb]
        chunks = ALLCH if tb < 4 else CTXCH
        ons = []
        for i in range(2):
            def mk_unit(kc, i=i):
                def qk_fn(st):
                    mm(st[:, 0:nq], slots[64 * i:64 * i + 64, SK, kc * 128:(kc + 1) * 128], slots[64 * i:64 * i + 64, SQ, t0:t0 + nq],
                       True, True, [slB[SK][kc // 4], slB[SQ][tb]], [st])

                def av_fn(p, O, Sm, first, last):
                    mm(O[:, 0:nq], slots[:, SV, kc * 128:(kc + 1) * 128], p[:, 0:nq], first, last, [p, slB[SV][kc // 4]], [O])
                    mm(Sm[:, 0:nq], onesb, p[:, 0:nq], first, last, [p, cB], [Sm])
                return (nq, qk_fn, av_fn)
            O, Sm = attn_core([mk_unit(kc) for kc in chunks], 0.125)
            rs = rs_r.next()
            dve(lambda e, rs=rs, Sm=Sm: e.reciprocal(out=rs[:, 0:nq], in_=Sm[:, 0:nq]), [Sm], [rs])
            on = ft_r.next()
            dve(lambda e, on=on, O=O, rs=rs: e.tensor_tensor(out=on[:, 0:nq], in0=O[:, 0:nq], in1=rs[:, 0:nq], op=ALU.mult), [O, rs], [on])
            ons.append(on)
        dd = dd_r.next()
        dve(lambda e: e.scalar_tensor_tensor(out=dd[:, 0:nq], in0=ons[1][:, 0:nq], scalar=neglam, in1=ons[0][:, 0:nq],
                                             op0=ALU.mult, op1=ALU.add), [ons[0], ons[1], cB], [dd])

        def tail():
            sq = sq_r.next()
            act(sq[:, 0:nq], dd[:, 0:nq], AF.Square, [dd], [sq])
            ssP = psS.next()
            mm(ssP[:, 0:nq], onesb, sq[:, 0:nq], True, True, [sq, cB], [ssP])
            sd = ft_r.next()
            act(sd[:, 0:nq], ssP[:, 0:nq], AF.Ln, [ssP, cB], [sd], bias=epsb, scale=1.0 / 128)
            rs2 = rs_r.next()
            act(rs2[:, 0:nq], sd[:, 0:nq], AF.Exp, [sd], [rs2], scale=-0.5)
            yb = y_r.next()
            dve(lambda e: e.scalar_tensor_tensor(out=yb[:, 0:nq], in0=dd[:, 0:nq], scalar=og08, in1=rs2[:, 0:nq],
                                                 op0=ALU.mult, op1=ALU.mult), [dd, rs2, cB], [yb])
            wout_accum(l, wo_t, yb, tb, b)
        return tail

    def mla_qblock(tb, wo_t, b):
        l = 0
        SCQ0, SCQ1, SCKV, SKR, SQN, SQR, SKN, SMV = range(8)
        t0, nq = TBLK[tb]
        chunks = ALLCH if tb < 4 else CTXCH

        def mk_unit(kc):
            def qk_fn(st):
                mm(st[:, 0:nq], slots[:, SKN, kc * 128:(kc + 1) * 128], slots[:, SQN, t0:t0 + nq], True, False,
                   [slB[SKN][kc // 4], slB[SQN][tb]], [st])
                mm(st[:, 0:nq], slots[0:64, SKR, kc * 128:(kc + 1) * 128], slots[0:64, SQR, t0:t0 + nq], False, True,
                   [slB[SKR][kc // 4], slB[SQR][tb]], [st])

            def av_fn(p, O, Sm, first, last):
                mm(O[:, 0:nq], slots[:, SMV, kc * 128:(kc + 1) * 128], p[:, 0:nq], first, last, [p, slB[SMV][kc // 4]], [O])
                mm(Sm[:, 0:nq], onesb, p[:, 0:nq], first, last, [p, cB], [Sm])
            return (nq, qk_fn, av_fn)
        O, Sm = attn_core([mk_unit(kc) for kc in chunks], 192.0 ** -0.5)
        rs = rs_r.next()
        dve(lambda e: e.reciprocal(out=rs[:, 0:nq], in_=Sm[:, 0:nq]), [Sm], [rs])
        yb = y_r.next()
        dve(lambda e: e.tensor_tensor(out=yb[:, 0:nq], in0=O[:, 0:nq], in1=rs[:, 0:nq], op=ALU.mult), [O, rs], [yb])
        return lambda: wout_accum(l, wo_t, yb, tb, b)

    def mla_head_proj(h, tb):
        SCQ0, SCQ1, SCKV, SKR, SQN, SQR, SKN, SMV = range(8)
        t0, n = TBLK[tb]
        pn = psA.next()
        for kc in range(2):
            mm(pn[:, 0:n], uq_sb[:, kc, 192 * h:192 * h + 128], slots[:, SCQ0 + kc, t0:t0 + n], kc == 0, kc == 1, [uqB, slB[SCQ0 + kc][tb]], [pn])
        pr = psA.next()
        for kc in range(2):
            mm(pr[0:64, 0:n], uq_sb[:, kc, 192 * h + 128:192 * h + 192], slots[:, SCQ0 + kc, t0:t0 + n], kc == 0, kc == 1, [uqB, slB[SCQ0 + kc][tb]], [pr])
        pk = psA.next()
        mm(pk[:, 0:n], ukv_sb[:, 256 * h:256 * h + 128], slots[:, SCKV, t0:t0 + n], True, True, [ukvB, slB[SCKV][tb]], [pk])
        run_chains(
            group_norm([(pn, 128, onesb, pv2[:, R_MQN:R_MQN + 1], lambda t0, n: slots[:, SQN, t0:t0 + n], [slB[SQN][tb]]),
                        (pr, 64, onesb[0:64, :], pv2[0:64, R_MQR:R_MQR + 1], lambda t0, n: slots[0:64, SQR, t0:t0 + n], [slB[SQR][tb]])],
                       192, tb, [False, tb < 4]),
            group_norm([(pk, 128, onesb, pv2[:, R_MK:R_MK + 1], lambda t0, n: slots[:, SKN, t0:t0 + n], [slB[SKN][tb]])], 128, tb, [False]))

    import os as _os
    NA_OLDV = _os.environ.get("NA_OLDV", "1") == "1"
    NA_BLOCKS = [(0, 4), (4, 8), (12, 8), (20, 8), (28, 4)]
    if _os.environ.get("NA_R4", "0") == "1":
        NA_BLOCKS = [(4 * i, 4) for i in range(8)]

    def na_qblock(m, blk, wo_t, nab, b):
        l = 1
        SQ, SK, SV0 = 0, 1, 2
        q0, R = NA_BLOCKS[blk]
        t0, nq = q0 * 64, R * 64
        qtb = tbs_of(t0, nq)
        qbufs = [slB[SQ][t_] for t_ in qtb]
        if q0 == 0 or q0 == 28:
            lrows = [0, 2, 4, 6] if q0 == 0 else [24, 26, 28, 30]
            nbase = lambda r: (7 - (r - q0) - 1) * 64
        else:
            lrows = [q0 - 4 + 2 * i for i in range(R // 2 + 4)]
            nbase = lambda r: (14 + 10 - (r - q0)) * 64
        chunks = [("l", r) for r in lrows] + [("c", 32), ("c", 34)]
        per_unit = 512 // nq
        yb = y_r.next()
        for e_ in range(2):
            pl, ph = 64 * e_, 64 * e_ + 64
            ol, oh = 64 - pl, 128 - pl

            def mk_unit(grp, pl=pl, ph=ph, e_=e_):
                def qk_fn(st):
                    for hf, (kind, r) in enumerate(grp):
                        kc = r // 2
                        mm(st[:, hf * nq:(hf + 1) * nq], slots[pl:ph, SK, kc * 128:(kc + 1) * 128], slots[pl:ph, SQ, t0:t0 + nq], True, kind == "c",
                           [slB[SK][kc // 4]] + qbufs, [st])
                        if kind == "l":
                            nb_ = nbase(r)
                            mm(st[:, hf * nq:(hf + 1) * nq], identb, nab[:, e_, nb_:nb_ + nq], False, True, [nab, cB], [st])

                def av_fn(p, O, Sm, first, last):
                    for hf, (kind, r) in enumerate(grp):
                        kc = r // 2
                        vs = SV0 if NA_OLDV else SV0 + e_
                        mm(O[:, 0:nq], slots[:, vs, kc * 128:(kc + 1) * 128], p[:, hf * nq:(hf + 1) * nq],
                           first and hf == 0, last and hf == len(grp) - 1, [p, slB[vs][kc // 4]], [O])
                        if NA_OLDV:
                            mm(Sm[:, 0:nq], onesb, p[:, hf * nq:(hf + 1) * nq], first and hf == 0, last and hf == len(grp) - 1, [p, cB], [Sm])
                return (nq * len(grp), qk_fn, av_fn)
            O, Sm = attn_core([mk_unit(chunks[i:i + per_unit]) for i in range(0, len(chunks), per_unit)], 1.0, need_sm=NA_OLDV)
            rs = rs_r.next()
            if NA_OLDV:
                dve(lambda e, rs=rs, Sm=Sm, pl=pl, ph=ph: e.reciprocal(out=rs[pl:ph, 0:nq], in_=Sm[pl:ph, 0:nq]), [Sm], [rs])
            else:
                dve(lambda e, rs=rs, O=O, pl=pl, ph=ph, ol=ol, oh=oh: e.reciprocal(out=rs[pl:ph, 0:nq], in_=O[ol:oh, 0:nq]), [O], [rs])
            dve(lambda e, O=O, rs=rs, pl=pl, ph=ph: e.tensor_tensor(out=yb[pl:ph, 0:nq], in0=O[pl:ph, 0:nq], in1=rs[pl:ph, 0:nq], op=ALU.mult),
                [O, rs], [yb])
        return lambda: wout_accum(l, wo_t, yb, qtb[0], b, q0=t0, nq=nq)

    def proj_v_na(wt):
        for t in range(18):
            tb = t // 4
            ps = psS.next()
            for kc in range(8):
                mm(ps[:, 0:128], hT[:, kc, t * 128:(t + 1) * 128], wt[:, kc * 128:(kc + 1) * 128], kc == 0, kc == 7,
                   [wt, hB[kc][tb]], [ps])
            o0 = slots[:, 2, t * 128:t * 128 + 64]
            o1 = slots[:, 3, t * 128 + 64:(t + 1) * 128]
            dve(lambda e, o0=o0, ps=ps: e.tensor_copy(out=o0, in_=ps[:, 0:64]), [ps], [slB[2][tb]])
            act(o1, ps[:, 64:128], AF.Copy, [ps], [slB[3][tb]])

    def qblocks(need_ctx):
        return [0, 1, 2, 3, 4] if need_ctx else [0, 1, 2, 3]

    class Pend:
        def __init__(self):
            self.t = None

        def push(self, tail):
            old = self.t
            self.t = tail
            if old is not None:
                old()

        def flush(self):
            if self.t is not None:
                self.t()
            self.t = None

    def layer0(b):
        l = 0
        S.alias([ropeB] + [bb for sl in slB[4:8] for bb in sl], nab_bufs)
        dma("sync", ropecs, rope_d, [], [ropeB])
        norm_mod(l, 0, b, range(5))
        gq = pv2[:, R_DAQ:R_DAQ + 1]
        gk = pv2[:, R_DAK:R_DAK + 1]
        SQ, SK, SV = 0, 1, 2
        pend = Pend()
        for h in range(4):
            wq = load_wtile(ev_s[h], evB[h])
            wk = load_wtile(ev_s[4 + h], evB[4 + h])
            wv = load_wtile(ev_s[8 + h], evB[8 + h])
            for tb in range(5):
                rp = tb < 4
                psq = proj_fm(wq, 128, tb)
                psk = proj_fm(wk, 128, tb)
                run_chains(group_norm([(psq, 128, blk64, gq, lambda t0, n: slots[:, SQ, t0:t0 + n], [slB[SQ][tb]])], 64, tb, [rp]),
                           group_norm([(psk, 128, blk64, gk, lambda t0, n: slots[:, SK, t0:t0 + n], [slB[SK][tb]])], 64, tb, [rp]))
            proj_v_tm(wv, SV)
            pend.flush()
            wo_t = wproj_r.next()
            dma("sync", wo_t[:], wout_s[l, h * 128:(h + 1) * 128, :], [woutB[l]], [wo_t])
            for tb in range(5):
                pend.push(da_qblock(tb, wo_t, b))
        pend.flush()
        SCQ0, SCQ1, SCKV, SKR, SQN, SQR, SKN, SMV = range(8)
        wcq0 = load_wtile(ev_s[12], evB[12])
        wcq1 = load_wtile(ev_s[13], evB[13])
        wckv = load_wtile(ev_s[14], evB[14])
        wkr = load_wtile(ev_s[15], evB[15])
        for tb in range(5):
            p0 = proj_fm(wcq0, 128, tb)
            p1 = proj_fm(wcq1, 128, tb)
            p2 = proj_fm(wckv, 128, tb)
            p3 = proj_fm(wkr, 64, tb)
            run_chains(
                group_norm([(p0, 128, onesb, pv2[:, R_QA0:R_QA0 + 1], lambda t0, n: slots[:, SCQ0, t0:t0 + n], [slB[SCQ0][tb]]),
                            (p1, 128, onesb, pv2[:, R_QA1:R_QA1 + 1], lambda t0, n: slots[:, SCQ1, t0:t0 + n], [slB[SCQ1][tb]])], 256, tb, [False, False]),
                group_norm([(p2, 128, onesb, pv2[:, R_KVA:R_KVA + 1], lambda t0, n: slots[:, SCKV, t0:t0 + n], [slB[SCKV][tb]])], 128, tb, [False]))
            run_chains(
                group_norm([(p3, 64, onesb[0:64, :], pv2[0:64, R_MKR:R_MKR + 1], lambda t0, n: slots[0:64, SKR, t0:t0 + n], [slB[SKR][tb]])], 64, tb, [tb < 4]))
        for h in range(4):
            for tb in range(5):
                mla_head_proj(h, tb)
            for t in range(18):
                tb = t // 4
                ps = psS.next()
                mm(ps[:, 0:128], slots[:, SCKV, t * 128:(t + 1) * 128], ukv_sb[:, 256 * h + 128:256 * h + 256], True, True, [ukvB, slB[SCKV][tb]], [ps])
                o = slots[:, SMV, t * 128:(t + 1) * 128]
                if t % 2 == 0:
                    dve(lambda e, o=o, ps=ps: e.tensor_copy(out=o, in_=ps[:, 0:128]), [ps], [slB[SMV][tb]])
                else:
                    act(o, ps[:, 0:128], AF.Copy, [ps], [slB[SMV][tb]])
            pend.flush()
            wo_t = wproj_r.next()
            dma("sync", wo_t[:], wout_s[l, (4 + h) * 128:(5 + h) * 128, :], [woutB[l]], [wo_t])
            for tb in range(5):
                pend.push(mla_qblock(tb, wo_t, b))
        pend.flush()
        norm_mod(l, 1, b, range(5))
        ffn(l, b, range(5))

    def layer1(b):
        l = 1
        S.alias(nab_bufs, [ropeB] + [bb for sl in slB[4:8] for bb in sl])
        norm_mod(l, 0, b, range(5))
        SQ, SK, SV = 0, 1, 2
        if not NA_OLDV:
            for t in range(18):
                o0 = slots[:, 2, t * 128 + 64:(t + 1) * 128]
                o1 = slots[:, 3, t * 128:t * 128 + 64]
                dve(lambda e, o0=o0: e.tensor_copy(out=o0, in_=onesb[:, 0:64]), [cB], [slB[2][t // 4]])
                act(o1, onesb[:, 0:64], AF.Copy, [cB], [slB[3][t // 4]])
        gq = naq8
        gk = pv2[:, R_NAK:R_NAK + 1]
        pend = Pend()
        for m in range(8):
            wq = load_wtile(od_s[m], odB[m])
            wk = load_wtile(od_s[8 + m], odB[8 + m])
            wv = load_wtile(od_s[16 + m], odB[16 + m])
            nab = nab_r.next()
            dma("gpsimd", nab[:], nab_d[2 * m:2 * m + 2].rearrange("h p f -> p h f"), [], [nab])
            for tb in range(5):
                psk = proj_fm(wk, 128, tb)
                ch = [group_norm([(psk, 128, blk64, gk, lambda t0, n: slots[:, SK, t0:t0 + n], [slB[SK][tb]])], 64, tb, [False])]
                if tb < 4:
                    psq = proj_fm(wq, 128, tb)
                    ch.append(group_norm([(psq, 128, blk64, gq, lambda t0, n: slots[:, SQ, t0:t0 + n], [slB[SQ][tb]])], 64, tb, [False]))
                run_chains(*ch)
            if NA_OLDV:
                proj_v_tm(wv, 2)
            else:
                proj_v_na(wv)
            pend.flush()
            wo_t = wproj_r.next()
            dma("sync", wo_t[:], wout_s[l, m * 128:(m + 1) * 128, :], [woutB[l]], [wo_t])
            import os as _os
            for blk in [int(v) for v in _os.environ.get("NA_DEBUG_BLOCKS", ",".join(str(i) for i in range(len(NA_BLOCKS)))).split(",")]:
                pend.push(na_qblock(m, blk, wo_t, nab, b))
        pend.flush()
        norm_mod(l, 1, b, range(4))
        ffn(l, b, range(4))

    final_ops = []
    for b in range(nb):
        load_x(b)
        if dbg == "x0":
            break
        layer0(b)
        if dbg == "l0" or nlayers == 1:
            break
        layer1(b)
        final_ops += store_out(b)
    if dbg is not None:
        for c in range(8):
            final_ops.append(dma("sync", dbg_d[:, c, :], xT[:, c, :], [xB[c][tb] for tb in range(5)], [outB]))
    S.emit(final_ops)
    return nc


_NC_CACHE = {}


def kernel(**inputs):
    inp = {k: np.ascontiguousarray(np.asarray(v)) for k, v in inputs.items()}
    if "nc" not in _NC_CACHE:
        _NC_CACHE["nc"] = build_nc()
    nc = _NC_CACHE["nc"]
    cmat = _const_mats()
    rope = _rope_tables()
    nab = _na_bias_tables(inp["na_rpb"][0])
    lamv = np.stack([inp["da_lq1"][0], inp["da_lk1"][0], inp["da_lq2"][0], inp["da_lk2"][0]]).astype(np.float32)
    in_maps = []
    for core in range(NCORES):
        b0 = core * NB
        pm1, pm2 = _pack_params(inp, b0)
        in_maps.append({
            "x": inp["x"][b0:b0 + NB], "ctx": inp["ctx"][b0:b0 + NB],
            "mod_w": inp["mod_w"], "w_out": inp["w_out"], "ffn_w_in": inp["ffn_w_in"], "ffn_w_out": inp["ffn_w_out"],
            "ev_w_in": inp["ev_w_in"][0], "mla_w_uq": inp["mla_w_uq"][0], "mla_w_ukv": inp["mla_w_ukv"][0],
            "od_w_in": inp["od_w_in"][0], "lamv": lamv, "pm1": pm1, "pm2": pm2, "cmat": cmat, "ropecs": rope, "nabias": nab,
        })
    res = run_bass_kernel_spmd(nc, in_maps, core_ids=list(range(NCORES)))
    out = np.concatenate([np.asarray(r["out"]) for r in res.results], axis=0)
    return out.astype(np.float32)
```

```python
import contextlib
import numpy as np
import concourse.bass as bass
import concourse.mybir as mybir
from concourse.bass_utils import run_bass_kernel_spmd

F32 = mybir.dt.float32
BF16 = mybir.dt.bfloat16
ALU = mybir.AluOpType
AF = mybir.ActivationFunctionType
AX = mybir.AxisListType

NCORES = 8
NB = 2
D = 1024
T = 2304
TL = 2048
TBLK = [(0, 512), (512, 512), (1024, 512), (1536, 512), (2048, 256)]
DFF = 2816
EPS = 1e-6
NEG = -1e30

ENGINES = ("tensor", "vector", "scalar", "gpsimd", "sync")
DMA_POOL = {"sync": (0, 14), "gpsimd": (14, 8), "scalar": (22, 2)}
N_DMA_SEMS = 24
SAME_ENGINE_SYNC = True


class Buf:
    __slots__ = ("name", "writer", "readers", "gen")

    def __init__(self, name):
        self.name = name
        self.writer = None
        self.readers = []
        self.gen = 0


class Tile:
    __slots__ = ("ap", "buf", "gen")

    def __init__(self, ap, buf):
        self.ap = ap
        self.buf = buf
        self.gen = buf.gen

    def __getitem__(self, k):
        return self.ap[k]


class Ring:
    def __init__(self, aps, name):
        self.items = [(ap, Buf("%s%d" % (name, i))) for i, ap in enumerate(aps)]
        self.i = 0

    def next(self):
        ap, buf = self.items[self.i % len(self.items)]
        self.i += 1
        buf.gen += 1
        return Tile(ap, buf)


def _unwrap(lst):
    out = []
    for b in lst:
        if isinstance(b, Tile):
            assert b.gen == b.buf.gen, "stale ring tile %s" % b.buf.name
            out.append(b.buf)
        else:
            out.append(b)
    return out


class Op:
    __slots__ = ("eng", "fn", "is_dma", "cdeps", "ddeps", "idx", "flag", "count", "dsem", "dval", "dprev")

    def __init__(self, eng, fn, is_dma):
        self.eng = eng
        self.fn = fn
        self.is_dma = is_dma
        self.cdeps = {}
        self.ddeps = set()
        self.flag = False
        self.count = 0
        self.dsem = None
        self.dval = 0
        self.dprev = None


class Sched:
    def __init__(self, nc):
        self.nc = nc
        self.ops = []
        self.per_eng = {e: [] for e in ENGINES}
        self.n_dma_e = {}

    def _adddep(self, o, d):
        if d is None or d is o:
            return
        if d.is_dma:
            o.ddeps.add(d)
        else:
            if d.eng == o.eng and not o.is_dma:
                if o.eng == "tensor" or not SAME_ENGINE_SYNC:
                    return
            cur = o.cdeps.get(d.eng)
            if cur is None or cur.idx < d.idx:
                o.cdeps[d.eng] = d

    def op(self, eng, fn, reads=(), writes=(), dma=False):
        reads = _unwrap(reads)
        writes = _unwrap(writes)
        o = Op(eng, fn, dma)
        o.idx = len(self.ops)
        for b in reads:
            self._adddep(o, b.writer)
        for b in writes:
            self._adddep(o, b.writer)
            for r in b.readers:
                self._adddep(o, r)
        for b in reads:
            b.readers.append(o)
        for b in writes:
            b.writer = o
            b.readers = []
        if dma:
            base, cnt = DMA_POOL[eng]
            k = self.n_dma_e.get(eng, 0)
            o.dsem = base + k % cnt
            o.dval = 16 * (k // cnt + 1)
            self.n_dma_e[eng] = k + 1
        self.ops.append(o)
        self.per_eng[eng].append(o)
        return o

    def alias(self, new_bufs, old_bufs):
        users = []
        for b in old_bufs:
            if b.writer is not None:
                users.append(b.writer)
            users.extend(b.readers)
        for b in new_bufs:
            b.writer = None
            b.readers = list(users)

    def emit(self, final_wait_ops):
        nc = self.nc
        for o in self.ops:
            for d in o.cdeps.values():
                d.flag = True
        for o in final_wait_ops:
            if not o.is_dma:
                o.flag = True
        for e in ENGINES:
            c = 0
            for o in self.per_eng[e]:
                if o.flag and not o.is_dma:
                    c += 1
                o.count = c
        last_on_sem = {}
        for o in self.ops:
            if o.is_dma:
                o.dprev = last_on_sem.get(o.dsem)
                last_on_sem[o.dsem] = o
        with contextlib.ExitStack() as es:
            esem = {e: es.enter_context(nc.semaphore("s_" + e)) for e in ENGINES}
            dsems = [es.enter_context(nc.semaphore("d_%d" % i)) for i in range(N_DMA_SEMS)]
            block = es.enter_context(nc.Block())

            def run_engine(e, eng):
                known = {x: 0 for x in ENGINES}
                dknown = [0] * N_DMA_SEMS
                for o in self.per_eng[e]:
                    dw = list(o.ddeps)
                    if o.is_dma and o.dprev is not None:
                        dw.append(o.dprev)
                    for d in dw:
                        if dknown[d.dsem] < d.dval:
                            eng.wait_ge(dsems[d.dsem], d.dval)
                            dknown[d.dsem] = d.dval
                    for d in o.cdeps.values():
                        if known[d.eng] < d.count:
                            eng.wait_ge(esem[d.eng], d.count)
                            known[d.eng] = d.count
                    ins = o.fn(eng)
                    if o.is_dma:
                        ins.then_inc(dsems[o.dsem], 16)
                    elif o.flag:
                        ins.then_inc(esem[e], 1)
                if e == "sync":
                    for d in final_wait_ops:
                        if d.is_dma:
                            eng.wait_ge(dsems[d.dsem], d.dval)
                        else:
                            eng.wait_ge(esem[d.eng], d.count)

            @block.tensor
            def _(eng):
                run_engine("tensor", eng)

            @block.vector
            def _(eng):
                run_engine("vector", eng)

            @block.scalar
            def _(eng):
                run_engine("scalar", eng)

            @block.gpsimd
            def _(eng):
                run_engine("gpsimd", eng)

            @block.sync
            def _(eng):
                run_engine("sync", eng)


def _rope_tables():
    t = np.arange(TL)
    row = (t // 64).astype(np.float32)
    col = (t % 64).astype(np.float32)
    inv = (np.float32(10000.0) ** (-np.arange(0, 32, 2, dtype=np.float32) / np.float32(32))).astype(np.float32)
    ar = (row[:, None] * inv).astype(np.float32)
    ac = (col[:, None] * inv).astype(np.float32)
    cr, sr, cc, sc = np.cos(ar), np.sin(ar), np.cos(ac), np.sin(ac)
    C = np.zeros((64, TL), np.float32)
    Sg = np.zeros((64, TL), np.float32)
    for d in range(64):
        i = d % 16
        first = (d % 32) < 16
        if d < 32:
            C[d] = cr[:, i]
            Sg[d] = -sr[:, i] if first else sr[:, i]
        else:
            C[d] = cc[:, i]
            Sg[d] = -sc[:, i] if first else sc[:, i]
    cs = np.zeros((128, 2, TL), np.float32)
    cs[:64, 0] = C
    cs[64:, 0] = C
    cs[:64, 1] = Sg
    cs[64:, 1] = Sg
    return cs


def _const_mats():
    m = np.zeros((128, 4, 128), np.float32)
    m[:, 0, :] = np.eye(128, dtype=np.float32)
    m[:, 1, :] = 1.0
    m[:64, 2, :64] = 1.0
    m[64:, 2, 64:] = 1.0
    for mm_ in range(128):
        d = mm_ % 64
        p = d + 16 if (d % 32) < 16 else d - 16
        m[(mm_ // 64) * 64 + p, 3, mm_] = 1.0
    return m


NAB_F = 36 * 64


def _na_bias_tables(rpb):
    kc = np.arange(64)[:, None]
    qc = np.arange(64)[None, :]
    cs = np.clip(qc - 8, 0, 48)
    colv = (kc >= cs) & (kc < cs + 16)
    dc = np.clip(kc - qc, -15, 15) + 15
    out = np.full((16, 128, 36, 64), NEG, np.float32)
    for kr2 in range(2):
        rows = slice(kr2 * 64, (kr2 + 1) * 64)
        for j in range(1, 15):
            dr = kr2 + 7 - j
            if -7 <= dr <= 7:
                out[:, rows, j - 1, :] = np.where(colv[None], rpb[:, dr + 7][:, dc], np.float32(NEG))
        for j in range(22):
            dr = kr2 + 10 - j
            if -4 <= dr <= 3:
                out[:, rows, 14 + j, :] = np.where(colv[None], rpb[:, dr + 7][:, dc], np.float32(NEG))
    return out.reshape(16, 128, NAB_F)


R_C = 0
R_DAQ, R_DAK, R_DAO, R_QA0, R_QA1, R_KVA, R_MQN, R_MQR, R_MK, R_MKR, R_NAQ, R_NAK = range(24, 36)
NPM2 = 64


def _pack_params(inp, b0):
    pm1 = np.zeros((128, 128), np.float32)
    pm1[0:96] = inp["mod_b"].reshape(96, 128)
    pm1[96:112] = inp["norm_mix_g"].reshape(16, 128)
    pm1[112:128] = inp["norm_ffn_g"].reshape(16, 128)
    pm2 = np.zeros((NPM2, 128), np.float32)
    pm2[0:8] = inp["c"][b0].reshape(8, 128)
    pm2[8:16] = inp["c"][b0 + 1].reshape(8, 128)
    pm2[16:24] = inp["c_ctx"].reshape(8, 128)
    rep = lambda v: np.concatenate([v, v])
    pm2[R_DAQ] = rep(inp["da_q_g"][0])
    pm2[R_DAK] = rep(inp["da_k_g"][0])
    pm2[R_DAO] = inp["da_out_g"][0]
    pm2[R_QA0] = inp["mla_q_a_g"][0][:128]
    pm2[R_QA1] = inp["mla_q_a_g"][0][128:]
    pm2[R_KVA] = inp["mla_kv_a_g"][0]
    pm2[R_MQN] = inp["mla_q_g"][0][:128]
    pm2[R_MQR] = rep(inp["mla_q_g"][0][128:])
    pm2[R_MK] = inp["mla_k_g"][0]
    pm2[R_MKR] = rep(inp["mla_kr_g"][0])
    pm2[R_NAQ] = rep(inp["na_q_g"][0])
    pm2[R_NAK] = rep(inp["na_k_g"][0])
    return pm1, pm2


def build_nc(nb=NB, nlayers=2, dbg=None):
    nc = bass.Bass("TRN2", target_bir_lowering=False)
    dt_in = lambda name, shape: nc.dram_tensor(name, shape, F32, kind="ExternalInput").ap()
    x_d = dt_in("x", [nb, TL, D])
    ctx_d = dt_in("ctx", [nb, 256, D])
    modw_d = dt_in("mod_w", [2, D, 6 * D])
    wout_d = dt_in("w_out", [2, D, D])
    fwi_d = dt_in("ffn_w_in", [2, D, 2 * DFF])
    fwo_d = dt_in("ffn_w_out", [2, DFF, D])
    ev_d = dt_in("ev_w_in", [D, 1984])
    uq_d = dt_in("mla_w_uq", [256, 768])
    ukv_d = dt_in("mla_w_ukv", [128, 1024])
    od_d = dt_in("od_w_in", [D, 3072])
    lam_d = dt_in("lamv", [4, 64])
    pm1_d = dt_in("pm1", [128, 128])
    pm2_d = dt_in("pm2", [NPM2, 128])
    cmat_d = dt_in("cmat", [128, 4, 128])
    rope_d = dt_in("ropecs", [128, 2, TL])
    nab_d = dt_in("nabias", [16, 128, NAB_F])
    out_d = nc.dram_tensor("out", [nb, TL, D], F32, kind="ExternalOutput").ap()
    dbg_d = None
    if dbg is not None:
        dbg_d = nc.dram_tensor("dbg", [128, 8, T], F32, kind="ExternalOutput").ap()

    sc = lambda name, shape: nc.dram_tensor(name, shape, BF16, kind="Internal").ap()
    ev_s = sc("ev_s", [16, 128, 8, 128])
    od_s = sc("od_s", [24, 128, 8, 128])
    wout_s = sc("wout_s", [2, D, D])
    wi_s = sc("wi_s", [2, 22, 128, 8, 256])
    wo_s = sc("wo_s", [2, 8, 128, 22, 128])

    S = Sched(nc)
    sb = nc.alloc_sbuf_tensor
    xT = sb("xT", [128, 8, T], F32)
    hT = sb("hT", [128, 8, T], BF16)
    ARENA = 36352
    arena = sb("arena", [128, ARENA], BF16)
    NSLOT = 8
    slots = arena[:, 0:NSLOT * T].rearrange("p (s t) -> p s t", s=NSLOT)
    off = NSLOT * T
    ropecs = arena[:, off:off + 2 * TL * 2].bitcast(F32).rearrange("p (a t) -> p a t", a=2)
    nabt = arena[:, 4 * T:4 * T + 2 * 2 * NAB_F].rearrange("p (u h f) -> p u h f", u=2, h=2)
    off += 2 * TL * 2
    ptiles = arena[:, off:off + 4 * 512].rearrange("p (s n) -> p s n", s=4)
    off += 4 * 512
    ytiles = arena[:, off:off + 2 * 512].rearrange("p (s n) -> p s n", s=2)
    off += 2 * 512
    uq_sb = arena[:, off:off + 2 * 768].rearrange("p (k n) -> p k n", k=2)
    off += 2 * 768
    ukv_sb = arena[:, off:off + 1024]
    off += 1024
    wproj = arena[:, off:off + 4 * 1024].rearrange("p (s n) -> p s n", s=4)
    off += 4 * 1024
    assert off <= ARENA, off
    a_sb = arena[:, 0:22 * 512].rearrange("p (j n) -> p j n", j=22)
    foff = 22 * 512
    wA = arena[:, foff:foff + 4 * 2048].rearrange("p (s n) -> p s n", s=4)
    foff += 4 * 2048
    wB = arena[:, foff:foff + 2 * 22 * 128].rearrange("p (s n) -> p s n", s=2)
    foff += 2 * 22 * 128
    assert foff <= ARENA, foff
    modw_t = arena[:, 0:2 * 8192].bitcast(F32).rearrange("p (s k n) -> p s k n", s=2, k=8)

    sqt = sb("sqt", [128, 4, 512], BF16)
    ftm = sb("ftm", [128, 6, 512], F32)
    ddt = sb("ddt", [128, 2, 512], F32)
    rst = sb("rst", [128, 2, 512], F32)
    stg = arena[:, 0:4096].bitcast(F32).rearrange("p (s n) -> p s n", s=2)
    cmf = sb("cmf", [128, 128], F32)
    cmb = sb("cmb", [128, 4, 128], BF16)
    pv1 = sb("pv1", [128, 128], F32)
    pv2 = sb("pv2", [128, NPM2], F32)
    modv = sb("modv", [128, 2, 48, 3], F32)
    Av = sb("Av", [128, 2, 2, 3, 8], F32)
    siluT = sb("siluT", [128, 3, 8], F32)
    sm = sb("sm", [128, 16], F32)

    identf = cmf
    identb = cmb[:, 0, :]
    onesb = cmb[:, 1, :]
    blk64 = cmb[:, 2, :]
    permb = cmb[:, 3, :]
    epsb = sm[:, 0:1]
    neglam = sm[:, 1:2]
    og08 = sm[:, 2:3]
    naq8 = sm[:, 3:4]

    pst = [nc.alloc_psum_tensor("ps%d" % i, [128, 512], F32) for i in range(8)]
    psS = Ring(pst[0:4], "psS")
    psA = Ring(pst[4:8], "psA")
    sq_r = Ring([sqt[:, i, :] for i in range(4)], "sq")
    ft_r = Ring([ftm[:, i, :] for i in range(6)], "ft")
    dd_r = Ring([ddt[:, i, :] for i in range(2)], "dd")
    rs_r = Ring([rst[:, i, :] for i in range(2)], "rs")
    stg_r = Ring([stg[:, i, :] for i in range(2)], "stg")
    p_r = Ring([ptiles[:, i, :] for i in range(4)], "pt")
    y_r = Ring([ytiles[:, i, :] for i in range(2)], "yt")
    wproj_r = Ring([wproj[:, i, :] for i in range(4)], "wp")
    wA_r = Ring([wA[:, i, :] for i in range(4)], "wA")
    wB_r = Ring([wB[:, i, :] for i in range(2)], "wB")
    modw_r = Ring([modw_t[:, i] for i in range(2)], "mw")
    nab_r = Ring([nabt[:, i] for i in range(2)], "nab")

    xB = [[Buf("x%d_%d" % (c, tb)) for tb in range(5)] for c in range(8)]
    hB = [[Buf("h%d_%d" % (c, tb)) for tb in range(5)] for c in range(8)]
    slB = [[Buf("sl%d_%d" % (s, tb)) for tb in range(5)] for s in range(NSLOT)]
    aB = [Buf("a%d" % j) for j in range(22)]
    cB = Buf("consts")
    ropeB = Buf("rope")
    uqB = Buf("uq")
    ukvB = Buf("ukv")
    evB = [Buf("ev%d" % i) for i in range(16)]
    odB = [Buf("od%d" % i) for i in range(24)]
    woutB = [Buf("wout%d" % l) for l in range(2)]
    wiB = [[Buf("wi%d_%d" % (l, j)) for j in range(22)] for l in range(2)]
    woB = [[Buf("wo%d_%d" % (l, n)) for n in range(8)] for l in range(2)]
    modB = Buf("modv")
    outB = Buf("outd")
    arena_attn_bufs = [b for r in slB for b in r] + [ropeB, uqB, ukvB] + [it[1] for rr in (p_r, y_r, wproj_r, nab_r) for it in rr.items]
    arena_ffn_bufs = aB + [it[1] for rr in (wA_r, wB_r) for it in rr.items]
    arena_pro_bufs = [it[1] for it in modw_r.items]
    stg_bufs = [it[1] for it in stg_r.items]
    nab_bufs = [it[1] for it in nab_r.items]

    def mm(out, lhsT, rhs, start, stop, reads, writes):
        return S.op("tensor", lambda e: e.matmul(out, lhsT=lhsT, rhs=rhs, start=start, stop=stop), reads, writes)

    def act(out, in_, func, reads, writes, bias=None, scale=None):
        kw = {}
        if bias is not None:
            kw["bias"] = bias
        if scale is not None:
            kw["scale"] = scale
        return S.op("scalar", lambda e: e.activation(out=out, in_=in_, func=func, **kw), reads, writes)

    def dve(fn, reads, writes, eng="vector"):
        return S.op(eng, fn, reads, writes)

    def dma(eng, out, in_, reads, writes):
        return S.op(eng, lambda e: e.dma_start(out=out, in_=in_), reads, writes, dma=True)

    dma("sync", cmf[:], cmat_d[:, 0, :], [], [cB])
    dma("gpsimd", cmb[:], cmat_d, [], [cB])
    pmr_t = ft_r.next()
    pmr = pmr_t[:, 0:128]
    dma("sync", pmr, pm1_d, [], [pmr_t])
    lam_t = ft_r.next()
    lamt = lam_t[:, 0:256].rearrange("p (a n) -> p a n", a=4)
    dve(lambda e: e.memset(sm[:], 0.0), [], [cB])
    dve(lambda e: e.memset(epsb, EPS), [], [cB])
    for i in range(4):
        dma("sync", lamt[:, i, :], lam_d[i, :].partition_broadcast(128), [], [lam_t])
    t_ps = psA.next()
    mm(t_ps[:, 0:128], pmr, identf[:], True, True, [cB, pmr_t], [t_ps])
    dve(lambda e: e.tensor_copy(out=pv1[:], in_=t_ps[:, 0:128]), [t_ps], [cB])
    pm2r = ft_r.next()
    dma("sync", pm2r[0:NPM2, 0:128], pm2_d, [], [pm2r])
    t_ps2 = psA.next()
    mm(t_ps2[:, 0:NPM2], pm2r[0:NPM2, 0:128], identf[0:NPM2, 0:NPM2], True, True, [pm2r, cB], [t_ps2])
    dve(lambda e: e.tensor_copy(out=pv2[:], in_=t_ps2[:, 0:NPM2]), [t_ps2], [cB])
    act(siluT[:].rearrange("p s k -> p (s k)"), pv2[:, 0:24], AF.Silu, [cB], [cB])
    dve(lambda e: e.tensor_tensor(out=lamt[:, 0, :], in0=lamt[:, 0, :], in1=lamt[:, 1, :], op=ALU.mult), [cB, lam_t], [lam_t])
    dve(lambda e: e.tensor_tensor(out=lamt[:, 2, :], in0=lamt[:, 2, :], in1=lamt[:, 3, :], op=ALU.mult), [cB, lam_t], [lam_t])
    dve(lambda e: e.tensor_reduce(out=sm[:, 4:5], in_=lamt[:, 0, :], axis=AX.X, op=ALU.add), [cB, lam_t], [cB])
    dve(lambda e: e.tensor_reduce(out=sm[:, 5:6], in_=lamt[:, 2, :], axis=AX.X, op=ALU.add), [cB, lam_t], [cB])
    act(sm[:, 4:6], sm[:, 4:6], AF.Exp, [cB], [cB])
    dve(lambda e: e.tensor_tensor(out=sm[:, 6:7], in0=sm[:, 5:6], in1=sm[:, 4:5], op=ALU.subtract), [cB], [cB])
    dve(lambda e: e.tensor_scalar(out=neglam, in0=sm[:, 6:7], scalar1=-0.2, scalar2=None, op0=ALU.add), [cB], [cB])
    dve(lambda e: e.tensor_scalar(out=og08, in0=pv2[:, R_DAO:R_DAO + 1], scalar1=0.8, scalar2=None, op0=ALU.mult), [cB], [cB])
    dve(lambda e: e.tensor_scalar(out=naq8, in0=pv2[:, R_NAQ:R_NAQ + 1], scalar1=0.125, scalar2=None, op0=ALU.mult), [cB], [cB])

    for l in range(nlayers):
        mps = psA.next()
        mview = mps[:, 0:144].rearrange("p (n s) -> p n s", s=3)
        for g in range(12):
            wt = modw_r.next()
            dma("sync", wt[:], modw_d[l].rearrange("(k p) n -> p k n", p=128)[:, :, g * 512:(g + 1) * 512], [], [wt])
            for nn in range(4):
                n = g * 4 + nn
                for kc in range(8):
                    mm(mview[:, n, :], wt[:, kc, nn * 128:(nn + 1) * 128], siluT[:, :, kc], kc == 0, kc == 7, [wt, cB], [mps])
        for s in range(3):
            dve(lambda e, s=s, l=l, mview=mview: e.tensor_tensor(out=modv[:, l, :, s], in0=mview[:, :, s], in1=pv1[:, l * 48:(l + 1) * 48], op=ALU.add), [mps, cB], [modB])
        for w_, part, grow in ((0, 1, 96), (1, 4, 112)):
            for s in range(3):
                dve(lambda e, l=l, w_=w_, part=part, grow=grow, s=s: e.scalar_tensor_tensor(
                    out=Av[:, l, w_, s, :], in0=modv[:, l, part * 8:(part + 1) * 8, s], scalar=1.0,
                    in1=pv1[:, grow + l * 8:grow + (l + 1) * 8], op0=ALU.add, op1=ALU.mult), [modB, cB], [modB])

    def Acol(l, w_, s, c):
        return Av[:, l, w_, s, c:c + 1]

    def Mcol(l, part, s, c):
        return modv[:, l, part * 8 + c, s:s + 1]

    S.alias(arena_attn_bufs, arena_pro_bufs)
    ev_src = ev_d.rearrange("(k p) n -> p k n", p=128)
    for nbk in range(16):
        w = 128 if nbk < 15 else 64
        dma("gpsimd", ev_s[nbk, :, :, 0:w], ev_src[:, :, nbk * 128:nbk * 128 + w], [], [evB[nbk]])
    dma("gpsimd", uq_sb, uq_d.rearrange("(k p) n -> p k n", p=128), [], [uqB])
    dma("gpsimd", ukv_sb, ukv_d, [], [ukvB])

    def precast_layer(l):
        dma("gpsimd", wout_s[l], wout_d[l], [], [woutB[l]])
        src = fwi_d[l].rearrange("(k p) n -> p k n", p=128)
        for j in range(22):
            dma("gpsimd", wi_s[l, j, :, :, 0:128], src[:, :, j * 128:(j + 1) * 128], [], [wiB[l][j]])
            dma("gpsimd", wi_s[l, j, :, :, 128:256], src[:, :, DFF + j * 128:DFF + (j + 1) * 128], [], [wiB[l][j]])
        srco = fwo_d[l].rearrange("(j p) n -> p j n", p=128)
        for n in range(8):
            dma("gpsimd", wo_s[l, n], srco[:, :, n * 128:(n + 1) * 128], [], [woB[l][n]])

    precast_layer(0)
    if nlayers > 1:
        od_src = od_d.rearrange("(k p) n -> p k n", p=128)
        for nbk in range(24):
            dma("gpsimd", od_s[nbk], od_src[:, :, nbk * 128:(nbk + 1) * 128], [], [odB[nbk]])
        precast_layer(1)

    def load_x(b):
        S.alias(stg_bufs, arena_attn_bufs)
        for t in range(18):
            st = stg_r.next()
            src = x_d[b, t * 128:(t + 1) * 128, :] if t < 16 else ctx_d[b, (t - 16) * 128:(t - 15) * 128, :]
            dma("sync", st[:], src, [], [st])
            tb = t // 4
            for half in range(2):
                ps = psA.next()
                for cc in range(4):
                    c = half * 4 + cc
                    mm(ps[:, cc * 128:(cc + 1) * 128], st[:, c * 128:(c + 1) * 128], identf[:], True, True, [st, cB], [ps])
                o = xT[:, half * 4:(half + 1) * 4, t * 128:(t + 1) * 128]
                i_ = ps[:, :].rearrange("p (c n) -> p c n", c=4)
                wr = [xB[half * 4 + cc][tb] for cc in range(4)]
                if half == 0:
                    dve(lambda e, o=o, i_=i_: e.tensor_copy(out=o, in_=i_), [ps], wr)
                else:
                    act(o, i_, AF.Copy, [ps], wr)
        S.alias(arena_attn_bufs, stg_bufs)

    def store_out(b):
        ops = []
        S.alias(stg_bufs, arena_attn_bufs)
        for t in range(16):
            st = stg_r.next()
            tb = t // 4
            for half in range(2):
                ps = psA.next()
                for cc in range(4):
                    c = half * 4 + cc
                    mm(ps[:, cc * 128:(cc + 1) * 128], xT[:, c, t * 128:(t + 1) * 128], identf[:], True, True, [xB[c][tb], cB], [ps])
                o = st[:, half * 512:(half + 1) * 512]
                if half == 0:
                    dve(lambda e, o=o, ps=ps: e.tensor_copy(out=o, in_=ps[:, :]), [ps], [st])
                else:
                    act(o, ps[:, :], AF.Copy, [ps], [st])
            ops.append(dma("sync", out_d[b, t * 128:(t + 1) * 128, :], st[:], [st], [outB]))
        S.alias(arena_attn_bufs, stg_bufs)
        return ops

    def norm_mod(l, w_, b, tbs):
        for tb in tbs:
            t0, n = TBLK[tb]
            s = 2 if tb == 4 else b
            ssP = psS.next()
            for c in range(8):
                sq = sq_r.next()
                act(sq[:, 0:n], xT[:, c, t0:t0 + n], AF.Square, [xB[c][tb]], [sq])
                mm(ssP[:, 0:n], onesb, sq[:, 0:n], c == 0, c == 7, [sq, cB], [ssP])
            sd = ft_r.next()
            act(sd[:, 0:n], ssP[:, 0:n], AF.Ln, [ssP, cB], [sd], bias=epsb, scale=1.0 / D)
            rs = rs_r.next()
            act(rs[:, 0:n], sd[:, 0:n], AF.Exp, [sd], [rs], scale=-0.5)
            for c in range(8):
                tmp = ft_r.next()
                dve(lambda e, tmp=tmp, c=c, rs=rs, t0=t0, n=n, s=s: e.scalar_tensor_tensor(
                    out=tmp[:, 0:n], in0=xT[:, c, t0:t0 + n], scalar=Acol(l, w_, s, c), in1=rs[:, 0:n],
                    op0=ALU.mult, op1=ALU.mult), [xB[c][tb], rs, modB], [tmp])
                act(hT[:, c, t0:t0 + n], tmp[:, 0:n], AF.Identity, [tmp, modB], [hB[c][tb]],
                    bias=Mcol(l, 0 if w_ == 0 else 3, s, c))

    def group_norm(tiles, nfeat, tb, rope):
        t0, n = TBLK[tb]
        ssP = psS.next()
        for i, (ps, M, G, gain, dfn, dbufs) in enumerate(tiles):
            sq = sq_r.next()
            act(sq[0:M, 0:n], ps[0:M, 0:n], AF.Square, [ps], [sq])
            mm(ssP[:, 0:n], G, sq[0:M, 0:n], i == 0, i == len(tiles) - 1, [sq, cB], [ssP])
        yield
        sd = ft_r.next()
        act(sd[:, 0:n], ssP[:, 0:n], AF.Ln, [ssP, cB], [sd], bias=epsb, scale=1.0 / nfeat)
        yield
        rs = rs_r.next()
        act(rs[:, 0:n], sd[:, 0:n], AF.Exp, [sd], [rs], scale=-0.5)
        yield
        todo = []
        for (ps, M, G, gain, dfn, dbufs), rp in zip(tiles, rope):
            if not rp:
                dve(lambda e, ps=ps, M=M, gain=gain, dfn=dfn: e.scalar_tensor_tensor(
                    out=dfn(t0, n), in0=ps[0:M, 0:n], scalar=gain, in1=rs[0:M, 0:n], op0=ALU.mult, op1=ALU.mult),
                    [ps, rs, cB], dbufs)
                continue
            xn = ft_r.next()
            dve(lambda e, ps=ps, M=M, gain=gain, xn=xn: e.scalar_tensor_tensor(
                out=xn[0:M, 0:n], in0=ps[0:M, 0:n], scalar=gain, in1=rs[0:M, 0:n], op0=ALU.mult, op1=ALU.mult),
                [ps, rs, cB], [xn])
            todo.append((M, dfn, dbufs, xn))
        if not todo:
            return
        yield
        st2 = []
        for (M, dfn, dbufs, xn) in todo:
            xb = sq_r.next()
            act(xb[0:M, 0:n], xn[0:M, 0:n], AF.Copy, [xn], [xb])
            pp = psS.next()
            mm(pp[0:M, 0:n], permb[0:M, 0:M], xb[0:M, 0:n], True, True, [xb, cB], [pp])
            st2.append((M, dfn, dbufs, xn, pp))
        yield
        st3 = []
        for (M, dfn, dbufs, xn, pp) in st2:
            dve(lambda e, xn=xn, M=M: e.tensor_tensor(out=xn[0:M, 0:n], in0=xn[0:M, 0:n], in1=ropecs[0:M, 0, t0:t0 + n], op=ALU.mult),
                [xn, ropeB], [xn])
            t2 = ft_r.next()
            dve(lambda e, pp=pp, t2=t2, M=M: e.tensor_tensor(out=t2[0:M, 0:n], in0=pp[0:M, 0:n], in1=ropecs[0:M, 1, t0:t0 + n], op=ALU.mult),
                [pp, ropeB], [t2])
            st3.append((M, dfn, dbufs, xn, t2))
        yield
        for (M, dfn, dbufs, xn, t2) in st3:
            dve(lambda e, xn=xn, t2=t2, M=M, dfn=dfn: e.tensor_tensor(out=dfn(t0, n), in0=xn[0:M, 0:n], in1=t2[0:M, 0:n], op=ALU.add),
                [xn, t2], dbufs)

    def run_chains(*gens):
        gens = list(gens)
        while gens:
            for g in list(gens):
                try:
                    next(g)
                except StopIteration:
                    gens.remove(g)

    def load_wtile(scr, scrB, ring=None):
        wt = (ring or wproj_r).next()
        dma("sync", wt[:], scr.rearrange("p k n -> p (k n)"), [scrB], [wt])
        return wt

    def proj_fm(wt, M, tb, col0=0, wstride=128):
        t0, n = TBLK[tb]
        ps = psA.next()
        for kc in range(8):
            mm(ps[0:M, 0:n], wt[:, kc * wstride + col0:kc * wstride + col0 + M], hT[:, kc, t0:t0 + n], kc == 0, kc == 7,
               [wt, hB[kc][tb]], [ps])
        return ps

    def proj_v_tm(wt, slot, ntiles=18):
        for t in range(ntiles):
            tb = t // 4
            ps = psS.next()
            for kc in range(8):
                mm(ps[:, 0:128], hT[:, kc, t * 128:(t + 1) * 128], wt[:, kc * 128:(kc + 1) * 128], kc == 0, kc == 7,
                   [wt, hB[kc][tb]], [ps])
            o = slots[:, slot, t * 128:(t + 1) * 128]
            if t % 2 == 0:
                dve(lambda e, o=o, ps=ps: e.tensor_copy(out=o, in_=ps[:, 0:128]), [ps], [slB[slot][tb]])
            else:
                act(o, ps[:, 0:128], AF.Copy, [ps], [slB[slot][tb]])

    def attn_core(units, scale, group=2, need_sm=True):
        O = psA.next()
        Sm = psA.next() if need_sm else None
        groups = [units[i:i + group] for i in range(0, len(units), group)]
        sts = {}

        def emit_qk(gi):
            for ui, u in enumerate(groups[gi]):
                st = psS.next()
                u[1](st)
                sts[(gi, ui)] = st
        emit_qk(0)
        if len(groups) > 1:
            emit_qk(1)
        n_units = len(units)
        done = 0
        for gi, g in enumerate(groups):
            ps = []
            for ui, u in enumerate(g):
                st = sts.pop((gi, ui))
                p = p_r.next()
                act(p[:, 0:u[0]], st[:, 0:u[0]], AF.Exp, [st], [p], scale=scale)
                ps.append(p)
            for ui in reversed(range(len(g))):
                first = done == 0
                done += 1
                g[ui][2](ps[ui], O, Sm, first, done == n_units)
            if gi + 2 < len(groups):
                emit_qk(gi + 2)
        return O, Sm

    def tbs_of(t0, n):
        return sorted(set([t0 // 512, (t0 + n - 1) // 512]))

    def wout_accum(l, wt, yb, tb, b, q0=None, nq=None):
        t0, n = TBLK[tb]
        if q0 is not None:
            t0, n = q0, nq
        s = 2 if tb == 4 else b
        tbl_ = tbs_of(t0, n)
        for nchk in range(8):
            ps = psS.next()
            mm(ps[:, 0:n], wt[:, nchk * 128:(nchk + 1) * 128], yb[:, 0:n], True, True, [wt, yb], [ps])
            xbs = [xB[nchk][t_] for t_ in tbl_]
            dve(lambda e, ps=ps, nchk=nchk, t0=t0, n=n, s=s: e.scalar_tensor_tensor(
                out=xT[:, nchk, t0:t0 + n], in0=ps[:, 0:n], scalar=Mcol(l, 2, s, nchk), in1=xT[:, nchk, t0:t0 + n],
                op0=ALU.mult, op1=ALU.add), [ps, modB] + xbs, xbs)

    def ffn(l, b, tbs):
        S.alias(arena_ffn_bufs, arena_attn_bufs)
        for tb in tbs:
            t0, n = TBLK[tb]
            s = 2 if tb == 4 else b
            for j in range(22):
                wt = load_wtile(wi_s[l, j], wiB[l][j], wA_r)
                G = psS.next()
                U = psS.next()
                for kc in range(8):
                    mm(G[:, 0:n], wt[:, kc * 256:kc * 256 + 128], hT[:, kc, t0:t0 + n], kc == 0, kc == 7, [wt, hB[kc][tb]], [G])
                for kc in range(8):
                    mm(U[:, 0:n], wt[:, kc * 256 + 128:kc * 256 + 256], hT[:, kc, t0:t0 + n], kc == 0, kc == 7, [wt, hB[kc][tb]], [U])
                sg = ft_r.next()
                act(sg[:, 0:n], G[:, 0:n], AF.Silu, [G], [sg])
                dve(lambda e, sg=sg, U=U, j=j, n=n: e.tensor_tensor(out=a_sb[:, j, 0:n], in0=sg[:, 0:n], in1=U[:, 0:n], op=ALU.mult),
                    [sg, U], [aB[j]])
            for nchk in range(8):
                wt = load_wtile(wo_s[l, nchk], woB[l][nchk], wB_r)
                ps = psA.next()
                for j in range(22):
                    mm(ps[:, 0:n], wt[:, j * 128:(j + 1) * 128], a_sb[:, j, 0:n], j == 0, j == 21, [wt, aB[j]], [ps])
                dve(lambda e, ps=ps, nchk=nchk, t0=t0, n=n, s=s: e.scalar_tensor_tensor(
                    out=xT[:, nchk, t0:t0 + n], in0=ps[:, 0:n], scalar=Mcol(l, 5, s, nchk), in1=xT[:, nchk, t0:t0 + n],
                    op0=ALU.mult, op1=ALU.add), [ps, modB, xB[nchk][tb]], [xB[nchk][tb]])
        S.alias(arena_attn_bufs, arena_ffn_bufs)

    ALLCH = list(range(18))
    CTXCH = [16, 17]

    def da_qblock(tb, wo_t, b):
        l = 0
        SQ, SK, SV = 0, 1, 2
        t0, nq = TBLK[tb]
        chunks = ALLCH if tb < 4 else CTXCH
        ons = []
        for i in range(2):
            def mk_unit(kc, i=i):
                def qk_fn(st):
                    mm(st[:, 0:nq], slots[64 * i:64 * i + 64, SK, kc * 128:(kc + 1) * 128], slots[64 * i:64 * i + 64, SQ, t0:t0 + nq],
                       True, True, [slB[SK][kc // 4], slB[SQ][tb]], [st])

                def av_fn(p, O, Sm, first, last):
                    mm(O[:, 0:nq], slots[:, SV, kc * 128:(kc + 1) * 128], p[:, 0:nq], first, last, [p, slB[SV][kc // 4]], [O])
                    mm(Sm[:, 0:nq], onesb, p[:, 0:nq], first, last, [p, cB], [Sm])
                return (nq, qk_fn, av_fn)
            O, Sm = attn_core([mk_unit(kc) for kc in chunks], 0.125)
            rs = rs_r.next()
            dve(lambda e, rs=rs, Sm=Sm: e.reciprocal(out=rs[:, 0:nq], in_=Sm[:, 0:nq]), [Sm], [rs])
            on = ft_r.next()
            dve(lambda e, on=on, O=O, rs=rs: e.tensor_tensor(out=on[:, 0:nq], in0=O[:, 0:nq], in1=rs[:, 0:nq], op=ALU.mult), [O, rs], [on])
            ons.append(on)
        dd = dd_r.next()
        dve(lambda e: e.scalar_tensor_tensor(out=dd[:, 0:nq], in0=ons[1][:, 0:nq], scalar=neglam, in1=ons[0][:, 0:nq],
                                             op0=ALU.mult, op1=ALU.add), [ons[0], ons[1], cB], [dd])

        def tail():
            sq = sq_r.next()
            act(sq[:, 0:nq], dd[:, 0:nq], AF.Square, [dd], [sq])
            ssP = psS.next()
            mm(ssP[:, 0:nq], onesb, sq[:, 0:nq], True, True, [sq, cB], [ssP])
            sd = ft_r.next()
            act(sd[:, 0:nq], ssP[:, 0:nq], AF.Ln, [ssP, cB], [sd], bias=epsb, scale=1.0 / 128)
            rs2 = rs_r.next()
            act(rs2[:, 0:nq], sd[:, 0:nq], AF.Exp, [sd], [rs2], scale=-0.5)
            yb = y_r.next()
            dve(lambda e: e.scalar_tensor_tensor(out=yb[:, 0:nq], in0=dd[:, 0:nq], scalar=og08, in1=rs2[:, 0:nq],
                                                 op0=ALU.mult, op1=ALU.mult), [dd, rs2, cB], [yb])
            wout_accum(l, wo_t, yb, tb, b)
        return tail

    def mla_qblock(tb, wo_t, b):
        l = 0
        SCQ0, SCQ1, SCKV, SKR, SQN, SQR, SKN, SMV = range(8)
        t0, nq = TBLK[tb]
        chunks = ALLCH if tb < 4 else CTXCH

        def mk_unit(kc):
            def qk_fn(st):
                mm(st[:, 0:nq], slots[:, SKN, kc * 128:(kc + 1) * 128], slots[:, SQN, t0:t0 + nq], True, False,
                   [slB[SKN][kc // 4], slB[SQN][tb]], [st])
                mm(st[:, 0:nq], slots[0:64, SKR, kc * 128:(kc + 1) * 128], slots[0:64, SQR, t0:t0 + nq], False, True,
                   [slB[SKR][kc // 4], slB[SQR][tb]], [st])

            def av_fn(p, O, Sm, first, last):
                mm(O[:, 0:nq], slots[:, SMV, kc * 128:(kc + 1) * 128], p[:, 0:nq], first, last, [p, slB[SMV][kc // 4]], [O])
                mm(Sm[:, 0:nq], onesb, p[:, 0:nq], first, last, [p, cB], [Sm])
            return (nq, qk_fn, av_fn)
        O, Sm = attn_core([mk_unit(kc) for kc in chunks], 192.0 ** -0.5)
        rs = rs_r.next()
        dve(lambda e: e.reciprocal(out=rs[:, 0:nq], in_=Sm[:, 0:nq]), [Sm], [rs])
        yb = y_r.next()
        dve(lambda e: e.tensor_tensor(out=yb[:, 0:nq], in0=O[:, 0:nq], in1=rs[:, 0:nq], op=ALU.mult), [O, rs], [yb])
        return lambda: wout_accum(l, wo_t, yb, tb, b)

    def mla_head_proj(h, tb):
        SCQ0, SCQ1, SCKV, SKR, SQN, SQR, SKN, SMV = range(8)
        t0, n = TBLK[tb]
        pn = psA.next()
        for kc in range(2):
            mm(pn[:, 0:n], uq_sb[:, kc, 192 * h:192 * h + 128], slots[:, SCQ0 + kc, t0:t0 + n], kc == 0, kc == 1, [uqB, slB[SCQ0 + kc][tb]], [pn])
        pr = psA.next()
        for kc in range(2):
            mm(pr[0:64, 0:n], uq_sb[:, kc, 192 * h + 128:192 * h + 192], slots[:, SCQ0 + kc, t0:t0 + n], kc == 0, kc == 1, [uqB, slB[SCQ0 + kc][tb]], [pr])
        pk = psA.next()
        mm(pk[:, 0:n], ukv_sb[:, 256 * h:256 * h + 128], slots[:, SCKV, t0:t0 + n], True, True, [ukvB, slB[SCKV][tb]], [pk])
        run_chains(
            group_norm([(pn, 128, onesb, pv2[:, R_MQN:R_MQN + 1], lambda t0, n: slots[:, SQN, t0:t0 + n], [slB[SQN][tb]]),
                        (pr, 64, onesb[0:64, :], pv2[0:64, R_MQR:R_MQR + 1], lambda t0, n: slots[0:64, SQR, t0:t0 + n], [slB[SQR][tb]])],
                       192, tb, [False, tb < 4]),
            group_norm([(pk, 128, onesb, pv2[:, R_MK:R_MK + 1], lambda t0, n: slots[:, SKN, t0:t0 + n], [slB[SKN][tb]])], 128, tb, [False]))

    import os as _os
    NA_OLDV = _os.environ.get("NA_OLDV", "0") == "1"
    NA_KEEPSUM = NA_OLDV or _os.environ.get("NA_KEEPSUM", "0") == "1"
    NA_BLOCKS = [(0, 4), (4, 8), (12, 8), (20, 8), (28, 4)]
    if _os.environ.get("NA_R4", "0") == "1":
        NA_BLOCKS = [(4 * i, 4) for i in range(8)]

    def na_qblock(m, blk, wo_t, nab, b):
        l = 1
        SQ, SK, SV0 = 0, 1, 2
        q0, R = NA_BLOCKS[blk]
        t0, nq = q0 * 64, R * 64
        qtb = tbs_of(t0, nq)
        qbufs = [slB[SQ][t_] for t_ in qtb]
        if q0 == 0 or q0 == 28:
            lrows = [0, 2, 4, 6] if q0 == 0 else [24, 26, 28, 30]
            nbase = lambda r: (7 - (r - q0) - 1) * 64
        else:
            lrows = [q0 - 4 + 2 * i for i in range(R // 2 + 4)]
            nbase = lambda r: (14 + 10 - (r - q0)) * 64
        chunks = [("l", r) for r in lrows] + [("c", 32), ("c", 34)]
        per_unit = 512 // nq
        yb = y_r.next()
        for e_ in range(2):
            pl, ph = 64 * e_, 64 * e_ + 64
            ol, oh = 64 - pl, 128 - pl

            def mk_unit(grp, pl=pl, ph=ph, e_=e_):
                def qk_fn(st):
                    for hf, (kind, r) in enumerate(grp):
                        kc = r // 2
                        mm(st[:, hf * nq:(hf + 1) * nq], slots[pl:ph, SK, kc * 128:(kc + 1) * 128], slots[pl:ph, SQ, t0:t0 + nq], True, kind == "c",
                           [slB[SK][kc // 4]] + qbufs, [st])
                        if kind == "l":
                            nb_ = nbase(r)
                            mm(st[:, hf * nq:(hf + 1) * nq], identb, nab[:, e_, nb_:nb_ + nq], False, True, [nab, cB], [st])

                def av_fn(p, O, Sm, first, last):
                    for hf, (kind, r) in enumerate(grp):
                        kc = r // 2
                        vs = SV0 if NA_OLDV else SV0 + e_
                        mm(O[:, 0:nq], slots[:, vs, kc * 128:(kc + 1) * 128], p[:, hf * nq:(hf + 1) * nq],
                           first and hf == 0, last and hf == len(grp) - 1, [p, slB[vs][kc // 4]], [O])
                        if NA_KEEPSUM:
                            mm(Sm[:, 0:nq], onesb, p[:, hf * nq:(hf + 1) * nq], first and hf == 0, last and hf == len(grp) - 1, [p, cB], [Sm])
                return (nq * len(grp), qk_fn, av_fn)
            O, Sm = attn_core([mk_unit(chunks[i:i + per_unit]) for i in range(0, len(chunks), per_unit)], 1.0, need_sm=NA_KEEPSUM)
            rs = rs_r.next()
            if NA_KEEPSUM:
                dve(lambda e, rs=rs, Sm=Sm, pl=pl, ph=ph: e.reciprocal(out=rs[pl:ph, 0:nq], in_=Sm[pl:ph, 0:nq]), [Sm], [rs])
            else:
                dve(lambda e, rs=rs, O=O, pl=pl, ph=ph, ol=ol, oh=oh: e.reciprocal(out=rs[pl:ph, 0:nq], in_=O[ol:oh, 0:nq]), [O], [rs])
            dve(lambda e, O=O, rs=rs, pl=pl, ph=ph: e.tensor_tensor(out=yb[pl:ph, 0:nq], in0=O[pl:ph, 0:nq], in1=rs[pl:ph, 0:nq], op=ALU.mult),
                [O, rs], [yb])
        return lambda: wout_accum(l, wo_t, yb, qtb[0], b, q0=t0, nq=nq)

    def proj_v_na(wt):
        for t in range(18):
            tb = t // 4
            ps = psS.next()
            for kc in range(8):
                mm(ps[:, 0:128], hT[:, kc, t * 128:(t + 1) * 128], wt[:, kc * 128:(kc + 1) * 128], kc == 0, kc == 7,
                   [wt, hB[kc][tb]], [ps])
            a0 = 2 * T + t * 128
            o = arena[:, a0:a0 + 2 * (T + 64)].rearrange("p (s c) -> p s c", s=2)[:, :, 0:64]
            i_ = ps[:, 0:128].rearrange("p (s c) -> p s c", s=2)
            if t % 2 == 0:
                dve(lambda e, o=o, i_=i_: e.tensor_copy(out=o, in_=i_), [ps], [slB[2][tb], slB[3][tb]])
            else:
                act(o, i_, AF.Copy, [ps], [slB[2][tb], slB[3][tb]])

    def qblocks(need_ctx):
        return [0, 1, 2, 3, 4] if need_ctx else [0, 1, 2, 3]

    class Pend:
        def __init__(self):
            self.t = None

        def push(self, tail):
            old = self.t
            self.t = tail
            if old is not None:
                old()

        def flush(self):
            if self.t is not None:
                self.t()
            self.t = None

    def layer0(b):
        l = 0
        S.alias([ropeB] + [bb for sl in slB[4:8] for bb in sl], nab_bufs)
        dma("sync", ropecs, rope_d, [], [ropeB])
        norm_mod(l, 0, b, range(5))
        gq = pv2[:, R_DAQ:R_DAQ + 1]
        gk = pv2[:, R_DAK:R_DAK + 1]
        SQ, SK, SV = 0, 1, 2
        pend = Pend()
        for h in range(4):
            wq = load_wtile(ev_s[h], evB[h])
            wk = load_wtile(ev_s[4 + h], evB[4 + h])
            wv = load_wtile(ev_s[8 + h], evB[8 + h])
            for tb in range(5):
                rp = tb < 4
                psq = proj_fm(wq, 128, tb)
                psk = proj_fm(wk, 128, tb)
                run_chains(group_norm([(psq, 128, blk64, gq, lambda t0, n: slots[:, SQ, t0:t0 + n], [slB[SQ][tb]])], 64, tb, [rp]),
                           group_norm([(psk, 128, blk64, gk, lambda t0, n: slots[:, SK, t0:t0 + n], [slB[SK][tb]])], 64, tb, [rp]))
            proj_v_tm(wv, SV)
            pend.flush()
            wo_t = wproj_r.next()
            dma("sync", wo_t[:], wout_s[l, h * 128:(h + 1) * 128, :], [woutB[l]], [wo_t])
            for tb in range(5):
                pend.push(da_qblock(tb, wo_t, b))
        pend.flush()
        SCQ0, SCQ1, SCKV, SKR, SQN, SQR, SKN, SMV = range(8)
        wcq0 = load_wtile(ev_s[12], evB[12])
        wcq1 = load_wtile(ev_s[13], evB[13])
        wckv = load_wtile(ev_s[14], evB[14])
        wkr = load_wtile(ev_s[15], evB[15])
        for tb in range(5):
            p0 = proj_fm(wcq0, 128, tb)
            p1 = proj_fm(wcq1, 128, tb)
            p2 = proj_fm(wckv, 128, tb)
            p3 = proj_fm(wkr, 64, tb)
            run_chains(
                group_norm([(p0, 128, onesb, pv2[:, R_QA0:R_QA0 + 1], lambda t0, n: slots[:, SCQ0, t0:t0 + n], [slB[SCQ0][tb]]),
                            (p1, 128, onesb, pv2[:, R_QA1:R_QA1 + 1], lambda t0, n: slots[:, SCQ1, t0:t0 + n], [slB[SCQ1][tb]])], 256, tb, [False, False]),
                group_norm([(p2, 128, onesb, pv2[:, R_KVA:R_KVA + 1], lambda t0, n: slots[:, SCKV, t0:t0 + n], [slB[SCKV][tb]])], 128, tb, [False]))
            run_chains(
                group_norm([(p3, 64, onesb[0:64, :], pv2[0:64, R_MKR:R_MKR + 1], lambda t0, n: slots[0:64, SKR, t0:t0 + n], [slB[SKR][tb]])], 64, tb, [tb < 4]))
        for h in range(4):
            for tb in range(5):
                mla_head_proj(h, tb)
            for t in range(18):
                tb = t // 4
                ps = psS.next()
                mm(ps[:, 0:128], slots[:, SCKV, t * 128:(t + 1) * 128], ukv_sb[:, 256 * h + 128:256 * h + 256], True, True, [ukvB, slB[SCKV][tb]], [ps])
                o = slots[:, SMV, t * 128:(t + 1) * 128]
                if t % 2 == 0:
                    dve(lambda e, o=o, ps=ps: e.tensor_copy(out=o, in_=ps[:, 0:128]), [ps], [slB[SMV][tb]])
                else:
                    act(o, ps[:, 0:128], AF.Copy, [ps], [slB[SMV][tb]])
            pend.flush()
            wo_t = wproj_r.next()
            dma("sync", wo_t[:], wout_s[l, (4 + h) * 128:(5 + h) * 128, :], [woutB[l]], [wo_t])
            for tb in range(5):
                pend.push(mla_qblock(tb, wo_t, b))
        pend.flush()
        norm_mod(l, 1, b, range(5))
        ffn(l, b, range(5))

    def layer1(b):
        l = 1
        import os as _os
        S.alias(nab_bufs, [ropeB] + [bb for sl in slB[4:8] for bb in sl])
        norm_mod(l, 0, b, range(5))
        SQ, SK, SV = 0, 1, 2
        if (not NA_OLDV) or _os.environ.get('NA_T1', '0') == '1':
            for t in range(18):
                o0 = slots[:, 2, t * 128 + 64:(t + 1) * 128]
                o1 = slots[:, 3, t * 128:t * 128 + 64]
                dve(lambda e, o0=o0: e.tensor_copy(out=o0, in_=onesb[:, 0:64]), [cB], [slB[2][t // 4]])
                act(o1, onesb[:, 0:64], AF.Copy, [cB], [slB[3][t // 4]])
        gq = naq8
        gk = pv2[:, R_NAK:R_NAK + 1]
        pend = Pend()
        for m in range(8):
            wq = load_wtile(od_s[m], odB[m])
            wk = load_wtile(od_s[8 + m], odB[8 + m])
            wv = load_wtile(od_s[16 + m], odB[16 + m])
            nab = nab_r.next()
            dma("gpsimd", nab[:], nab_d[2 * m:2 * m + 2].rearrange("h p f -> p h f"), [], [nab])
            for tb in range(5):
                psk = proj_fm(wk, 128, tb)
                ch = [group_norm([(psk, 128, blk64, gk, lambda t0, n: slots[:, SK, t0:t0 + n], [slB[SK][tb]])], 64, tb, [False])]
                if tb < 4:
                    psq = proj_fm(wq, 128, tb)
                    ch.append(group_norm([(psq, 128, blk64, gq, lambda t0, n: slots[:, SQ, t0:t0 + n], [slB[SQ][tb]])], 64, tb, [False]))
                run_chains(*ch)
            if NA_OLDV and _os.environ.get('NA_T2', '0') != '1':
                proj_v_tm(wv, 2)
            else:
                proj_v_na(wv)
            pend.flush()
            wo_t = wproj_r.next()
            dma("sync", wo_t[:], wout_s[l, m * 128:(m + 1) * 128, :], [woutB[l]], [wo_t])
            import os as _os
            for blk in [int(v) for v in _os.environ.get("NA_DEBUG_BLOCKS", ",".join(str(i) for i in range(len(NA_BLOCKS)))).split(",")]:
                pend.push(na_qblock(m, blk, wo_t, nab, b))
        pend.flush()
        norm_mod(l, 1, b, range(4))
        ffn(l, b, range(4))

    final_ops = []
    for b in range(nb):
        load_x(b)
        if dbg == "x0":
            break
        layer0(b)
        if dbg == "l0" or nlayers == 1:
            break
        layer1(b)
        final_ops += store_out(b)
    if dbg is not None:
        for c in range(8):
            final_ops.append(dma("sync", dbg_d[:, c, :], xT[:, c, :], [xB[c][tb] for tb in range(5)], [outB]))
    S.emit(final_ops)
    return nc


_NC_CACHE = {}


def kernel(**inputs):
    inp = {k: np.ascontiguousarray(np.asarray(v)) for k, v in inputs.items()}
    if "nc" not in _NC_CACHE:
        _NC_CACHE["nc"] = build_nc()
    nc = _NC_CACHE["nc"]
    cmat = _const_mats()
    rope = _rope_tables()
    nab = _na_bias_tables(inp["na_rpb"][0])
    lamv = np.stack([inp["da_lq1"][0], inp["da_lk1"][0], inp["da_lq2"][0], inp["da_lk2"][0]]).astype(np.float32)
    in_maps = []
    for core in range(NCORES):
        b0 = core * NB
        pm1, pm2 = _pack_params(inp, b0)
        in_maps.append({
            "x": inp["x"][b0:b0 + NB], "ctx": inp["ctx"][b0:b0 + NB],
            "mod_w": inp["mod_w"], "w_out": inp["w_out"], "ffn_w_in": inp["ffn_w_in"], "ffn_w_out": inp["ffn_w_out"],
            "ev_w_in": inp["ev_w_in"][0], "mla_w_uq": inp["mla_w_uq"][0], "mla_w_ukv": inp["mla_w_ukv"][0],
            "od_w_in": inp["od_w_in"][0], "lamv": lamv, "pm1": pm1, "pm2": pm2, "cmat": cmat, "ropecs": rope, "nabias": nab,
        })
    res = run_bass_kernel_spmd(nc, in_maps, core_ids=list(range(NCORES)))
    out = np.concatenate([np.asarray(r["out"]) for r in res.results], axis=0)
    return out.astype(np.float32)
```

```python
import contextlib
import numpy as np
import concourse.bass as bass
import concourse.mybir as mybir
from concourse.bass_utils import run_bass_kernel_spmd

F32 = mybir.dt.float32
BF16 = mybir.dt.bfloat16
ALU = mybir.AluOpType
AF = mybir.ActivationFunctionType
AX = mybir.AxisListType

NCORES = 8
NB = 2
D = 1024
T = 2304
TL = 2048
TBLK = [(0, 512), (512, 512), (1024, 512), (1536, 512), (2048, 256)]
DFF = 2816
EPS = 1e-6
NEG = -1e30

ENGINES = ("tensor", "vector", "scalar", "gpsimd", "sync")
DMA_POOL = {"sync": (0, 14), "gpsimd": (14, 8), "scalar": (22, 2)}
N_DMA_SEMS = 24
SAME_ENGINE_SYNC = True


class Buf:
    __slots__ = ("name", "writer", "readers", "gen")

    def __init__(self, name):
        self.name = name
        self.writer = None
        self.readers = []
        self.gen = 0


class Tile:
    __slots__ = ("ap", "buf", "gen")

    def __init__(self, ap, buf):
        self.ap = ap
        self.buf = buf
        self.gen = buf.gen

    def __getitem__(self, k):
        return self.ap[k]


class Ring:
    def __init__(self, aps, name):
        self.items = [(ap, Buf("%s%d" % (name, i))) for i, ap in enumerate(aps)]
        self.i = 0

    def next(self):
        ap, buf = self.items[self.i % len(self.items)]
        self.i += 1
        buf.gen += 1
        return Tile(ap, buf)


def _unwrap(lst):
    out = []
    for b in lst:
        if isinstance(b, Tile):
            assert b.gen == b.buf.gen, "stale ring tile %s" % b.buf.name
            out.append(b.buf)
        else:
            out.append(b)
    return out


class Op:
    __slots__ = ("eng", "fn", "is_dma", "cdeps", "ddeps", "idx", "flag", "count", "dsem", "dval", "dprev")

    def __init__(self, eng, fn, is_dma):
        self.eng = eng
        self.fn = fn
        self.is_dma = is_dma
        self.cdeps = {}
        self.ddeps = set()
        self.flag = False
        self.count = 0
        self.dsem = None
        self.dval = 0
        self.dprev = None


class Sched:
    def __init__(self, nc):
        self.nc = nc
        self.ops = []
        self.per_eng = {e: [] for e in ENGINES}
        self.n_dma_e = {}

    def _adddep(self, o, d):
        if d is None or d is o:
            return
        if d.is_dma:
            o.ddeps.add(d)
        else:
            if d.eng == o.eng and not o.is_dma:
                if o.eng == "tensor" or not SAME_ENGINE_SYNC:
                    return
            cur = o.cdeps.get(d.eng)
            if cur is None or cur.idx < d.idx:
                o.cdeps[d.eng] = d

    def op(self, eng, fn, reads=(), writes=(), dma=False):
        reads = _unwrap(reads)
        writes = _unwrap(writes)
        o = Op(eng, fn, dma)
        o.idx = len(self.ops)
        for b in reads:
            self._adddep(o, b.writer)
        for b in writes:
            self._adddep(o, b.writer)
            for r in b.readers:
                self._adddep(o, r)
        for b in reads:
            b.readers.append(o)
        for b in writes:
            b.writer = o
            b.readers = []
        if dma:
            base, cnt = DMA_POOL[eng]
            k = self.n_dma_e.get(eng, 0)
            o.dsem = base + k % cnt
            o.dval = 16 * (k // cnt + 1)
            self.n_dma_e[eng] = k + 1
        self.ops.append(o)
        self.per_eng[eng].append(o)
        return o

    def alias(self, new_bufs, old_bufs):
        users = []
        for b in old_bufs:
            if b.writer is not None:
                users.append(b.writer)
            users.extend(b.readers)
        for b in new_bufs:
            b.writer = None
            b.readers = list(users)

    def emit(self, final_wait_ops):
        nc = self.nc
        for o in self.ops:
            for d in o.cdeps.values():
                d.flag = True
        for o in final_wait_ops:
            if not o.is_dma:
                o.flag = True
        for e in ENGINES:
            c = 0
            for o in self.per_eng[e]:
                if o.flag and not o.is_dma:
                    c += 1
                o.count = c
        last_on_sem = {}
        for o in self.ops:
            if o.is_dma:
                o.dprev = last_on_sem.get(o.dsem)
                last_on_sem[o.dsem] = o
        with contextlib.ExitStack() as es:
            esem = {e: es.enter_context(nc.semaphore("s_" + e)) for e in ENGINES}
            dsems = [es.enter_context(nc.semaphore("d_%d" % i)) for i in range(N_DMA_SEMS)]
            block = es.enter_context(nc.Block())

            def run_engine(e, eng):
                known = {x: 0 for x in ENGINES}
                dknown = [0] * N_DMA_SEMS
                for o in self.per_eng[e]:
                    dw = list(o.ddeps)
                    if o.is_dma and o.dprev is not None:
                        dw.append(o.dprev)
                    for d in dw:
                        if dknown[d.dsem] < d.dval:
                            eng.wait_ge(dsems[d.dsem], d.dval)
                            dknown[d.dsem] = d.dval
                    for d in o.cdeps.values():
                        if known[d.eng] < d.count:
                            eng.wait_ge(esem[d.eng], d.count)
                            known[d.eng] = d.count
                    ins = o.fn(eng)
                    if o.is_dma:
                        ins.then_inc(dsems[o.dsem], 16)
                    elif o.flag:
                        ins.then_inc(esem[e], 1)
                if e == "sync":
                    for d in final_wait_ops:
                        if d.is_dma:
                            eng.wait_ge(dsems[d.dsem], d.dval)
                        else:
                            eng.wait_ge(esem[d.eng], d.count)

            @block.tensor
            def _(eng):
                run_engine("tensor", eng)

            @block.vector
            def _(eng):
                run_engine("vector", eng)

            @block.scalar
            def _(eng):
                run_engine("scalar", eng)

            @block.gpsimd
            def _(eng):
                run_engine("gpsimd", eng)

            @block.sync
            def _(eng):
                run_engine("sync", eng)


def _rope_tables():
    t = np.arange(TL)
    row = (t // 64).astype(np.float32)
    col = (t % 64).astype(np.float32)
    inv = (np.float32(10000.0) ** (-np.arange(0, 32, 2, dtype=np.float32) / np.float32(32))).astype(np.float32)
    ar = (row[:, None] * inv).astype(np.float32)
    ac = (col[:, None] * inv).astype(np.float32)
    cr, sr, cc, sc = np.cos(ar), np.sin(ar), np.cos(ac), np.sin(ac)
    C = np.zeros((64, TL), np.float32)
    Sg = np.zeros((64, TL), np.float32)
    for d in range(64):
        i = d % 16
        first = (d % 32) < 16
        if d < 32:
            C[d] = cr[:, i]
            Sg[d] = -sr[:, i] if first else sr[:, i]
        else:
            C[d] = cc[:, i]
            Sg[d] = -sc[:, i] if first else sc[:, i]
    cs = np.zeros((128, 2, TL), np.float32)
    cs[:64, 0] = C
    cs[64:, 0] = C
    cs[:64, 1] = Sg
    cs[64:, 1] = Sg
    return cs


def _const_mats():
    m = np.zeros((128, 4, 128), np.float32)
    m[:, 0, :] = np.eye(128, dtype=np.float32)
    m[:, 1, :] = 1.0
    m[:64, 2, :64] = 1.0
    m[64:, 2, 64:] = 1.0
    for mm_ in range(128):
        d = mm_ % 64
        p = d + 16 if (d % 32) < 16 else d - 16
        m[(mm_ // 64) * 64 + p, 3, mm_] = 1.0
    return m


NAB_F = 36 * 64


def _na_bias_tables(rpb):
    kc = np.arange(64)[:, None]
    qc = np.arange(64)[None, :]
    cs = np.clip(qc - 8, 0, 48)
    colv = (kc >= cs) & (kc < cs + 16)
    dc = np.clip(kc - qc, -15, 15) + 15
    out = np.full((16, 128, 36, 64), NEG, np.float32)
    for kr2 in range(2):
        rows = slice(kr2 * 64, (kr2 + 1) * 64)
        for j in range(1, 15):
            dr = kr2 + 7 - j
            if -7 <= dr <= 7:
                out[:, rows, j - 1, :] = np.where(colv[None], rpb[:, dr + 7][:, dc], np.float32(NEG))
        for j in range(22):
            dr = kr2 + 10 - j
            if -4 <= dr <= 3:
                out[:, rows, 14 + j, :] = np.where(colv[None], rpb[:, dr + 7][:, dc], np.float32(NEG))
    return out.reshape(16, 128, NAB_F)


R_C = 0
R_DAQ, R_DAK, R_DAO, R_QA0, R_QA1, R_KVA, R_MQN, R_MQR, R_MK, R_MKR, R_NAQ, R_NAK = range(24, 36)
NPM2 = 64


def _pack_params(inp, b0):
    pm1 = np.zeros((128, 128), np.float32)
    pm1[0:96] = inp["mod_b"].reshape(96, 128)
    pm1[96:112] = inp["norm_mix_g"].reshape(16, 128)
    pm1[112:128] = inp["norm_ffn_g"].reshape(16, 128)
    pm2 = np.zeros((NPM2, 128), np.float32)
    pm2[0:8] = inp["c"][b0].reshape(8, 128)
    pm2[8:16] = inp["c"][b0 + 1].reshape(8, 128)
    pm2[16:24] = inp["c_ctx"].reshape(8, 128)
    rep = lambda v: np.concatenate([v, v])
    pm2[R_DAQ] = rep(inp["da_q_g"][0])
    pm2[R_DAK] = rep(inp["da_k_g"][0])
    pm2[R_DAO] = inp["da_out_g"][0]
    pm2[R_QA0] = inp["mla_q_a_g"][0][:128]
    pm2[R_QA1] = inp["mla_q_a_g"][0][128:]
    pm2[R_KVA] = inp["mla_kv_a_g"][0]
    pm2[R_MQN] = inp["mla_q_g"][0][:128]
    pm2[R_MQR] = rep(inp["mla_q_g"][0][128:])
    pm2[R_MK] = inp["mla_k_g"][0]
    pm2[R_MKR] = rep(inp["mla_kr_g"][0])
    pm2[R_NAQ] = rep(inp["na_q_g"][0])
    pm2[R_NAK] = rep(inp["na_k_g"][0])
    return pm1, pm2


def build_nc(nb=NB, nlayers=2, dbg=None):
    nc = bass.Bass("TRN2", target_bir_lowering=False)
    dt_in = lambda name, shape: nc.dram_tensor(name, shape, F32, kind="ExternalInput").ap()
    x_d = dt_in("x", [nb, TL, D])
    ctx_d = dt_in("ctx", [nb, 256, D])
    modw_d = dt_in("mod_w", [2, D, 6 * D])
    wout_d = dt_in("w_out", [2, D, D])
    fwi_d = dt_in("ffn_w_in", [2, D, 2 * DFF])
    fwo_d = dt_in("ffn_w_out", [2, DFF, D])
    ev_d = dt_in("ev_w_in", [D, 1984])
    uq_d = dt_in("mla_w_uq", [256, 768])
    ukv_d = dt_in("mla_w_ukv", [128, 1024])
    od_d = dt_in("od_w_in", [D, 3072])
    lam_d = dt_in("lamv", [4, 64])
    pm1_d = dt_in("pm1", [128, 128])
    pm2_d = dt_in("pm2", [NPM2, 128])
    cmat_d = dt_in("cmat", [128, 4, 128])
    rope_d = dt_in("ropecs", [128, 2, TL])
    nab_d = dt_in("nabias", [16, 128, NAB_F])
    out_d = nc.dram_tensor("out", [nb, TL, D], F32, kind="ExternalOutput").ap()
    dbg_d = None
    if dbg is not None:
        dbg_d = nc.dram_tensor("dbg", [128, 8, T], F32, kind="ExternalOutput").ap()

    sc = lambda name, shape: nc.dram_tensor(name, shape, BF16, kind="Internal").ap()
    ev_s = sc("ev_s", [16, 128, 8, 128])
    od_s = sc("od_s", [24, 128, 8, 128])
    wout_s = sc("wout_s", [2, D, D])
    wi_s = sc("wi_s", [2, 22, 128, 8, 256])
    wo_s = sc("wo_s", [2, 8, 128, 22, 128])

    S = Sched(nc)
    sb = nc.alloc_sbuf_tensor
    xT = sb("xT", [128, 8, T], F32)
    hT = sb("hT", [128, 8, T], BF16)
    ARENA = 36352
    arena = sb("arena", [128, ARENA], BF16)
    NSLOT = 8
    slots = arena[:, 0:NSLOT * T].rearrange("p (s t) -> p s t", s=NSLOT)
    off = NSLOT * T
    ropecs = arena[:, off:off + 2 * TL * 2].bitcast(F32).rearrange("p (a t) -> p a t", a=2)
    nabt = arena[:, 4 * T:4 * T + 2 * 2 * NAB_F].rearrange("p (u h f) -> p u h f", u=2, h=2)
    off += 2 * TL * 2
    ptiles = arena[:, off:off + 4 * 512].rearrange("p (s n) -> p s n", s=4)
    off += 4 * 512
    ytiles = arena[:, off:off + 2 * 512].rearrange("p (s n) -> p s n", s=2)
    off += 2 * 512
    uq_sb = arena[:, off:off + 2 * 768].rearrange("p (k n) -> p k n", k=2)
    off += 2 * 768
    ukv_sb = arena[:, off:off + 1024]
    off += 1024
    wproj = arena[:, off:off + 4 * 1024].rearrange("p (s n) -> p s n", s=4)
    off += 4 * 1024
    assert off <= ARENA, off
    a_sb = arena[:, 0:22 * 512].rearrange("p (j n) -> p j n", j=22)
    foff = 22 * 512
    wA = arena[:, foff:foff + 4 * 2048].rearrange("p (s n) -> p s n", s=4)
    foff += 4 * 2048
    wB = arena[:, foff:foff + 2 * 22 * 128].rearrange("p (s n) -> p s n", s=2)
    foff += 2 * 22 * 128
    assert foff <= ARENA, foff
    modw_t = arena[:, 0:2 * 8192].bitcast(F32).rearrange("p (s k n) -> p s k n", s=2, k=8)

    sqt = sb("sqt", [128, 4, 512], BF16)
    ftm = sb("ftm", [128, 6, 512], F32)
    ddt = sb("ddt", [128, 2, 512], F32)
    rst = sb("rst", [128, 2, 512], F32)
    stg = arena[:, 0:4096].bitcast(F32).rearrange("p (s n) -> p s n", s=2)
    cmf = sb("cmf", [128, 128], F32)
    cmb = sb("cmb", [128, 4, 128], BF16)
    pv1 = sb("pv1", [128, 128], F32)
    pv2 = sb("pv2", [128, NPM2], F32)
    modv = sb("modv", [128, 2, 48, 3], F32)
    Av = sb("Av", [128, 2, 2, 3, 8], F32)
    siluT = sb("siluT", [128, 3, 8], F32)
    sm = sb("sm", [128, 16], F32)

    identf = cmf
    identb = cmb[:, 0, :]
    onesb = cmb[:, 1, :]
    blk64 = cmb[:, 2, :]
    permb = cmb[:, 3, :]
    epsb = sm[:, 0:1]
    neglam = sm[:, 1:2]
    og08 = sm[:, 2:3]
    naq8 = sm[:, 3:4]

    pst = [nc.alloc_psum_tensor("ps%d" % i, [128, 512], F32) for i in range(8)]
    psS = Ring(pst[0:4], "psS")
    psA = Ring(pst[4:8], "psA")
    sq_r = Ring([sqt[:, i, :] for i in range(4)], "sq")
    ft_r = Ring([ftm[:, i, :] for i in range(6)], "ft")
    dd_r = Ring([ddt[:, i, :] for i in range(2)], "dd")
    rs_r = Ring([rst[:, i, :] for i in range(2)], "rs")
    stg_r = Ring([stg[:, i, :] for i in range(2)], "stg")
    p_r = Ring([ptiles[:, i, :] for i in range(4)], "pt")
    y_r = Ring([ytiles[:, i, :] for i in range(2)], "yt")
    wproj_r = Ring([wproj[:, i, :] for i in range(4)], "wp")
    wA_r = Ring([wA[:, i, :] for i in range(4)], "wA")
    wB_r = Ring([wB[:, i, :] for i in range(2)], "wB")
    modw_r = Ring([modw_t[:, i] for i in range(2)], "mw")
    nab_r = Ring([nabt[:, i] for i in range(2)], "nab")

    xB = [[Buf("x%d_%d" % (c, tb)) for tb in range(5)] for c in range(8)]
    hB = [[Buf("h%d_%d" % (c, tb)) for tb in range(5)] for c in range(8)]
    slB = [[Buf("sl%d_%d" % (s, tb)) for tb in range(5)] for s in range(NSLOT)]
    aB = [Buf("a%d" % j) for j in range(22)]
    cB = Buf("consts")
    ropeB = Buf("rope")
    uqB = Buf("uq")
    ukvB = Buf("ukv")
    evB = [Buf("ev%d" % i) for i in range(16)]
    odB = [Buf("od%d" % i) for i in range(24)]
    woutB = [Buf("wout%d" % l) for l in range(2)]
    wiB = [[Buf("wi%d_%d" % (l, j)) for j in range(22)] for l in range(2)]
    woB = [[Buf("wo%d_%d" % (l, n)) for n in range(8)] for l in range(2)]
    modB = Buf("modv")
    outB = Buf("outd")
    arena_attn_bufs = [b for r in slB for b in r] + [ropeB, uqB, ukvB] + [it[1] for rr in (p_r, y_r, wproj_r, nab_r) for it in rr.items]
    arena_ffn_bufs = aB + [it[1] for rr in (wA_r, wB_r) for it in rr.items]
    arena_pro_bufs = [it[1] for it in modw_r.items]
    stg_bufs = [it[1] for it in stg_r.items]
    nab_bufs = [it[1] for it in nab_r.items]

    def mm(out, lhsT, rhs, start, stop, reads, writes):
        return S.op("tensor", lambda e: e.matmul(out, lhsT=lhsT, rhs=rhs, start=start, stop=stop), reads, writes)

    def act(out, in_, func, reads, writes, bias=None, scale=None):
        kw = {}
        if bias is not None:
            kw["bias"] = bias
        if scale is not None:
            kw["scale"] = scale
        return S.op("scalar", lambda e: e.activation(out=out, in_=in_, func=func, **kw), reads, writes)

    def dve(fn, reads, writes, eng="vector"):
        return S.op(eng, fn, reads, writes)

    def dma(eng, out, in_, reads, writes):
        return S.op(eng, lambda e: e.dma_start(out=out, in_=in_), reads, writes, dma=True)

    dma("sync", cmf[:], cmat_d[:, 0, :], [], [cB])
    dma("gpsimd", cmb[:], cmat_d, [], [cB])
    pmr_t = ft_r.next()
    pmr = pmr_t[:, 0:128]
    dma("sync", pmr, pm1_d, [], [pmr_t])
    lam_t = ft_r.next()
    lamt = lam_t[:, 0:256].rearrange("p (a n) -> p a n", a=4)
    dve(lambda e: e.memset(sm[:], 0.0), [], [cB])
    dve(lambda e: e.memset(epsb, EPS), [], [cB])
    for i in range(4):
        dma("sync", lamt[:, i, :], lam_d[i, :].partition_broadcast(128), [], [lam_t])
    t_ps = psA.next()
    mm(t_ps[:, 0:128], pmr, identf[:], True, True, [cB, pmr_t], [t_ps])
    dve(lambda e: e.tensor_copy(out=pv1[:], in_=t_ps[:, 0:128]), [t_ps], [cB])
    pm2r = ft_r.next()
    dma("sync", pm2r[0:NPM2, 0:128], pm2_d, [], [pm2r])
    t_ps2 = psA.next()
    mm(t_ps2[:, 0:NPM2], pm2r[0:NPM2, 0:128], identf[0:NPM2, 0:NPM2], True, True, [pm2r, cB], [t_ps2])
    dve(lambda e: e.tensor_copy(out=pv2[:], in_=t_ps2[:, 0:NPM2]), [t_ps2], [cB])
    act(siluT[:].rearrange("p s k -> p (s k)"), pv2[:, 0:24], AF.Silu, [cB], [cB])
    dve(lambda e: e.tensor_tensor(out=lamt[:, 0, :], in0=lamt[:, 0, :], in1=lamt[:, 1, :], op=ALU.mult), [cB, lam_t], [lam_t])
    dve(lambda e: e.tensor_tensor(out=lamt[:, 2, :], in0=lamt[:, 2, :], in1=lamt[:, 3, :], op=ALU.mult), [cB, lam_t], [lam_t])
    dve(lambda e: e.tensor_reduce(out=sm[:, 4:5], in_=lamt[:, 0, :], axis=AX.X, op=ALU.add), [cB, lam_t], [cB])
    dve(lambda e: e.tensor_reduce(out=sm[:, 5:6], in_=lamt[:, 2, :], axis=AX.X, op=ALU.add), [cB, lam_t], [cB])
    act(sm[:, 4:6], sm[:, 4:6], AF.Exp, [cB], [cB])
    dve(lambda e: e.tensor_tensor(out=sm[:, 6:7], in0=sm[:, 5:6], in1=sm[:, 4:5], op=ALU.subtract), [cB], [cB])
    dve(lambda e: e.tensor_scalar(out=neglam, in0=sm[:, 6:7], scalar1=-0.2, scalar2=None, op0=ALU.add), [cB], [cB])
    dve(lambda e: e.tensor_scalar(out=og08, in0=pv2[:, R_DAO:R_DAO + 1], scalar1=0.8, scalar2=None, op0=ALU.mult), [cB], [cB])
    dve(lambda e: e.tensor_scalar(out=naq8, in0=pv2[:, R_NAQ:R_NAQ + 1], scalar1=0.125, scalar2=None, op0=ALU.mult), [cB], [cB])

    for l in range(nlayers):
        mps = psA.next()
        mview = mps[:, 0:144].rearrange("p (n s) -> p n s", s=3)
        for g in range(12):
            wt = modw_r.next()
            dma("sync", wt[:], modw_d[l].rearrange("(k p) n -> p k n", p=128)[:, :, g * 512:(g + 1) * 512], [], [wt])
            for nn in range(4):
                n = g * 4 + nn
                for kc in range(8):
                    mm(mview[:, n, :], wt[:, kc, nn * 128:(nn + 1) * 128], siluT[:, :, kc], kc == 0, kc == 7, [wt, cB], [mps])
        for s in range(3):
            dve(lambda e, s=s, l=l, mview=mview: e.tensor_tensor(out=modv[:, l, :, s], in0=mview[:, :, s], in1=pv1[:, l * 48:(l + 1) * 48], op=ALU.add), [mps, cB], [modB])
        for w_, part, grow in ((0, 1, 96), (1, 4, 112)):
            for s in range(3):
                dve(lambda e, l=l, w_=w_, part=part, grow=grow, s=s: e.scalar_tensor_tensor(
                    out=Av[:, l, w_, s, :], in0=modv[:, l, part * 8:(part + 1) * 8, s], scalar=1.0,
                    in1=pv1[:, grow + l * 8:grow + (l + 1) * 8], op0=ALU.add, op1=ALU.mult), [modB, cB], [modB])

    def Acol(l, w_, s, c):
        return Av[:, l, w_, s, c:c + 1]

    def Mcol(l, part, s, c):
        return modv[:, l, part * 8 + c, s:s + 1]

    S.alias(arena_attn_bufs, arena_pro_bufs)
    ev_src = ev_d.rearrange("(k p) n -> p k n", p=128)
    for nbk in range(16):
        w = 128 if nbk < 15 else 64
        dma("gpsimd", ev_s[nbk, :, :, 0:w], ev_src[:, :, nbk * 128:nbk * 128 + w], [], [evB[nbk]])
    dma("gpsimd", uq_sb, uq_d.rearrange("(k p) n -> p k n", p=128), [], [uqB])
    dma("gpsimd", ukv_sb, ukv_d, [], [ukvB])

    def precast_layer(l):
        dma("gpsimd", wout_s[l], wout_d[l], [], [woutB[l]])
        src = fwi_d[l].rearrange("(k p) n -> p k n", p=128)
        for j in range(22):
            dma("gpsimd", wi_s[l, j, :, :, 0:128], src[:, :, j * 128:(j + 1) * 128], [], [wiB[l][j]])
            dma("gpsimd", wi_s[l, j, :, :, 128:256], src[:, :, DFF + j * 128:DFF + (j + 1) * 128], [], [wiB[l][j]])
        srco = fwo_d[l].rearrange("(j p) n -> p j n", p=128)
        for n in range(8):
            dma("gpsimd", wo_s[l, n], srco[:, :, n * 128:(n + 1) * 128], [], [woB[l][n]])

    precast_layer(0)
    if nlayers > 1:
        od_src = od_d.rearrange("(k p) n -> p k n", p=128)
        for nbk in range(24):
            dma("gpsimd", od_s[nbk], od_src[:, :, nbk * 128:(nbk + 1) * 128], [], [odB[nbk]])
        precast_layer(1)

    def load_x(b):
        S.alias(stg_bufs, arena_attn_bufs)
        for t in range(18):
            st = stg_r.next()
            src = x_d[b, t * 128:(t + 1) * 128, :] if t < 16 else ctx_d[b, (t - 16) * 128:(t - 15) * 128, :]
            dma("sync", st[:], src, [], [st])
            tb = t // 4
            for half in range(2):
                ps = psA.next()
                for cc in range(4):
                    c = half * 4 + cc
                    mm(ps[:, cc * 128:(cc + 1) * 128], st[:, c * 128:(c + 1) * 128], identf[:], True, True, [st, cB], [ps])
                o = xT[:, half * 4:(half + 1) * 4, t * 128:(t + 1) * 128]
                i_ = ps[:, :].rearrange("p (c n) -> p c n", c=4)
                wr = [xB[half * 4 + cc][tb] for cc in range(4)]
                if half == 0:
                    dve(lambda e, o=o, i_=i_: e.tensor_copy(out=o, in_=i_), [ps], wr)
                else:
                    act(o, i_, AF.Copy, [ps], wr)
        S.alias(arena_attn_bufs, stg_bufs)

    def store_out(b):
        ops = []
        S.alias(stg_bufs, arena_attn_bufs)
        for t in range(16):
            st = stg_r.next()
            tb = t // 4
            for half in range(2):
                ps = psA.next()
                for cc in range(4):
                    c = half * 4 + cc
                    mm(ps[:, cc * 128:(cc + 1) * 128], xT[:, c, t * 128:(t + 1) * 128], identf[:], True, True, [xB[c][tb], cB], [ps])
                o = st[:, half * 512:(half + 1) * 512]
                if half == 0:
                    dve(lambda e, o=o, ps=ps: e.tensor_copy(out=o, in_=ps[:, :]), [ps], [st])
                else:
                    act(o, ps[:, :], AF.Copy, [ps], [st])
            ops.append(dma("sync", out_d[b, t * 128:(t + 1) * 128, :], st[:], [st], [outB]))
        S.alias(arena_attn_bufs, stg_bufs)
        return ops

    def norm_mod(l, w_, b, tbs):
        for tb in tbs:
            t0, n = TBLK[tb]
            s = 2 if tb == 4 else b
            ssP = psS.next()
            for c in range(8):
                sq = sq_r.next()
                act(sq[:, 0:n], xT[:, c, t0:t0 + n], AF.Square, [xB[c][tb]], [sq])
                mm(ssP[:, 0:n], onesb, sq[:, 0:n], c == 0, c == 7, [sq, cB], [ssP])
            sd = ft_r.next()
            act(sd[:, 0:n], ssP[:, 0:n], AF.Ln, [ssP, cB], [sd], bias=epsb, scale=1.0 / D)
            rs = rs_r.next()
            act(rs[:, 0:n], sd[:, 0:n], AF.Exp, [sd], [rs], scale=-0.5)
            for c in range(8):
                tmp = ft_r.next()
                dve(lambda e, tmp=tmp, c=c, rs=rs, t0=t0, n=n, s=s: e.scalar_tensor_tensor(
                    out=tmp[:, 0:n], in0=xT[:, c, t0:t0 + n], scalar=Acol(l, w_, s, c), in1=rs[:, 0:n],
                    op0=ALU.mult, op1=ALU.mult), [xB[c][tb], rs, modB], [tmp])
                act(hT[:, c, t0:t0 + n], tmp[:, 0:n], AF.Identity, [tmp, modB], [hB[c][tb]],
                    bias=Mcol(l, 0 if w_ == 0 else 3, s, c))

    def group_norm(tiles, nfeat, tb, rope):
        t0, n = TBLK[tb]
        ssP = psS.next()
        for i, (ps, M, G, gain, dfn, dbufs) in enumerate(tiles):
            sq = sq_r.next()
            act(sq[0:M, 0:n], ps[0:M, 0:n], AF.Square, [ps], [sq])
            mm(ssP[:, 0:n], G, sq[0:M, 0:n], i == 0, i == len(tiles) - 1, [sq, cB], [ssP])
        yield
        sd = ft_r.next()
        act(sd[:, 0:n], ssP[:, 0:n], AF.Ln, [ssP, cB], [sd], bias=epsb, scale=1.0 / nfeat)
        yield
        rs = rs_r.next()
        act(rs[:, 0:n], sd[:, 0:n], AF.Exp, [sd], [rs], scale=-0.5)
        yield
        todo = []
        for (ps, M, G, gain, dfn, dbufs), rp in zip(tiles, rope):
            if not rp:
                dve(lambda e, ps=ps, M=M, gain=gain, dfn=dfn: e.scalar_tensor_tensor(
                    out=dfn(t0, n), in0=ps[0:M, 0:n], scalar=gain, in1=rs[0:M, 0:n], op0=ALU.mult, op1=ALU.mult),
                    [ps, rs, cB], dbufs)
                continue
            xn = ft_r.next()
            dve(lambda e, ps=ps, M=M, gain=gain, xn=xn: e.scalar_tensor_tensor(
                out=xn[0:M, 0:n], in0=ps[0:M, 0:n], scalar=gain, in1=rs[0:M, 0:n], op0=ALU.mult, op1=ALU.mult),
                [ps, rs, cB], [xn])
            todo.append((M, dfn, dbufs, xn))
        if not todo:
            return
        yield
        st2 = []
        for (M, dfn, dbufs, xn) in todo:
            xb = sq_r.next()
            act(xb[0:M, 0:n], xn[0:M, 0:n], AF.Copy, [xn], [xb])
            pp = psS.next()
            mm(pp[0:M, 0:n], permb[0:M, 0:M], xb[0:M, 0:n], True, True, [xb, cB], [pp])
            st2.append((M, dfn, dbufs, xn, pp))
        yield
        st3 = []
        for (M, dfn, dbufs, xn, pp) in st2:
            dve(lambda e, xn=xn, M=M: e.tensor_tensor(out=xn[0:M, 0:n], in0=xn[0:M, 0:n], in1=ropecs[0:M, 0, t0:t0 + n], op=ALU.mult),
                [xn, ropeB], [xn])
            t2 = ft_r.next()
            dve(lambda e, pp=pp, t2=t2, M=M: e.tensor_tensor(out=t2[0:M, 0:n], in0=pp[0:M, 0:n], in1=ropecs[0:M, 1, t0:t0 + n], op=ALU.mult),
                [pp, ropeB], [t2])
            st3.append((M, dfn, dbufs, xn, t2))
        yield
        for (M, dfn, dbufs, xn, t2) in st3:
            dve(lambda e, xn=xn, t2=t2, M=M, dfn=dfn: e.tensor_tensor(out=dfn(t0, n), in0=xn[0:M, 0:n], in1=t2[0:M, 0:n], op=ALU.add),
                [xn, t2], dbufs)

    def run_chains(*gens):
        gens = list(gens)
        while gens:
            for g in list(gens):
                try:
                    next(g)
                except StopIteration:
                    gens.remove(g)

    def load_wtile(scr, scrB, ring=None):
        wt = (ring or wproj_r).next()
        dma("sync", wt[:], scr.rearrange("p k n -> p (k n)"), [scrB], [wt])
        return wt

    def proj_fm(wt, M, tb, col0=0, wstride=128):
        t0, n = TBLK[tb]
        ps = psA.next()
        for kc in range(8):
            mm(ps[0:M, 0:n], wt[:, kc * wstride + col0:kc * wstride + col0 + M], hT[:, kc, t0:t0 + n], kc == 0, kc == 7,
               [wt, hB[kc][tb]], [ps])
        return ps

    def proj_v_tm(wt, slot, ntiles=18):
        for t in range(ntiles):
            tb = t // 4
            ps = psS.next()
            for kc in range(8):
                mm(ps[:, 0:128], hT[:, kc, t * 128:(t + 1) * 128], wt[:, kc * 128:(kc + 1) * 128], kc == 0, kc == 7,
                   [wt, hB[kc][tb]], [ps])
            o = slots[:, slot, t * 128:(t + 1) * 128]
            if t % 2 == 0:
                dve(lambda e, o=o, ps=ps: e.tensor_copy(out=o, in_=ps[:, 0:128]), [ps], [slB[slot][tb]])
            else:
                act(o, ps[:, 0:128], AF.Copy, [ps], [slB[slot][tb]])

    def attn_core(units, scale, group=2, need_sm=True):
        O = psA.next()
        Sm = psA.next() if need_sm else None
        groups = [units[i:i + group] for i in range(0, len(units), group)]
        sts = {}

        def emit_qk(gi):
            for ui, u in enumerate(groups[gi]):
                st = psS.next()
                u[1](st)
                sts[(gi, ui)] = st
        emit_qk(0)
        if len(groups) > 1:
            emit_qk(1)
        n_units = len(units)
        done = 0
        for gi, g in enumerate(groups):
            ps = []
            for ui, u in enumerate(g):
                st = sts.pop((gi, ui))
                p = p_r.next()
                act(p[:, 0:u[0]], st[:, 0:u[0]], AF.Exp, [st], [p], scale=scale)
                if len(u) > 3 and u[3] is not None:
                    u[3](p)
                ps.append(p)
            for ui in reversed(range(len(g))):
                first = done == 0
                done += 1
                g[ui][2](ps[ui], O, Sm, first, done == n_units)
            if gi + 2 < len(groups):
                emit_qk(gi + 2)
        return O, Sm

    def tbs_of(t0, n):
        return sorted(set([t0 // 512, (t0 + n - 1) // 512]))

    def wout_accum(l, wt, yb, tb, b, q0=None, nq=None):
        t0, n = TBLK[tb]
        if q0 is not None:
            t0, n = q0, nq
        s = 2 if tb == 4 else b
        tbl_ = tbs_of(t0, n)
        for nchk in range(8):
            ps = psS.next()
            mm(ps[:, 0:n], wt[:, nchk * 128:(nchk + 1) * 128], yb[:, 0:n], True, True, [wt, yb], [ps])
            xbs = [xB[nchk][t_] for t_ in tbl_]
            dve(lambda e, ps=ps, nchk=nchk, t0=t0, n=n, s=s: e.scalar_tensor_tensor(
                out=xT[:, nchk, t0:t0 + n], in0=ps[:, 0:n], scalar=Mcol(l, 2, s, nchk), in1=xT[:, nchk, t0:t0 + n],
                op0=ALU.mult, op1=ALU.add), [ps, modB] + xbs, xbs)

    def ffn(l, b, tbs):
        S.alias(arena_ffn_bufs, arena_attn_bufs)
        for tb in tbs:
            t0, n = TBLK[tb]
            s = 2 if tb == 4 else b
            for j in range(22):
                wt = load_wtile(wi_s[l, j], wiB[l][j], wA_r)
                G = psS.next()
                U = psS.next()
                for kc in range(8):
                    mm(G[:, 0:n], wt[:, kc * 256:kc * 256 + 128], hT[:, kc, t0:t0 + n], kc == 0, kc == 7, [wt, hB[kc][tb]], [G])
                for kc in range(8):
                    mm(U[:, 0:n], wt[:, kc * 256 + 128:kc * 256 + 256], hT[:, kc, t0:t0 + n], kc == 0, kc == 7, [wt, hB[kc][tb]], [U])
                sg = ft_r.next()
                act(sg[:, 0:n], G[:, 0:n], AF.Silu, [G], [sg])
                dve(lambda e, sg=sg, U=U, j=j, n=n: e.tensor_tensor(out=a_sb[:, j, 0:n], in0=sg[:, 0:n], in1=U[:, 0:n], op=ALU.mult),
                    [sg, U], [aB[j]])
            for nchk in range(8):
                wt = load_wtile(wo_s[l, nchk], woB[l][nchk], wB_r)
                ps = psA.next()
                for j in range(22):
                    mm(ps[:, 0:n], wt[:, j * 128:(j + 1) * 128], a_sb[:, j, 0:n], j == 0, j == 21, [wt, aB[j]], [ps])
                dve(lambda e, ps=ps, nchk=nchk, t0=t0, n=n, s=s: e.scalar_tensor_tensor(
                    out=xT[:, nchk, t0:t0 + n], in0=ps[:, 0:n], scalar=Mcol(l, 5, s, nchk), in1=xT[:, nchk, t0:t0 + n],
                    op0=ALU.mult, op1=ALU.add), [ps, modB, xB[nchk][tb]], [xB[nchk][tb]])
        S.alias(arena_attn_bufs, arena_ffn_bufs)

    ALLCH = list(range(18))
    CTXCH = [16, 17]

    def da_qblock(tb, wo_t, b):
        l = 0
        SQ, SK, SV = 0, 1, 2
        t0, nq = TBLK[tb]
        chunks = ALLCH if tb < 4 else CTXCH
        ons = []
        for i in range(2):
            def mk_unit(kc, i=i):
                def qk_fn(st):
                    mm(st[:, 0:nq], slots[64 * i:64 * i + 64, SK, kc * 128:(kc + 1) * 128], slots[64 * i:64 * i + 64, SQ, t0:t0 + nq],
                       True, True, [slB[SK][kc // 4], slB[SQ][tb]], [st])

                def av_fn(p, O, Sm, first, last):
                    mm(O[:, 0:nq], slots[:, SV, kc * 128:(kc + 1) * 128], p[:, 0:nq], first, last, [p, slB[SV][kc // 4]], [O])
                    mm(Sm[:, 0:nq], onesb, p[:, 0:nq], first, last, [p, cB], [Sm])
                return (nq, qk_fn, av_fn)
            O, Sm = attn_core([mk_unit(kc) for kc in chunks], 0.125)
            rs = rs_r.next()
            dve(lambda e, rs=rs, Sm=Sm: e.reciprocal(out=rs[:, 0:nq], in_=Sm[:, 0:nq]), [Sm], [rs])
            on = ft_r.next()
            dve(lambda e, on=on, O=O, rs=rs: e.tensor_tensor(out=on[:, 0:nq], in0=O[:, 0:nq], in1=rs[:, 0:nq], op=ALU.mult), [O, rs], [on])
            ons.append(on)
        dd = dd_r.next()
        dve(lambda e: e.scalar_tensor_tensor(out=dd[:, 0:nq], in0=ons[1][:, 0:nq], scalar=neglam, in1=ons[0][:, 0:nq],
                                             op0=ALU.mult, op1=ALU.add), [ons[0], ons[1], cB], [dd])

        def tail():
            sq = sq_r.next()
            act(sq[:, 0:nq], dd[:, 0:nq], AF.Square, [dd], [sq])
            ssP = psS.next()
            mm(ssP[:, 0:nq], onesb, sq[:, 0:nq], True, True, [sq, cB], [ssP])
            sd = ft_r.next()
            act(sd[:, 0:nq], ssP[:, 0:nq], AF.Ln, [ssP, cB], [sd], bias=epsb, scale=1.0 / 128)
            rs2 = rs_r.next()
            act(rs2[:, 0:nq], sd[:, 0:nq], AF.Exp, [sd], [rs2], scale=-0.5)
            yb = y_r.next()
            dve(lambda e: e.scalar_tensor_tensor(out=yb[:, 0:nq], in0=dd[:, 0:nq], scalar=og08, in1=rs2[:, 0:nq],
                                                 op0=ALU.mult, op1=ALU.mult), [dd, rs2, cB], [yb])
            wout_accum(l, wo_t, yb, tb, b)
        return tail

    def mla_qblock(tb, wo_t, b):
        l = 0
        SCQ0, SCQ1, SCKV, SKR, SQN, SQR, SKN, SMV = range(8)
        t0, nq = TBLK[tb]
        chunks = ALLCH if tb < 4 else CTXCH

        def mk_unit(kc):
            def qk_fn(st):
                mm(st[:, 0:nq], slots[:, SKN, kc * 128:(kc + 1) * 128], slots[:, SQN, t0:t0 + nq], True, False,
                   [slB[SKN][kc // 4], slB[SQN][tb]], [st])
                mm(st[:, 0:nq], slots[0:64, SKR, kc * 128:(kc + 1) * 128], slots[0:64, SQR, t0:t0 + nq], False, True,
                   [slB[SKR][kc // 4], slB[SQR][tb]], [st])

            def av_fn(p, O, Sm, first, last):
                mm(O[:, 0:nq], slots[:, SMV, kc * 128:(kc + 1) * 128], p[:, 0:nq], first, last, [p, slB[SMV][kc // 4]], [O])
                mm(Sm[:, 0:nq], onesb, p[:, 0:nq], first, last, [p, cB], [Sm])
            return (nq, qk_fn, av_fn)
        O, Sm = attn_core([mk_unit(kc) for kc in chunks], 192.0 ** -0.5)
        rs = rs_r.next()
        dve(lambda e: e.reciprocal(out=rs[:, 0:nq], in_=Sm[:, 0:nq]), [Sm], [rs])
        yb = y_r.next()
        dve(lambda e: e.tensor_tensor(out=yb[:, 0:nq], in0=O[:, 0:nq], in1=rs[:, 0:nq], op=ALU.mult), [O, rs], [yb])
        return lambda: wout_accum(l, wo_t, yb, tb, b)

    def mla_head_proj(h, tb):
        SCQ0, SCQ1, SCKV, SKR, SQN, SQR, SKN, SMV = range(8)
        t0, n = TBLK[tb]
        pn = psA.next()
        for kc in range(2):
            mm(pn[:, 0:n], uq_sb[:, kc, 192 * h:192 * h + 128], slots[:, SCQ0 + kc, t0:t0 + n], kc == 0, kc == 1, [uqB, slB[SCQ0 + kc][tb]], [pn])
        pr = psA.next()
        for kc in range(2):
            mm(pr[0:64, 0:n], uq_sb[:, kc, 192 * h + 128:192 * h + 192], slots[:, SCQ0 + kc, t0:t0 + n], kc == 0, kc == 1, [uqB, slB[SCQ0 + kc][tb]], [pr])
        pk = psA.next()
        mm(pk[:, 0:n], ukv_sb[:, 256 * h:256 * h + 128], slots[:, SCKV, t0:t0 + n], True, True, [ukvB, slB[SCKV][tb]], [pk])
        run_chains(
            group_norm([(pn, 128, onesb, pv2[:, R_MQN:R_MQN + 1], lambda t0, n: slots[:, SQN, t0:t0 + n], [slB[SQN][tb]]),
                        (pr, 64, onesb[0:64, :], pv2[0:64, R_MQR:R_MQR + 1], lambda t0, n: slots[0:64, SQR, t0:t0 + n], [slB[SQR][tb]])],
                       192, tb, [False, tb < 4]),
            group_norm([(pk, 128, onesb, pv2[:, R_MK:R_MK + 1], lambda t0, n: slots[:, SKN, t0:t0 + n], [slB[SKN][tb]])], 128, tb, [False]))

    import os as _os
    NA_OLDV = _os.environ.get("NA_OLDV", "1") == "1"
    NA_KEEPSUM = NA_OLDV or _os.environ.get("NA_KEEPSUM", "0") == "1"
    NA_BLOCKS = [(0, 4), (4, 8), (12, 8), (20, 8), (28, 4)]
    if _os.environ.get("NA_R4", "0") == "1":
        NA_BLOCKS = [(4 * i, 4) for i in range(8)]

    def na_qblock(m, blk, wo_t, nab, b):
        l = 1
        SQ, SK, SV0 = 0, 1, 2
        q0, R = NA_BLOCKS[blk]
        t0, nq = q0 * 64, R * 64
        qtb = tbs_of(t0, nq)
        qbufs = [slB[SQ][t_] for t_ in qtb]
        if q0 == 0 or q0 == 28:
            lrows = [0, 2, 4, 6] if q0 == 0 else [24, 26, 28, 30]
            nbase = lambda r: (7 - (r - q0) - 1) * 64
        else:
            lrows = [q0 - 4 + 2 * i for i in range(R // 2 + 4)]
            nbase = lambda r: (14 + 10 - (r - q0)) * 64
        chunks = [("l", r) for r in lrows] + [("c", 32), ("c", 34)]
        per_unit = 512 // nq
        yb = y_r.next()
        for e_ in range(2):
            pl, ph = 64 * e_, 64 * e_ + 64
            ol, oh = 64 - pl, 128 - pl

            def mk_unit(grp, pl=pl, ph=ph, e_=e_):
                def qk_fn(st):
                    for hf, (kind, r) in enumerate(grp):
                        kc = r // 2
                        mm(st[:, hf * nq:(hf + 1) * nq], slots[pl:ph, SK, kc * 128:(kc + 1) * 128], slots[pl:ph, SQ, t0:t0 + nq], True, True,
                           [slB[SK][kc // 4]] + qbufs, [st])

                def post_fn(p):
                    for hf, (kind, r) in enumerate(grp):
                        if kind == "l":
                            nb_ = nbase(r)
                            dve(lambda e, p=p, hf=hf, nb_=nb_: e.tensor_tensor(out=p[:, hf * nq:(hf + 1) * nq], in0=p[:, hf * nq:(hf + 1) * nq],
                                                                              in1=nab[:, e_, nb_:nb_ + nq], op=ALU.mult), [p, nab], [p])

                def av_fn(p, O, Sm, first, last):
                    for hf, (kind, r) in enumerate(grp):
                        kc = r // 2
                        vs = SV0 if NA_OLDV else SV0 + e_
                        mm(O[:, 0:nq], slots[:, vs, kc * 128:(kc + 1) * 128], p[:, hf * nq:(hf + 1) * nq],
                           first and hf == 0, last and hf == len(grp) - 1, [p, slB[vs][kc // 4]], [O])
                        if NA_KEEPSUM:
                            mm(Sm[:, 0:nq], onesb, p[:, hf * nq:(hf + 1) * nq], first and hf == 0, last and hf == len(grp) - 1, [p, cB], [Sm])
                return (nq * len(grp), qk_fn, av_fn, post_fn)
            O, Sm = attn_core([mk_unit(chunks[i:i + per_unit]) for i in range(0, len(chunks), per_unit)], 1.0, need_sm=NA_KEEPSUM)
            rs = rs_r.next()
            lnv = ft_r.next()
            if NA_KEEPSUM:
                act(lnv[pl:ph, 0:nq], Sm[pl:ph, 0:nq], AF.Ln, [Sm], [lnv])
            else:
                act(lnv[pl:ph, 0:nq], O[ol:oh, 0:nq], AF.Ln, [O], [lnv])
            act(rs[pl:ph, 0:nq], lnv[pl:ph, 0:nq], AF.Exp, [lnv], [rs], scale=-1.0)
            dve(lambda e, O=O, rs=rs, pl=pl, ph=ph: e.tensor_tensor(out=yb[pl:ph, 0:nq], in0=O[pl:ph, 0:nq], in1=rs[pl:ph, 0:nq], op=ALU.mult),
                [O, rs], [yb])
        return lambda: wout_accum(l, wo_t, yb, qtb[0], b, q0=t0, nq=nq)

    def proj_v_na(wt):
        for t in range(18):
            tb = t // 4
            ps = psS.next()
            for kc in range(8):
                mm(ps[:, 0:128], hT[:, kc, t * 128:(t + 1) * 128], wt[:, kc * 128:(kc + 1) * 128], kc == 0, kc == 7,
                   [wt, hB[kc][tb]], [ps])
            a0 = 2 * T + t * 128
            o = arena[:, a0:a0 + 2 * (T + 64)].rearrange("p (s c) -> p s c", s=2)[:, :, 0:64]
            i_ = ps[:, 0:128].rearrange("p (s c) -> p s c", s=2)
            if t % 2 == 0:
                dve(lambda e, o=o, i_=i_: e.tensor_copy(out=o, in_=i_), [ps], [slB[2][tb], slB[3][tb]])
            else:
                act(o, i_, AF.Copy, [ps], [slB[2][tb], slB[3][tb]])

    def qblocks(need_ctx):
        return [0, 1, 2, 3, 4] if need_ctx else [0, 1, 2, 3]

    class Pend:
        def __init__(self):
            self.t = None

        def push(self, tail):
            old = self.t
            self.t = tail
            if old is not None:
                old()

        def flush(self):
            if self.t is not None:
                self.t()
            self.t = None

    def layer0(b):
        l = 0
        S.alias([ropeB] + [bb for sl in slB[4:8] for bb in sl], nab_bufs)
        dma("sync", ropecs, rope_d, [], [ropeB])
        norm_mod(l, 0, b, range(5))
        gq = pv2[:, R_DAQ:R_DAQ + 1]
        gk = pv2[:, R_DAK:R_DAK + 1]
        SQ, SK, SV = 0, 1, 2
        pend = Pend()
        for h in range(4):
            wq = load_wtile(ev_s[h], evB[h])
            wk = load_wtile(ev_s[4 + h], evB[4 + h])
            wv = load_wtile(ev_s[8 + h], evB[8 + h])
            for tb in range(5):
                rp = tb < 4
                psq = proj_fm(wq, 128, tb)
                psk = proj_fm(wk, 128, tb)
                run_chains(group_norm([(psq, 128, blk64, gq, lambda t0, n: slots[:, SQ, t0:t0 + n], [slB[SQ][tb]])], 64, tb, [rp]),
                           group_norm([(psk, 128, blk64, gk, lambda t0, n: slots[:, SK, t0:t0 + n], [slB[SK][tb]])], 64, tb, [rp]))
            proj_v_tm(wv, SV)
            pend.flush()
            wo_t = wproj_r.next()
            dma("sync", wo_t[:], wout_s[l, h * 128:(h + 1) * 128, :], [woutB[l]], [wo_t])
            for tb in range(5):
                pend.push(da_qblock(tb, wo_t, b))
        pend.flush()
        SCQ0, SCQ1, SCKV, SKR, SQN, SQR, SKN, SMV = range(8)
        wcq0 = load_wtile(ev_s[12], evB[12])
        wcq1 = load_wtile(ev_s[13], evB[13])
        wckv = load_wtile(ev_s[14], evB[14])
        wkr = load_wtile(ev_s[15], evB[15])
        for tb in range(5):
            p0 = proj_fm(wcq0, 128, tb)
            p1 = proj_fm(wcq1, 128, tb)
            p2 = proj_fm(wckv, 128, tb)
            p3 = proj_fm(wkr, 64, tb)
            run_chains(
                group_norm([(p0, 128, onesb, pv2[:, R_QA0:R_QA0 + 1], lambda t0, n: slots[:, SCQ0, t0:t0 + n], [slB[SCQ0][tb]]),
                            (p1, 128, onesb, pv2[:, R_QA1:R_QA1 + 1], lambda t0, n: slots[:, SCQ1, t0:t0 + n], [slB[SCQ1][tb]])], 256, tb, [False, False]),
                group_norm([(p2, 128, onesb, pv2[:, R_KVA:R_KVA + 1], lambda t0, n: slots[:, SCKV, t0:t0 + n], [slB[SCKV][tb]])], 128, tb, [False]))
            run_chains(
                group_norm([(p3, 64, onesb[0:64, :], pv2[0:64, R_MKR:R_MKR + 1], lambda t0, n: slots[0:64, SKR, t0:t0 + n], [slB[SKR][tb]])], 64, tb, [tb < 4]))
        for h in range(4):
            for tb in range(5):
                mla_head_proj(h, tb)
            for t in range(18):
                tb = t // 4
                ps = psS.next()
                mm(ps[:, 0:128], slots[:, SCKV, t * 128:(t + 1) * 128], ukv_sb[:, 256 * h + 128:256 * h + 256], True, True, [ukvB, slB[SCKV][tb]], [ps])
                o = slots[:, SMV, t * 128:(t + 1) * 128]
                if t % 2 == 0:
                    dve(lambda e, o=o, ps=ps: e.tensor_copy(out=o, in_=ps[:, 0:128]), [ps], [slB[SMV][tb]])
                else:
                    act(o, ps[:, 0:128], AF.Copy, [ps], [slB[SMV][tb]])
            pend.flush()
            wo_t = wproj_r.next()
            dma("sync", wo_t[:], wout_s[l, (4 + h) * 128:(5 + h) * 128, :], [woutB[l]], [wo_t])
            for tb in range(5):
                pend.push(mla_qblock(tb, wo_t, b))
        pend.flush()
        norm_mod(l, 1, b, range(5))
        ffn(l, b, range(5))

    def layer1(b):
        l = 1
        import os as _os
        S.alias(nab_bufs, [ropeB] + [bb for sl in slB[4:8] for bb in sl])
        norm_mod(l, 0, b, range(5))
        SQ, SK, SV = 0, 1, 2
        if (not NA_OLDV) or _os.environ.get('NA_T1', '0') == '1':
            for t in range(18):
                o0 = slots[:, 2, t * 128 + 64:(t + 1) * 128]
                o1 = slots[:, 3, t * 128:t * 128 + 64]
                dve(lambda e, o0=o0: e.tensor_copy(out=o0, in_=onesb[:, 0:64]), [cB], [slB[2][t // 4]])
                act(o1, onesb[:, 0:64], AF.Copy, [cB], [slB[3][t // 4]])
        gq = naq8
        gk = pv2[:, R_NAK:R_NAK + 1]
        pend = Pend()
        for m in range(8):
            wq = load_wtile(od_s[m], odB[m])
            wk = load_wtile(od_s[8 + m], odB[8 + m])
            wv = load_wtile(od_s[16 + m], odB[16 + m])
            nab = nab_r.next()
            dma("gpsimd", nab[:], nab_d[2 * m:2 * m + 2].rearrange("h p f -> p h f"), [], [nab])
            for e_ in range(2):
                act(nab[:, e_, :], nab[:, e_, :], AF.Exp, [nab], [nab])
            for tb in range(5):
                psk = proj_fm(wk, 128, tb)
                ch = [group_norm([(psk, 128, blk64, gk, lambda t0, n: slots[:, SK, t0:t0 + n], [slB[SK][tb]])], 64, tb, [False])]
                if tb < 4:
                    psq = proj_fm(wq, 128, tb)
                    ch.append(group_norm([(psq, 128, blk64, gq, lambda t0, n: slots[:, SQ, t0:t0 + n], [slB[SQ][tb]])], 64, tb, [False]))
                run_chains(*ch)
            if NA_OLDV and _os.environ.get('NA_T2', '0') != '1':
                proj_v_tm(wv, 2)
            else:
                proj_v_na(wv)
            pend.flush()
            wo_t = wproj_r.next()
            dma("sync", wo_t[:], wout_s[l, m * 128:(m + 1) * 128, :], [woutB[l]], [wo_t])
            import os as _os
            for blk in [int(v) for v in _os.environ.get("NA_DEBUG_BLOCKS", ",".join(str(i) for i in range(len(NA_BLOCKS)))).split(",")]:
                pend.push(na_qblock(m, blk, wo_t, nab, b))
        pend.flush()
        norm_mod(l, 1, b, range(4))
        ffn(l, b, range(4))

    final_ops = []
    for b in range(nb):
        load_x(b)
        if dbg == "x0":
            break
        layer0(b)
        if dbg == "l0" or nlayers == 1:
            break
        layer1(b)
        final_ops += store_out(b)
    if dbg is not None:
        for c in range(8):
            final_ops.append(dma("sync", dbg_d[:, c, :], xT[:, c, :], [xB[c][tb] for tb in range(5)], [outB]))
    S.emit(final_ops)
    return nc


_NC_CACHE = {}


def kernel(**inputs):
    inp = {k: np.ascontiguousarray(np.asarray(v)) for k, v in inputs.items()}
    if "nc" not in _NC_CACHE:
        _NC_CACHE["nc"] = build_nc()
    nc = _NC_CACHE["nc"]
    cmat = _const_mats()
    rope = _rope_tables()
    nab = _na_bias_tables(inp["na_rpb"][0])
    lamv = np.stack([inp["da_lq1"][0], inp["da_lk1"][0], inp["da_lq2"][0], inp["da_lk2"][0]]).astype(np.float32)
    in_maps = []
    for core in range(NCORES):
        b0 = core * NB
        pm1, pm2 = _pack_params(inp, b0)
        in_maps.append({
            "x": inp["x"][b0:b0 + NB], "ctx": inp["ctx"][b0:b0 + NB],
            "mod_w": inp["mod_w"], "w_out": inp["w_out"], "ffn_w_in": inp["ffn_w_in"], "ffn_w_out": inp["ffn_w_out"],
            "ev_w_in": inp["ev_w_in"][0], "mla_w_uq": inp["mla_w_uq"][0], "mla_w_ukv": inp["mla_w_ukv"][0],
            "od_w_in": inp["od_w_in"][0], "lamv": lamv, "pm1": pm1, "pm2": pm2, "cmat": cmat, "ropecs": rope, "nabias": nab,
        })
    res = run_bass_kernel_spmd(nc, in_maps, core_ids=list(range(NCORES)))
    out = np.concatenate([np.asarray(r["out"]) for r in res.results], axis=0)
    return out.astype(np.float32)
```

```python
import contextlib
import numpy as np
import concourse.bass as bass
import concourse.mybir as mybir
from concourse.bass_utils import run_bass_kernel_spmd

F32 = mybir.dt.float32
BF16 = mybir.dt.bfloat16
ALU = mybir.AluOpType
AF = mybir.ActivationFunctionType
AX = mybir.AxisListType

NCORES = 8
NB = 2
D = 1024
T = 2304
TL = 2048
TBLK = [(0, 512), (512, 512), (1024, 512), (1536, 512), (2048, 256)]
DFF = 2816
EPS = 1e-6
NEG = -1e30

ENGINES = ("tensor", "vector", "scalar", "gpsimd", "sync")
DMA_POOL = {"sync": (0, 14), "gpsimd": (14, 8), "scalar": (22, 2)}
N_DMA_SEMS = 24
SAME_ENGINE_SYNC = True


class Buf:
    __slots__ = ("name", "writer", "readers", "gen")

    def __init__(self, name):
        self.name = name
        self.writer = None
        self.readers = []
        self.gen = 0


class Tile:
    __slots__ = ("ap", "buf", "gen")

    def __init__(self, ap, buf):
        self.ap = ap
        self.buf = buf
        self.gen = buf.gen

    def __getitem__(self, k):
        return self.ap[k]


class Ring:
    def __init__(self, aps, name):
        self.items = [(ap, Buf("%s%d" % (name, i))) for i, ap in enumerate(aps)]
        self.i = 0

    def next(self):
        ap, buf = self.items[self.i % len(self.items)]
        self.i += 1
        buf.gen += 1
        return Tile(ap, buf)


def _unwrap(lst):
    out = []
    for b in lst:
        if isinstance(b, Tile):
            assert b.gen == b.buf.gen, "stale ring tile %s" % b.buf.name
            out.append(b.buf)
        else:
            out.append(b)
    return out


class Op:
    __slots__ = ("eng", "fn", "is_dma", "cdeps", "ddeps", "idx", "flag", "count", "dsem", "dval", "dprev")

    def __init__(self, eng, fn, is_dma):
        self.eng = eng
        self.fn = fn
        self.is_dma = is_dma
        self.cdeps = {}
        self.ddeps = set()
        self.flag = False
        self.count = 0
        self.dsem = None
        self.dval = 0
        self.dprev = None


class Sched:
    def __init__(self, nc):
        self.nc = nc
        self.ops = []
        self.per_eng = {e: [] for e in ENGINES}
        self.n_dma_e = {}

    def _adddep(self, o, d):
        if d is None or d is o:
            return
        if d.is_dma:
            o.ddeps.add(d)
        else:
            if d.eng == o.eng and not o.is_dma:
                if o.eng == "tensor" or not SAME_ENGINE_SYNC:
                    return
            cur = o.cdeps.get(d.eng)
            if cur is None or cur.idx < d.idx:
                o.cdeps[d.eng] = d

    def op(self, eng, fn, reads=(), writes=(), dma=False):
        reads = _unwrap(reads)
        writes = _unwrap(writes)
        o = Op(eng, fn, dma)
        o.idx = len(self.ops)
        for b in reads:
            self._adddep(o, b.writer)
        for b in writes:
            self._adddep(o, b.writer)
            for r in b.readers:
                self._adddep(o, r)
        for b in reads:
            b.readers.append(o)
        for b in writes:
            b.writer = o
            b.readers = []
        if dma:
            base, cnt = DMA_POOL[eng]
            k = self.n_dma_e.get(eng, 0)
            o.dsem = base + k % cnt
            o.dval = 16 * (k // cnt + 1)
            self.n_dma_e[eng] = k + 1
        self.ops.append(o)
        self.per_eng[eng].append(o)
        return o

    def alias(self, new_bufs, old_bufs):
        users = []
        for b in old_bufs:
            if b.writer is not None:
                users.append(b.writer)
            users.extend(b.readers)
        for b in new_bufs:
            b.writer = None
            b.readers = list(users)

    def emit(self, final_wait_ops):
        nc = self.nc
        for o in self.ops:
            for d in o.cdeps.values():
                d.flag = True
        for o in final_wait_ops:
            if not o.is_dma:
                o.flag = True
        for e in ENGINES:
            c = 0
            for o in self.per_eng[e]:
                if o.flag and not o.is_dma:
                    c += 1
                o.count = c
        last_on_sem = {}
        for o in self.ops:
            if o.is_dma:
                o.dprev = last_on_sem.get(o.dsem)
                last_on_sem[o.dsem] = o
        with contextlib.ExitStack() as es:
            esem = {e: es.enter_context(nc.semaphore("s_" + e)) for e in ENGINES}
            dsems = [es.enter_context(nc.semaphore("d_%d" % i)) for i in range(N_DMA_SEMS)]
            block = es.enter_context(nc.Block())

            def run_engine(e, eng):
                known = {x: 0 for x in ENGINES}
                dknown = [0] * N_DMA_SEMS
                for o in self.per_eng[e]:
                    dw = list(o.ddeps)
                    if o.is_dma and o.dprev is not None:
                        dw.append(o.dprev)
                    for d in dw:
                        if dknown[d.dsem] < d.dval:
                            eng.wait_ge(dsems[d.dsem], d.dval)
                            dknown[d.dsem] = d.dval
                    for d in o.cdeps.values():
                        if known[d.eng] < d.count:
                            eng.wait_ge(esem[d.eng], d.count)
                            known[d.eng] = d.count
                    ins = o.fn(eng)
                    if o.is_dma:
                        ins.then_inc(dsems[o.dsem], 16)
                    elif o.flag:
                        ins.then_inc(esem[e], 1)
                if e == "sync":
                    for d in final_wait_ops:
                        if d.is_dma:
                            eng.wait_ge(dsems[d.dsem], d.dval)
                        else:
                            eng.wait_ge(esem[d.eng], d.count)

            @block.tensor
            def _(eng):
                run_engine("tensor", eng)

            @block.vector
            def _(eng):
                run_engine("vector", eng)

            @block.scalar
            def _(eng):
                run_engine("scalar", eng)

            @block.gpsimd
            def _(eng):
                run_engine("gpsimd", eng)

            @block.sync
            def _(eng):
                run_engine("sync", eng)


def _rope_tables():
    t = np.arange(TL)
    row = (t // 64).astype(np.float32)
    col = (t % 64).astype(np.float32)
    inv = (np.float32(10000.0) ** (-np.arange(0, 32, 2, dtype=np.float32) / np.float32(32))).astype(np.float32)
    ar = (row[:, None] * inv).astype(np.float32)
    ac = (col[:, None] * inv).astype(np.float32)
    cr, sr, cc, sc = np.cos(ar), np.sin(ar), np.cos(ac), np.sin(ac)
    C = np.zeros((64, TL), np.float32)
    Sg = np.zeros((64, TL), np.float32)
    for d in range(64):
        i = d % 16
        first = (d % 32) < 16
        if d < 32:
            C[d] = cr[:, i]
            Sg[d] = -sr[:, i] if first else sr[:, i]
        else:
            C[d] = cc[:, i]
            Sg[d] = -sc[:, i] if first else sc[:, i]
    cs = np.zeros((128, 2, TL), np.float32)
    cs[:64, 0] = C
    cs[64:, 0] = C
    cs[:64, 1] = Sg
    cs[64:, 1] = Sg
    return cs


def _const_mats():
    m = np.zeros((128, 4, 128), np.float32)
    m[:, 0, :] = np.eye(128, dtype=np.float32)
    m[:, 1, :] = 1.0
    m[:64, 2, :64] = 1.0
    m[64:, 2, 64:] = 1.0
    for mm_ in range(128):
        d = mm_ % 64
        p = d + 16 if (d % 32) < 16 else d - 16
        m[(mm_ // 64) * 64 + p, 3, mm_] = 1.0
    return m


NAB_F = 36 * 64


def _na_bias_tables(rpb):
    kc = np.arange(64)[:, None]
    qc = np.arange(64)[None, :]
    cs = np.clip(qc - 8, 0, 48)
    colv = (kc >= cs) & (kc < cs + 16)
    dc = np.clip(kc - qc, -15, 15) + 15
    out = np.full((16, 128, 36, 64), NEG, np.float32)
    for kr2 in range(2):
        rows = slice(kr2 * 64, (kr2 + 1) * 64)
        for j in range(1, 15):
            dr = kr2 + 7 - j
            if -7 <= dr <= 7:
                out[:, rows, j - 1, :] = np.where(colv[None], rpb[:, dr + 7][:, dc], np.float32(NEG))
        for j in range(22):
            dr = kr2 + 10 - j
            if -4 <= dr <= 3:
                out[:, rows, 14 + j, :] = np.where(colv[None], rpb[:, dr + 7][:, dc], np.float32(NEG))
    return out.reshape(16, 128, NAB_F)


R_C = 0
R_DAQ, R_DAK, R_DAO, R_QA0, R_QA1, R_KVA, R_MQN, R_MQR, R_MK, R_MKR, R_NAQ, R_NAK = range(24, 36)
NPM2 = 64


def _pack_params(inp, b0):
    pm1 = np.zeros((128, 128), np.float32)
    pm1[0:96] = inp["mod_b"].reshape(96, 128)
    pm1[96:112] = inp["norm_mix_g"].reshape(16, 128)
    pm1[112:128] = inp["norm_ffn_g"].reshape(16, 128)
    pm2 = np.zeros((NPM2, 128), np.float32)
    pm2[0:8] = inp["c"][b0].reshape(8, 128)
    pm2[8:16] = inp["c"][b0 + 1].reshape(8, 128)
    pm2[16:24] = inp["c_ctx"].reshape(8, 128)
    rep = lambda v: np.concatenate([v, v])
    pm2[R_DAQ] = rep(inp["da_q_g"][0])
    pm2[R_DAK] = rep(inp["da_k_g"][0])
    pm2[R_DAO] = inp["da_out_g"][0]
    pm2[R_QA0] = inp["mla_q_a_g"][0][:128]
    pm2[R_QA1] = inp["mla_q_a_g"][0][128:]
    pm2[R_KVA] = inp["mla_kv_a_g"][0]
    pm2[R_MQN] = inp["mla_q_g"][0][:128]
    pm2[R_MQR] = rep(inp["mla_q_g"][0][128:])
    pm2[R_MK] = inp["mla_k_g"][0]
    pm2[R_MKR] = rep(inp["mla_kr_g"][0])
    pm2[R_NAQ] = rep(inp["na_q_g"][0])
    pm2[R_NAK] = rep(inp["na_k_g"][0])
    return pm1, pm2


def build_nc(nb=NB, nlayers=2, dbg=None):
    nc = bass.Bass("TRN2", target_bir_lowering=False)
    dt_in = lambda name, shape: nc.dram_tensor(name, shape, F32, kind="ExternalInput").ap()
    x_d = dt_in("x", [nb, TL, D])
    ctx_d = dt_in("ctx", [nb, 256, D])
    modw_d = dt_in("mod_w", [2, D, 6 * D])
    wout_d = dt_in("w_out", [2, D, D])
    fwi_d = dt_in("ffn_w_in", [2, D, 2 * DFF])
    fwo_d = dt_in("ffn_w_out", [2, DFF, D])
    ev_d = dt_in("ev_w_in", [D, 1984])
    uq_d = dt_in("mla_w_uq", [256, 768])
    ukv_d = dt_in("mla_w_ukv", [128, 1024])
    od_d = dt_in("od_w_in", [D, 3072])
    lam_d = dt_in("lamv", [4, 64])
    pm1_d = dt_in("pm1", [128, 128])
    pm2_d = dt_in("pm2", [NPM2, 128])
    cmat_d = dt_in("cmat", [128, 4, 128])
    rope_d = dt_in("ropecs", [128, 2, TL])
    nab_d = dt_in("nabias", [16, 128, NAB_F])
    out_d = nc.dram_tensor("out", [nb, TL, D], F32, kind="ExternalOutput").ap()
    dbg_d = None
    if dbg is not None:
        dbg_d = nc.dram_tensor("dbg", [128, 8, T], F32, kind="ExternalOutput").ap()

    sc = lambda name, shape: nc.dram_tensor(name, shape, BF16, kind="Internal").ap()
    ev_s = sc("ev_s", [16, 128, 8, 128])
    od_s = sc("od_s", [24, 128, 8, 128])
    wout_s = sc("wout_s", [2, D, D])
    wi_s = sc("wi_s", [2, 22, 128, 8, 256])
    wo_s = sc("wo_s", [2, 8, 128, 22, 128])

    S = Sched(nc)
    sb = nc.alloc_sbuf_tensor
    xT = sb("xT", [128, 8, T], F32)
    hT = sb("hT", [128, 8, T], BF16)
    ARENA = 36352
    arena = sb("arena", [128, ARENA], BF16)
    NSLOT = 8
    slots = arena[:, 0:NSLOT * T].rearrange("p (s t) -> p s t", s=NSLOT)
    off = NSLOT * T
    ropecs = arena[:, off:off + 2 * TL * 2].bitcast(F32).rearrange("p (a t) -> p a t", a=2)
    nabt = arena[:, 4 * T:4 * T + 2 * 2 * NAB_F].rearrange("p (u h f) -> p u h f", u=2, h=2)
    off += 2 * TL * 2
    ptiles = arena[:, off:off + 4 * 512].rearrange("p (s n) -> p s n", s=4)
    off += 4 * 512
    ytiles = arena[:, off:off + 2 * 512].rearrange("p (s n) -> p s n", s=2)
    off += 2 * 512
    uq_sb = arena[:, off:off + 2 * 768].rearrange("p (k n) -> p k n", k=2)
    off += 2 * 768
    ukv_sb = arena[:, off:off + 1024]
    off += 1024
    wproj = arena[:, off:off + 4 * 1024].rearrange("p (s n) -> p s n", s=4)
    off += 4 * 1024
    assert off <= ARENA, off
    a_sb = arena[:, 0:22 * 512].rearrange("p (j n) -> p j n", j=22)
    foff = 22 * 512
    wA = arena[:, foff:foff + 4 * 2048].rearrange("p (s n) -> p s n", s=4)
    foff += 4 * 2048
    wB = arena[:, foff:foff + 2 * 22 * 128].rearrange("p (s n) -> p s n", s=2)
    foff += 2 * 22 * 128
    assert foff <= ARENA, foff
    modw_t = arena[:, 0:3 * 4096].rearrange("p (s k n) -> p s k n", s=3, k=8)

    sqt = sb("sqt", [128, 4, 512], BF16)
    ftm = sb("ftm", [128, 6, 512], F32)
    ddt = sb("ddt", [128, 2, 512], F32)
    rst = sb("rst", [128, 2, 512], F32)
    stg = arena[:, 0:4096].bitcast(F32).rearrange("p (s n) -> p s n", s=2)
    cmf = sb("cmf", [128, 128], F32)
    cmb = sb("cmb", [128, 4, 128], BF16)
    pv1 = sb("pv1", [128, 128], F32)
    pv2 = sb("pv2", [128, NPM2], F32)
    modv = sb("modv", [128, 2, 48, 3], F32)
    Av = sb("Av", [128, 2, 2, 3, 8], F32)
    siluT = sb("siluT", [128, 3, 8], BF16)
    sm = sb("sm", [128, 16], F32)

    identf = cmf
    identb = cmb[:, 0, :]
    onesb = cmb[:, 1, :]
    blk64 = cmb[:, 2, :]
    permb = cmb[:, 3, :]
    epsb = sm[:, 0:1]
    neglam = sm[:, 1:2]
    og08 = sm[:, 2:3]
    naq8 = sm[:, 3:4]

    pst = [nc.alloc_psum_tensor("ps%d" % i, [128, 512], F32) for i in range(8)]
    psS = Ring(pst[0:4], "psS")
    psA = Ring(pst[4:8], "psA")
    sq_r = Ring([sqt[:, i, :] for i in range(4)], "sq")
    ft_r = Ring([ftm[:, i, :] for i in range(6)], "ft")
    dd_r = Ring([ddt[:, i, :] for i in range(2)], "dd")
    rs_r = Ring([rst[:, i, :] for i in range(2)], "rs")
    stg_r = Ring([stg[:, i, :] for i in range(2)], "stg")
    p_r = Ring([ptiles[:, i, :] for i in range(4)], "pt")
    y_r = Ring([ytiles[:, i, :] for i in range(2)], "yt")
    wproj_r = Ring([wproj[:, i, :] for i in range(4)], "wp")
    wA_r = Ring([wA[:, i, :] for i in range(4)], "wA")
    wB_r = Ring([wB[:, i, :] for i in range(2)], "wB")
    modw_r = Ring([modw_t[:, i] for i in range(3)], "mw")
    nab_r = Ring([nabt[:, i] for i in range(2)], "nab")

    xB = [[Buf("x%d_%d" % (c, tb)) for tb in range(5)] for c in range(8)]
    hB = [[Buf("h%d_%d" % (c, tb)) for tb in range(5)] for c in range(8)]
    slB = [[Buf("sl%d_%d" % (s, tb)) for tb in range(5)] for s in range(NSLOT)]
    aB = [Buf("a%d" % j) for j in range(22)]
    cB = Buf("consts")
    ropeB = Buf("rope")
    uqB = Buf("uq")
    ukvB = Buf("ukv")
    evB = [Buf("ev%d" % i) for i in range(16)]
    odB = [Buf("od%d" % i) for i in range(24)]
    woutB = [Buf("wout%d" % l) for l in range(2)]
    wiB = [[Buf("wi%d_%d" % (l, j)) for j in range(22)] for l in range(2)]
    woB = [[Buf("wo%d_%d" % (l, n)) for n in range(8)] for l in range(2)]
    modB = Buf("modv")
    outB = Buf("outd")
    arena_attn_bufs = [b for r in slB for b in r] + [ropeB, uqB, ukvB] + [it[1] for rr in (p_r, y_r, wproj_r, nab_r) for it in rr.items]
    arena_ffn_bufs = aB + [it[1] for rr in (wA_r, wB_r) for it in rr.items]
    arena_pro_bufs = [it[1] for it in modw_r.items]
    stg_bufs = [it[1] for it in stg_r.items]
    nab_bufs = [it[1] for it in nab_r.items]

    def mm(out, lhsT, rhs, start, stop, reads, writes):
        return S.op("tensor", lambda e: e.matmul(out, lhsT=lhsT, rhs=rhs, start=start, stop=stop), reads, writes)

    def act(out, in_, func, reads, writes, bias=None, scale=None):
        kw = {}
        if bias is not None:
            kw["bias"] = bias
        if scale is not None:
            kw["scale"] = scale
        return S.op("scalar", lambda e: e.activation(out=out, in_=in_, func=func, **kw), reads, writes)

    def dve(fn, reads, writes, eng="vector"):
        return S.op(eng, fn, reads, writes)

    def dma(eng, out, in_, reads, writes):
        return S.op(eng, lambda e: e.dma_start(out=out, in_=in_), reads, writes, dma=True)

    dma("sync", cmf[:], cmat_d[:, 0, :], [], [cB])
    dma("gpsimd", cmb[:], cmat_d, [], [cB])
    pmr_t = ft_r.next()
    pmr = pmr_t[:, 0:128]
    dma("sync", pmr, pm1_d, [], [pmr_t])
    lam_t = ft_r.next()
    lamt = lam_t[:, 0:256].rearrange("p (a n) -> p a n", a=4)
    dve(lambda e: e.memset(sm[:], 0.0), [], [cB])
    dve(lambda e: e.memset(epsb, EPS), [], [cB])
    for i in range(4):
        dma("sync", lamt[:, i, :], lam_d[i, :].partition_broadcast(128), [], [lam_t])
    t_ps = psA.next()
    mm(t_ps[:, 0:128], pmr, identf[:], True, True, [cB, pmr_t], [t_ps])
    dve(lambda e: e.tensor_copy(out=pv1[:], in_=t_ps[:, 0:128]), [t_ps], [cB])
    pm2r = ft_r.next()
    dma("sync", pm2r[0:NPM2, 0:128], pm2_d, [], [pm2r])
    t_ps2 = psA.next()
    mm(t_ps2[:, 0:NPM2], pm2r[0:NPM2, 0:128], identf[0:NPM2, 0:NPM2], True, True, [pm2r, cB], [t_ps2])
    dve(lambda e: e.tensor_copy(out=pv2[:], in_=t_ps2[:, 0:NPM2]), [t_ps2], [cB])
    act(siluT[:].rearrange("p s k -> p (s k)"), pv2[:, 0:24], AF.Silu, [cB], [cB])
    dve(lambda e: e.tensor_tensor(out=lamt[:, 0, :], in0=lamt[:, 0, :], in1=lamt[:, 1, :], op=ALU.mult), [cB, lam_t], [lam_t])
    dve(lambda e: e.tensor_tensor(out=lamt[:, 2, :], in0=lamt[:, 2, :], in1=lamt[:, 3, :], op=ALU.mult), [cB, lam_t], [lam_t])
    dve(lambda e: e.tensor_reduce(out=sm[:, 4:5], in_=lamt[:, 0, :], axis=AX.X, op=ALU.add), [cB, lam_t], [cB])
    dve(lambda e: e.tensor_reduce(out=sm[:, 5:6], in_=lamt[:, 2, :], axis=AX.X, op=ALU.add), [cB, lam_t], [cB])
    act(sm[:, 4:6], sm[:, 4:6], AF.Exp, [cB], [cB])
    dve(lambda e: e.tensor_tensor(out=sm[:, 6:7], in0=sm[:, 5:6], in1=sm[:, 4:5], op=ALU.subtract), [cB], [cB])
    dve(lambda e: e.tensor_scalar(out=neglam, in0=sm[:, 6:7], scalar1=-0.2, scalar2=None, op0=ALU.add), [cB], [cB])
    dve(lambda e: e.tensor_scalar(out=og08, in0=pv2[:, R_DAO:R_DAO + 1], scalar1=0.8, scalar2=None, op0=ALU.mult), [cB], [cB])
    dve(lambda e: e.tensor_scalar(out=naq8, in0=pv2[:, R_NAQ:R_NAQ + 1], scalar1=0.125, scalar2=None, op0=ALU.mult), [cB], [cB])

    for l in range(nlayers):
        mps = psA.next()
        mview = mps[:, 0:144].rearrange("p (n s) -> p n s", s=3)
        for g in range(12):
            wt = modw_r.next()
            dma("gpsimd", wt[:], modw_d[l].rearrange("(k p) n -> p k n", p=128)[:, :, g * 512:(g + 1) * 512], [], [wt])
            for nn in range(4):
                n = g * 4 + nn
                for kc in range(8):
                    mm(mview[:, n, :], wt[:, kc, nn * 128:(nn + 1) * 128], siluT[:, :, kc], kc == 0, kc == 7, [wt, cB], [mps])
        for s in range(3):
            dve(lambda e, s=s, l=l, mview=mview: e.tensor_tensor(out=modv[:, l, :, s], in0=mview[:, :, s], in1=pv1[:, l * 48:(l + 1) * 48], op=ALU.add), [mps, cB], [modB])
        for w_, part, grow in ((0, 1, 96), (1, 4, 112)):
            for s in range(3):
                dve(lambda e, l=l, w_=w_, part=part, grow=grow, s=s: e.scalar_tensor_tensor(
                    out=Av[:, l, w_, s, :], in0=modv[:, l, part * 8:(part + 1) * 8, s], scalar=1.0,
                    in1=pv1[:, grow + l * 8:grow + (l + 1) * 8], op0=ALU.add, op1=ALU.mult), [modB, cB], [modB])

    def Acol(l, w_, s, c):
        return Av[:, l, w_, s, c:c + 1]

    def Mcol(l, part, s, c):
        return modv[:, l, part * 8 + c, s:s + 1]

    S.alias(arena_attn_bufs, arena_pro_bufs)
    ev_src = ev_d.rearrange("(k p) n -> p k n", p=128)
    for nbk in range(16):
        w = 128 if nbk < 15 else 64
        dma("gpsimd", ev_s[nbk, :, :, 0:w], ev_src[:, :, nbk * 128:nbk * 128 + w], [], [evB[nbk]])
    dma("gpsimd", uq_sb, uq_d.rearrange("(k p) n -> p k n", p=128), [], [uqB])
    dma("gpsimd", ukv_sb, ukv_d, [], [ukvB])

    def precast_layer(l):
        dma("gpsimd", wout_s[l], wout_d[l], [], [woutB[l]])
        src = fwi_d[l].rearrange("(k p) n -> p k n", p=128)
        for j in range(22):
            dma("gpsimd", wi_s[l, j, :, :, 0:128], src[:, :, j * 128:(j + 1) * 128], [], [wiB[l][j]])
            dma("gpsimd", wi_s[l, j, :, :, 128:256], src[:, :, DFF + j * 128:DFF + (j + 1) * 128], [], [wiB[l][j]])
        srco = fwo_d[l].rearrange("(j p) n -> p j n", p=128)
        for n in range(8):
            dma("gpsimd", wo_s[l, n], srco[:, :, n * 128:(n + 1) * 128], [], [woB[l][n]])

    precast_layer(0)
    if nlayers > 1:
        od_src = od_d.rearrange("(k p) n -> p k n", p=128)
        for nbk in range(24):
            dma("gpsimd", od_s[nbk], od_src[:, :, nbk * 128:(nbk + 1) * 128], [], [odB[nbk]])
        precast_layer(1)

    def load_x(b):
        S.alias(stg_bufs, arena_attn_bufs)
        for t in range(18):
            st = stg_r.next()
            src = x_d[b, t * 128:(t + 1) * 128, :] if t < 16 else ctx_d[b, (t - 16) * 128:(t - 15) * 128, :]
            dma("sync", st[:], src, [], [st])
            tb = t // 4
            for half in range(2):
                ps = psA.next()
                for cc in range(4):
                    c = half * 4 + cc
                    mm(ps[:, cc * 128:(cc + 1) * 128], st[:, c * 128:(c + 1) * 128], identf[:], True, True, [st, cB], [ps])
                o = xT[:, half * 4:(half + 1) * 4, t * 128:(t + 1) * 128]
                i_ = ps[:, :].rearrange("p (c n) -> p c n", c=4)
                wr = [xB[half * 4 + cc][tb] for cc in range(4)]
                if half == 0:
                    dve(lambda e, o=o, i_=i_: e.tensor_copy(out=o, in_=i_), [ps], wr)
                else:
                    act(o, i_, AF.Copy, [ps], wr)
        S.alias(arena_attn_bufs, stg_bufs)

    def store_out(b):
        ops = []
        S.alias(stg_bufs, arena_attn_bufs)
        for t in range(16):
            st = stg_r.next()
            tb = t // 4
            for half in range(2):
                ps = psA.next()
                for cc in range(4):
                    c = half * 4 + cc
                    mm(ps[:, cc * 128:(cc + 1) * 128], xT[:, c, t * 128:(t + 1) * 128], identf[:], True, True, [xB[c][tb], cB], [ps])
                o = st[:, half * 512:(half + 1) * 512]
                if half == 0:
                    dve(lambda e, o=o, ps=ps: e.tensor_copy(out=o, in_=ps[:, :]), [ps], [st])
                else:
                    act(o, ps[:, :], AF.Copy, [ps], [st])
            ops.append(dma("sync", out_d[b, t * 128:(t + 1) * 128, :], st[:], [st], [outB]))
        S.alias(arena_attn_bufs, stg_bufs)
        return ops

    def norm_mod(l, w_, b, tbs):
        for tb in tbs:
            t0, n = TBLK[tb]
            s = 2 if tb == 4 else b
            ssP = psS.next()
            for c in range(8):
                sq = sq_r.next()
                act(sq[:, 0:n], xT[:, c, t0:t0 + n], AF.Square, [xB[c][tb]], [sq])
                mm(ssP[:, 0:n], onesb, sq[:, 0:n], c == 0, c == 7, [sq, cB], [ssP])
            sd = ft_r.next()
            act(sd[:, 0:n], ssP[:, 0:n], AF.Ln, [ssP, cB], [sd], bias=epsb, scale=1.0 / D)
            rs = rs_r.next()
            act(rs[:, 0:n], sd[:, 0:n], AF.Exp, [sd], [rs], scale=-0.5)
            for c in range(8):
                tmp = ft_r.next()
                dve(lambda e, tmp=tmp, c=c, rs=rs, t0=t0, n=n, s=s: e.scalar_tensor_tensor(
                    out=tmp[:, 0:n], in0=xT[:, c, t0:t0 + n], scalar=Acol(l, w_, s, c), in1=rs[:, 0:n],
                    op0=ALU.mult, op1=ALU.mult), [xB[c][tb], rs, modB], [tmp])
                act(hT[:, c, t0:t0 + n], tmp[:, 0:n], AF.Identity, [tmp, modB], [hB[c][tb]],
                    bias=Mcol(l, 0 if w_ == 0 else 3, s, c))

    def group_norm(tiles, nfeat, tb, rope):
        t0, n = TBLK[tb]
        ssP = psS.next()
        for i, (ps, M, G, gain, dfn, dbufs) in enumerate(tiles):
            sq = sq_r.next()
            act(sq[0:M, 0:n], ps[0:M, 0:n], AF.Square, [ps], [sq])
            mm(ssP[:, 0:n], G, sq[0:M, 0:n], i == 0, i == len(tiles) - 1, [sq, cB], [ssP])
        yield
        sd = ft_r.next()
        act(sd[:, 0:n], ssP[:, 0:n], AF.Ln, [ssP, cB], [sd], bias=epsb, scale=1.0 / nfeat)
        yield
        rs = rs_r.next()
        act(rs[:, 0:n], sd[:, 0:n], AF.Exp, [sd], [rs], scale=-0.5)
        yield
        todo = []
        for (ps, M, G, gain, dfn, dbufs), rp in zip(tiles, rope):
            if not rp:
                dve(lambda e, ps=ps, M=M, gain=gain, dfn=dfn: e.scalar_tensor_tensor(
                    out=dfn(t0, n), in0=ps[0:M, 0:n], scalar=gain, in1=rs[0:M, 0:n], op0=ALU.mult, op1=ALU.mult),
                    [ps, rs, cB], dbufs)
                continue
            xn = ft_r.next()
            dve(lambda e, ps=ps, M=M, gain=gain, xn=xn: e.scalar_tensor_tensor(
                out=xn[0:M, 0:n], in0=ps[0:M, 0:n], scalar=gain, in1=rs[0:M, 0:n], op0=ALU.mult, op1=ALU.mult),
                [ps, rs, cB], [xn])
            todo.append((M, dfn, dbufs, xn))
        if not todo:
            return
        yield
        st2 = []
        for (M, dfn, dbufs, xn) in todo:
            xb = sq_r.next()
            act(xb[0:M, 0:n], xn[0:M, 0:n], AF.Copy, [xn], [xb])
            pp = psS.next()
            mm(pp[0:M, 0:n], permb[0:M, 0:M], xb[0:M, 0:n], True, True, [xb, cB], [pp])
            st2.append((M, dfn, dbufs, xn, pp))
        yield
        st3 = []
        for (M, dfn, dbufs, xn, pp) in st2:
            dve(lambda e, xn=xn, M=M: e.tensor_tensor(out=xn[0:M, 0:n], in0=xn[0:M, 0:n], in1=ropecs[0:M, 0, t0:t0 + n], op=ALU.mult),
                [xn, ropeB], [xn])
            t2 = ft_r.next()
            dve(lambda e, pp=pp, t2=t2, M=M: e.tensor_tensor(out=t2[0:M, 0:n], in0=pp[0:M, 0:n], in1=ropecs[0:M, 1, t0:t0 + n], op=ALU.mult),
                [pp, ropeB], [t2])
            st3.append((M, dfn, dbufs, xn, t2))
        yield
        for (M, dfn, dbufs, xn, t2) in st3:
            dve(lambda e, xn=xn, t2=t2, M=M, dfn=dfn: e.tensor_tensor(out=dfn(t0, n), in0=xn[0:M, 0:n], in1=t2[0:M, 0:n], op=ALU.add),
                [xn, t2], dbufs)

    def run_chains(*gens):
        gens = list(gens)
        while gens:
            for g in list(gens):
                try:
                    next(g)
                except StopIteration:
                    gens.remove(g)

    def load_wtile(scr, scrB, ring=None):
        wt = (ring or wproj_r).next()
        dma("sync", wt[:], scr.rearrange("p k n -> p (k n)"), [scrB], [wt])
        return wt

    def proj_fm(wt, M, tb, col0=0, wstride=128):
        t0, n = TBLK[tb]
        ps = psA.next()
        for kc in range(8):
            mm(ps[0:M, 0:n], wt[:, kc * wstride + col0:kc * wstride + col0 + M], hT[:, kc, t0:t0 + n], kc == 0, kc == 7,
               [wt, hB[kc][tb]], [ps])
        return ps

    def proj_v_tm(wt, slot, ntiles=18):
        for t in range(ntiles):
            tb = t // 4
            ps = psS.next()
            for kc in range(8):
                mm(ps[:, 0:128], hT[:, kc, t * 128:(t + 1) * 128], wt[:, kc * 128:(kc + 1) * 128], kc == 0, kc == 7,
                   [wt, hB[kc][tb]], [ps])
            o = slots[:, slot, t * 128:(t + 1) * 128]
            if t % 2 == 0:
                dve(lambda e, o=o, ps=ps: e.tensor_copy(out=o, in_=ps[:, 0:128]), [ps], [slB[slot][tb]])
            else:
                act(o, ps[:, 0:128], AF.Copy, [ps], [slB[slot][tb]])

    def attn_core(units, scale, group=2, need_sm=True):
        O = psA.next()
        Sm = psA.next() if need_sm else None
        groups = [units[i:i + group] for i in range(0, len(units), group)]
        sts = {}

        def emit_qk(gi):
            for ui, u in enumerate(groups[gi]):
                st = psS.next()
                u[1](st)
                sts[(gi, ui)] = st
        emit_qk(0)
        if len(groups) > 1:
            emit_qk(1)
        n_units = len(units)
        done = 0
        for gi, g in enumerate(groups):
            ps = []
            for ui, u in enumerate(g):
                st = sts.pop((gi, ui))
                p = p_r.next()
                act(p[:, 0:u[0]], st[:, 0:u[0]], AF.Exp, [st], [p], scale=scale)
                if len(u) > 3 and u[3] is not None:
                    u[3](p)
                ps.append(p)
            for ui in reversed(range(len(g))):
                first = done == 0
                done += 1
                g[ui][2](ps[ui], O, Sm, first, done == n_units)
            if gi + 2 < len(groups):
                emit_qk(gi + 2)
        return O, Sm

    def tbs_of(t0, n):
        return sorted(set([t0 // 512, (t0 + n - 1) // 512]))

    def wout_accum(l, wt, yb, tb, b, q0=None, nq=None):
        t0, n = TBLK[tb]
        if q0 is not None:
            t0, n = q0, nq
        s = 2 if tb == 4 else b
        tbl_ = tbs_of(t0, n)
        for nchk in range(8):
            ps = psS.next()
            mm(ps[:, 0:n], wt[:, nchk * 128:(nchk + 1) * 128], yb[:, 0:n], True, True, [wt, yb], [ps])
            xbs = [xB[nchk][t_] for t_ in tbl_]
            dve(lambda e, ps=ps, nchk=nchk, t0=t0, n=n, s=s: e.scalar_tensor_tensor(
                out=xT[:, nchk, t0:t0 + n], in0=ps[:, 0:n], scalar=Mcol(l, 2, s, nchk), in1=xT[:, nchk, t0:t0 + n],
                op0=ALU.mult, op1=ALU.add), [ps, modB] + xbs, xbs)

    def ffn(l, b, tbs):
        S.alias(arena_ffn_bufs, arena_attn_bufs)
        for tb in tbs:
            t0, n = TBLK[tb]
            s = 2 if tb == 4 else b
            for j in range(22):
                wt = load_wtile(wi_s[l, j], wiB[l][j], wA_r)
                G = psS.next()
                U = psS.next()
                for kc in range(8):
                    mm(G[:, 0:n], wt[:, kc * 256:kc * 256 + 128], hT[:, kc, t0:t0 + n], kc == 0, kc == 7, [wt, hB[kc][tb]], [G])
                for kc in range(8):
                    mm(U[:, 0:n], wt[:, kc * 256 + 128:kc * 256 + 256], hT[:, kc, t0:t0 + n], kc == 0, kc == 7, [wt, hB[kc][tb]], [U])
                sg = ft_r.next()
                act(sg[:, 0:n], G[:, 0:n], AF.Silu, [G], [sg])
                dve(lambda e, sg=sg, U=U, j=j, n=n: e.tensor_tensor(out=a_sb[:, j, 0:n], in0=sg[:, 0:n], in1=U[:, 0:n], op=ALU.mult),
                    [sg, U], [aB[j]])
            for nchk in range(8):
                wt = load_wtile(wo_s[l, nchk], woB[l][nchk], wB_r)
                ps = psA.next()
                for j in range(22):
                    mm(ps[:, 0:n], wt[:, j * 128:(j + 1) * 128], a_sb[:, j, 0:n], j == 0, j == 21, [wt, aB[j]], [ps])
                dve(lambda e, ps=ps, nchk=nchk, t0=t0, n=n, s=s: e.scalar_tensor_tensor(
                    out=xT[:, nchk, t0:t0 + n], in0=ps[:, 0:n], scalar=Mcol(l, 5, s, nchk), in1=xT[:, nchk, t0:t0 + n],
                    op0=ALU.mult, op1=ALU.add), [ps, modB, xB[nchk][tb]], [xB[nchk][tb]])
        S.alias(arena_attn_bufs, arena_ffn_bufs)

    ALLCH = list(range(18))
    CTXCH = [16, 17]

    def da_qblock(tb, wo_t, b):
        l = 0
        SQ, SK, SV = 0, 1, 2
        t0, nq = TBLK[tb]
        chunks = ALLCH if tb < 4 else CTXCH
        ons = []
        for i in range(2):
            def mk_unit(kc, i=i):
                def qk_fn(st):
                    mm(st[:, 0:nq], slots[64 * i:64 * i + 64, SK, kc * 128:(kc + 1) * 128], slots[64 * i:64 * i + 64, SQ, t0:t0 + nq],
                       True, True, [slB[SK][kc // 4], slB[SQ][tb]], [st])

                def av_fn(p, O, Sm, first, last):
                    mm(O[:, 0:nq], slots[:, SV, kc * 128:(kc + 1) * 128], p[:, 0:nq], first, last, [p, slB[SV][kc // 4]], [O])
                    mm(Sm[:, 0:nq], onesb, p[:, 0:nq], first, last, [p, cB], [Sm])
                return (nq, qk_fn, av_fn)
            O, Sm = attn_core([mk_unit(kc) for kc in chunks], 0.125)
            rs = rs_r.next()
            dve(lambda e, rs=rs, Sm=Sm: e.reciprocal(out=rs[:, 0:nq], in_=Sm[:, 0:nq]), [Sm], [rs])
            on = ft_r.next()
            dve(lambda e, on=on, O=O, rs=rs: e.tensor_tensor(out=on[:, 0:nq], in0=O[:, 0:nq], in1=rs[:, 0:nq], op=ALU.mult), [O, rs], [on])
            ons.append(on)
        dd = dd_r.next()
        dve(lambda e: e.scalar_tensor_tensor(out=dd[:, 0:nq], in0=ons[1][:, 0:nq], scalar=neglam, in1=ons[0][:, 0:nq],
                                             op0=ALU.mult, op1=ALU.add), [ons[0], ons[1], cB], [dd])

        def tail():
            sq = sq_r.next()
            act(sq[:, 0:nq], dd[:, 0:nq], AF.Square, [dd], [sq])
            ssP = psS.next()
            mm(ssP[:, 0:nq], onesb, sq[:, 0:nq], True, True, [sq, cB], [ssP])
            sd = ft_r.next()
            act(sd[:, 0:nq], ssP[:, 0:nq], AF.Ln, [ssP, cB], [sd], bias=epsb, scale=1.0 / 128)
            rs2 = rs_r.next()
            act(rs2[:, 0:nq], sd[:, 0:nq], AF.Exp, [sd], [rs2], scale=-0.5)
            yb = y_r.next()
            dve(lambda e: e.scalar_tensor_tensor(out=yb[:, 0:nq], in0=dd[:, 0:nq], scalar=og08, in1=rs2[:, 0:nq],
                                                 op0=ALU.mult, op1=ALU.mult), [dd, rs2, cB], [yb])
            wout_accum(l, wo_t, yb, tb, b)
        return tail

    def mla_qblock(tb, wo_t, b):
        l = 0
        SCQ0, SCQ1, SCKV, SKR, SQN, SQR, SKN, SMV = range(8)
        t0, nq = TBLK[tb]
        chunks = ALLCH if tb < 4 else CTXCH

        def mk_unit(kc):
            def qk_fn(st):
                mm(st[:, 0:nq], slots[:, SKN, kc * 128:(kc + 1) * 128], slots[:, SQN, t0:t0 + nq], True, False,
                   [slB[SKN][kc // 4], slB[SQN][tb]], [st])
                mm(st[:, 0:nq], slots[0:64, SKR, kc * 128:(kc + 1) * 128], slots[0:64, SQR, t0:t0 + nq], False, True,
                   [slB[SKR][kc // 4], slB[SQR][tb]], [st])

            def av_fn(p, O, Sm, first, last):
                mm(O[:, 0:nq], slots[:, SMV, kc * 128:(kc + 1) * 128], p[:, 0:nq], first, last, [p, slB[SMV][kc // 4]], [O])
                mm(Sm[:, 0:nq], onesb, p[:, 0:nq], first, last, [p, cB], [Sm])
            return (nq, qk_fn, av_fn)
        O, Sm = attn_core([mk_unit(kc) for kc in chunks], 192.0 ** -0.5)
        rs = rs_r.next()
        dve(lambda e: e.reciprocal(out=rs[:, 0:nq], in_=Sm[:, 0:nq]), [Sm], [rs])
        yb = y_r.next()
        dve(lambda e: e.tensor_tensor(out=yb[:, 0:nq], in0=O[:, 0:nq], in1=rs[:, 0:nq], op=ALU.mult), [O, rs], [yb])
        return lambda: wout_accum(l, wo_t, yb, tb, b)

    def mla_head_proj(h, tb):
        SCQ0, SCQ1, SCKV, SKR, SQN, SQR, SKN, SMV = range(8)
        t0, n = TBLK[tb]
        pn = psA.next()
        for kc in range(2):
            mm(pn[:, 0:n], uq_sb[:, kc, 192 * h:192 * h + 128], slots[:, SCQ0 + kc, t0:t0 + n], kc == 0, kc == 1, [uqB, slB[SCQ0 + kc][tb]], [pn])
        pr = psA.next()
        for kc in range(2):
            mm(pr[0:64, 0:n], uq_sb[:, kc, 192 * h + 128:192 * h + 192], slots[:, SCQ0 + kc, t0:t0 + n], kc == 0, kc == 1, [uqB, slB[SCQ0 + kc][tb]], [pr])
        pk = psA.next()
        mm(pk[:, 0:n], ukv_sb[:, 256 * h:256 * h + 128], slots[:, SCKV, t0:t0 + n], True, True, [ukvB, slB[SCKV][tb]], [pk])
        run_chains(
            group_norm([(pn, 128, onesb, pv2[:, R_MQN:R_MQN + 1], lambda t0, n: slots[:, SQN, t0:t0 + n], [slB[SQN][tb]]),
                        (pr, 64, onesb[0:64, :], pv2[0:64, R_MQR:R_MQR + 1], lambda t0, n: slots[0:64, SQR, t0:t0 + n], [slB[SQR][tb]])],
                       192, tb, [False, tb < 4]),
            group_norm([(pk, 128, onesb, pv2[:, R_MK:R_MK + 1], lambda t0, n: slots[:, SKN, t0:t0 + n], [slB[SKN][tb]])], 128, tb, [False]))

    import os as _os
    NA_OLDV = _os.environ.get("NA_OLDV", "1") == "1"
    NA_KEEPSUM = NA_OLDV or _os.environ.get("NA_KEEPSUM", "0") == "1"
    NA_BLOCKS = [(0, 4), (4, 8), (12, 8), (20, 8), (28, 4)]
    if _os.environ.get("NA_R4", "0") == "1":
        NA_BLOCKS = [(4 * i, 4) for i in range(8)]

    def na_qblock(m, blk, wo_t, nab, b):
        l = 1
        SQ, SK, SV0 = 0, 1, 2
        q0, R = NA_BLOCKS[blk]
        t0, nq = q0 * 64, R * 64
        qtb = tbs_of(t0, nq)
        qbufs = [slB[SQ][t_] for t_ in qtb]
        if q0 == 0 or q0 == 28:
            lrows = [0, 2, 4, 6] if q0 == 0 else [24, 26, 28, 30]
            nbase = lambda r: (7 - (r - q0) - 1) * 64
        else:
            lrows = [q0 - 4 + 2 * i for i in range(R // 2 + 4)]
            nbase = lambda r: (14 + 10 - (r - q0)) * 64
        chunks = [("l", r) for r in lrows] + [("c", 32), ("c", 34)]
        per_unit = 512 // nq
        yb = y_r.next()
        for e_ in range(2):
            pl, ph = 64 * e_, 64 * e_ + 64
            ol, oh = 64 - pl, 128 - pl

            def mk_unit(grp, pl=pl, ph=ph, e_=e_):
                def qk_fn(st):
                    for hf, (kind, r) in enumerate(grp):
                        kc = r // 2
                        mm(st[:, hf * nq:(hf + 1) * nq], slots[pl:ph, SK, kc * 128:(kc + 1) * 128], slots[pl:ph, SQ, t0:t0 + nq], True, True,
                           [slB[SK][kc // 4]] + qbufs, [st])

                def post_fn(p):
                    for hf, (kind, r) in enumerate(grp):
                        if kind == "l":
                            nb_ = nbase(r)
                            dve(lambda e, p=p, hf=hf, nb_=nb_: e.tensor_tensor(out=p[:, hf * nq:(hf + 1) * nq], in0=p[:, hf * nq:(hf + 1) * nq],
                                                                              in1=nab[:, e_, nb_:nb_ + nq], op=ALU.mult), [p, nab], [p])

                def av_fn(p, O, Sm, first, last):
                    for hf, (kind, r) in enumerate(grp):
                        kc = r // 2
                        vs = SV0 if NA_OLDV else SV0 + e_
                        mm(O[:, 0:nq], slots[:, vs, kc * 128:(kc + 1) * 128], p[:, hf * nq:(hf + 1) * nq],
                           first and hf == 0, last and hf == len(grp) - 1, [p, slB[vs][kc // 4]], [O])
                        if NA_KEEPSUM:
                            mm(Sm[:, 0:nq], onesb, p[:, hf * nq:(hf + 1) * nq], first and hf == 0, last and hf == len(grp) - 1, [p, cB], [Sm])
                return (nq * len(grp), qk_fn, av_fn, post_fn)
            O, Sm = attn_core([mk_unit(chunks[i:i + per_unit]) for i in range(0, len(chunks), per_unit)], 1.0, need_sm=NA_KEEPSUM)
            rs = rs_r.next()
            lnv = ft_r.next()
            if NA_KEEPSUM:
                act(lnv[pl:ph, 0:nq], Sm[pl:ph, 0:nq], AF.Ln, [Sm], [lnv])
            else:
                act(lnv[pl:ph, 0:nq], O[ol:oh, 0:nq], AF.Ln, [O], [lnv])
            act(rs[pl:ph, 0:nq], lnv[pl:ph, 0:nq], AF.Exp, [lnv], [rs], scale=-1.0)
            dve(lambda e, O=O, rs=rs, pl=pl, ph=ph: e.tensor_tensor(out=yb[pl:ph, 0:nq], in0=O[pl:ph, 0:nq], in1=rs[pl:ph, 0:nq], op=ALU.mult),
                [O, rs], [yb])
        return lambda: wout_accum(l, wo_t, yb, qtb[0], b, q0=t0, nq=nq)

    def proj_v_na(wt):
        for t in range(18):
            tb = t // 4
            ps = psS.next()
            for kc in range(8):
                mm(ps[:, 0:128], hT[:, kc, t * 128:(t + 1) * 128], wt[:, kc * 128:(kc + 1) * 128], kc == 0, kc == 7,
                   [wt, hB[kc][tb]], [ps])
            a0 = 2 * T + t * 128
            o = arena[:, a0:a0 + 2 * (T + 64)].rearrange("p (s c) -> p s c", s=2)[:, :, 0:64]
            i_ = ps[:, 0:128].rearrange("p (s c) -> p s c", s=2)
            if t % 2 == 0:
                dve(lambda e, o=o, i_=i_: e.tensor_copy(out=o, in_=i_), [ps], [slB[2][tb], slB[3][tb]])
            else:
                act(o, i_, AF.Copy, [ps], [slB[2][tb], slB[3][tb]])

    def qblocks(need_ctx):
        return [0, 1, 2, 3, 4] if need_ctx else [0, 1, 2, 3]

    class Pend:
        def __init__(self):
            self.t = None

        def push(self, tail):
            old = self.t
            self.t = tail
            if old is not None:
                old()

        def flush(self):
            if self.t is not None:
                self.t()
            self.t = None

    def layer0(b):
        l = 0
        S.alias([ropeB] + [bb for sl in slB[4:8] for bb in sl], nab_bufs)
        dma("sync", ropecs, rope_d, [], [ropeB])
        norm_mod(l, 0, b, range(5))
        gq = pv2[:, R_DAQ:R_DAQ + 1]
        gk = pv2[:, R_DAK:R_DAK + 1]
        SQ, SK, SV = 0, 1, 2
        pend = Pend()
        for h in range(4):
            wq = load_wtile(ev_s[h], evB[h])
            wk = load_wtile(ev_s[4 + h], evB[4 + h])
            wv = load_wtile(ev_s[8 + h], evB[8 + h])
            for tb in range(5):
                rp = tb < 4
                psq = proj_fm(wq, 128, tb)
                psk = proj_fm(wk, 128, tb)
                run_chains(group_norm([(psq, 128, blk64, gq, lambda t0, n: slots[:, SQ, t0:t0 + n], [slB[SQ][tb]])], 64, tb, [rp]),
                           group_norm([(psk, 128, blk64, gk, lambda t0, n: slots[:, SK, t0:t0 + n], [slB[SK][tb]])], 64, tb, [rp]))
            proj_v_tm(wv, SV)
            pend.flush()
            wo_t = wproj_r.next()
            dma("sync", wo_t[:], wout_s[l, h * 128:(h + 1) * 128, :], [woutB[l]], [wo_t])
            for tb in range(5):
                pend.push(da_qblock(tb, wo_t, b))
        pend.flush()
        SCQ0, SCQ1, SCKV, SKR, SQN, SQR, SKN, SMV = range(8)
        wcq0 = load_wtile(ev_s[12], evB[12])
        wcq1 = load_wtile(ev_s[13], evB[13])
        wckv = load_wtile(ev_s[14], evB[14])
        wkr = load_wtile(ev_s[15], evB[15])
        for tb in range(5):
            p0 = proj_fm(wcq0, 128, tb)
            p1 = proj_fm(wcq1, 128, tb)
            p2 = proj_fm(wckv, 128, tb)
            p3 = proj_fm(wkr, 64, tb)
            run_chains(
                group_norm([(p0, 128, onesb, pv2[:, R_QA0:R_QA0 + 1], lambda t0, n: slots[:, SCQ0, t0:t0 + n], [slB[SCQ0][tb]]),
                            (p1, 128, onesb, pv2[:, R_QA1:R_QA1 + 1], lambda t0, n: slots[:, SCQ1, t0:t0 + n], [slB[SCQ1][tb]])], 256, tb, [False, False]),
                group_norm([(p2, 128, onesb, pv2[:, R_KVA:R_KVA + 1], lambda t0, n: slots[:, SCKV, t0:t0 + n], [slB[SCKV][tb]])], 128, tb, [False]))
            run_chains(
                group_norm([(p3, 64, onesb[0:64, :], pv2[0:64, R_MKR:R_MKR + 1], lambda t0, n: slots[0:64, SKR, t0:t0 + n], [slB[SKR][tb]])], 64, tb, [tb < 4]))
        for h in range(4):
            for tb in range(5):
                mla_head_proj(h, tb)
            for t in range(18):
                tb = t // 4
                ps = psS.next()
                mm(ps[:, 0:128], slots[:, SCKV, t * 128:(t + 1) * 128], ukv_sb[:, 256 * h + 128:256 * h + 256], True, True, [ukvB, slB[SCKV][tb]], [ps])
                o = slots[:, SMV, t * 128:(t + 1) * 128]
                if t % 2 == 0:
                    dve(lambda e, o=o, ps=ps: e.tensor_copy(out=o, in_=ps[:, 0:128]), [ps], [slB[SMV][tb]])
                else:
                    act(o, ps[:, 0:128], AF.Copy, [ps], [slB[SMV][tb]])
            pend.flush()
            wo_t = wproj_r.next()
            dma("sync", wo_t[:], wout_s[l, (4 + h) * 128:(5 + h) * 128, :], [woutB[l]], [wo_t])
            for tb in range(5):
                pend.push(mla_qblock(tb, wo_t, b))
        pend.flush()
        norm_mod(l, 1, b, range(5))
        ffn(l, b, range(5))

    def layer1(b):
        l = 1
        import os as _os
        S.alias(nab_bufs, [ropeB] + [bb for sl in slB[4:8] for bb in sl])
        norm_mod(l, 0, b, range(5))
        SQ, SK, SV = 0, 1, 2
        if (not NA_OLDV) or _os.environ.get('NA_T1', '0') == '1':
            for t in range(18):
                o0 = slots[:, 2, t * 128 + 64:(t + 1) * 128]
                o1 = slots[:, 3, t * 128:t * 128 + 64]
                dve(lambda e, o0=o0: e.tensor_copy(out=o0, in_=onesb[:, 0:64]), [cB], [slB[2][t // 4]])
                act(o1, onesb[:, 0:64], AF.Copy, [cB], [slB[3][t // 4]])
        gq = naq8
        gk = pv2[:, R_NAK:R_NAK + 1]
        pend = Pend()
        for m in range(8):
            wq = load_wtile(od_s[m], odB[m])
            wk = load_wtile(od_s[8 + m], odB[8 + m])
            wv = load_wtile(od_s[16 + m], odB[16 + m])
            nab = nab_r.next()
            dma("gpsimd", nab[:], nab_d[2 * m:2 * m + 2].rearrange("h p f -> p h f"), [], [nab])
            for e_ in range(2):
                act(nab[:, e_, :], nab[:, e_, :], AF.Exp, [nab], [nab])
            for tb in range(5):
                psk = proj_fm(wk, 128, tb)
                ch = [group_norm([(psk, 128, blk64, gk, lambda t0, n: slots[:, SK, t0:t0 + n], [slB[SK][tb]])], 64, tb, [False])]
                if tb < 4:
                    psq = proj_fm(wq, 128, tb)
                    ch.append(group_norm([(psq, 128, blk64, gq, lambda t0, n: slots[:, SQ, t0:t0 + n], [slB[SQ][tb]])], 64, tb, [False]))
                run_chains(*ch)
            if NA_OLDV and _os.environ.get('NA_T2', '0') != '1':
                proj_v_tm(wv, 2)
            else:
                proj_v_na(wv)
            pend.flush()
            wo_t = wproj_r.next()
            dma("sync", wo_t[:], wout_s[l, m * 128:(m + 1) * 128, :], [woutB[l]], [wo_t])
            import os as _os
            for blk in [int(v) for v in _os.environ.get("NA_DEBUG_BLOCKS", ",".join(str(i) for i in range(len(NA_BLOCKS)))).split(",")]:
                pend.push(na_qblock(m, blk, wo_t, nab, b))
        pend.flush()
        norm_mod(l, 1, b, range(4))
        ffn(l, b, range(4))

    final_ops = []
    for b in range(nb):
        load_x(b)
        if dbg == "x0":
            break
        layer0(b)
        if dbg == "l0" or nlayers == 1:
            break
        layer1(b)
        final_ops += store_out(b)
    if dbg is not None:
        for c in range(8):
            final_ops.append(dma("sync", dbg_d[:, c, :], xT[:, c, :], [xB[c][tb] for tb in range(5)], [outB]))
    S.emit(final_ops)
    return nc


_NC_CACHE = {}


def kernel(**inputs):
    inp = {k: np.ascontiguousarray(np.asarray(v)) for k, v in inputs.items()}
    if "nc" not in _NC_CACHE:
        _NC_CACHE["nc"] = build_nc()
    nc = _NC_CACHE["nc"]
    cmat = _const_mats()
    rope = _rope_tables()
    nab = _na_bias_tables(inp["na_rpb"][0])
    lamv = np.stack([inp["da_lq1"][0], inp["da_lk1"][0], inp["da_lq2"][0], inp["da_lk2"][0]]).astype(np.float32)
    in_maps = []
    for core in range(NCORES):
        b0 = core * NB
        pm1, pm2 = _pack_params(inp, b0)
        in_maps.append({
            "x": inp["x"][b0:b0 + NB], "ctx": inp["ctx"][b0:b0 + NB],
            "mod_w": inp["mod_w"], "w_out": inp["w_out"], "ffn_w_in": inp["ffn_w_in"], "ffn_w_out": inp["ffn_w_out"],
            "ev_w_in": inp["ev_w_in"][0], "mla_w_uq": inp["mla_w_uq"][0], "mla_w_ukv": inp["mla_w_ukv"][0],
            "od_w_in": inp["od_w_in"][0], "lamv": lamv, "pm1": pm1, "pm2": pm2, "cmat": cmat, "ropecs": rope, "nabias": nab,
        })
    res = run_bass_kernel_spmd(nc, in_maps, core_ids=list(range(NCORES)))
    out = np.concatenate([np.asarray(r["out"]) for r in res.results], axis=0)
    return out.astype(np.float32)
```

```python
import contextlib
import numpy as np
import concourse.bass as bass
import concourse.mybir as mybir
from concourse.bass_utils import run_bass_kernel_spmd

F32 = mybir.dt.float32
BF16 = mybir.dt.bfloat16
ALU = mybir.AluOpType
AF = mybir.ActivationFunctionType
AX = mybir.AxisListType

NCORES = 8
NB = 2
D = 1024
T = 2304
TL = 2048
TBLK = [(0, 512), (512, 512), (1024, 512), (1536, 512), (2048, 256)]
DFF = 2816
EPS = 1e-6
NEG = -1e30

ENGINES = ("tensor", "vector", "scalar", "gpsimd", "sync")
DMA_POOL = {"sync": (0, 14), "gpsimd": (14, 8), "scalar": (22, 2)}
N_DMA_SEMS = 24
SAME_ENGINE_SYNC = True


class Buf:
    __slots__ = ("name", "writer", "readers", "gen")

    def __init__(self, name):
        self.name = name
        self.writer = None
        self.readers = []
        self.gen = 0


class Tile:
    __slots__ = ("ap", "buf", "gen")

    def __init__(self, ap, buf):
        self.ap = ap
        self.buf = buf
        self.gen = buf.gen

    def __getitem__(self, k):
        return self.ap[k]


class Ring:
    def __init__(self, aps, name):
        self.items = [(ap, Buf("%s%d" % (name, i))) for i, ap in enumerate(aps)]
        self.i = 0

    def next(self):
        ap, buf = self.items[self.i % len(self.items)]
        self.i += 1
        buf.gen += 1
        return Tile(ap, buf)


def _unwrap(lst):
    out = []
    for b in lst:
        if isinstance(b, Tile):
            assert b.gen == b.buf.gen, "stale ring tile %s" % b.buf.name
            out.append(b.buf)
        else:
            out.append(b)
    return out


class Op:
    __slots__ = ("eng", "fn", "is_dma", "cdeps", "ddeps", "idx", "flag", "count", "dsem", "dval", "dprev")

    def __init__(self, eng, fn, is_dma):
        self.eng = eng
        self.fn = fn
        self.is_dma = is_dma
        self.cdeps = {}
        self.ddeps = set()
        self.flag = False
        self.count = 0
        self.dsem = None
        self.dval = 0
        self.dprev = None


class Sched:
    def __init__(self, nc):
        self.nc = nc
        self.ops = []
        self.per_eng = {e: [] for e in ENGINES}
        self.n_dma_e = {}

    def _adddep(self, o, d):
        if d is None or d is o:
            return
        if d.is_dma:
            o.ddeps.add(d)
        else:
            if d.eng == o.eng and not o.is_dma:
                if o.eng == "tensor" or not SAME_ENGINE_SYNC:
                    return
            cur = o.cdeps.get(d.eng)
            if cur is None or cur.idx < d.idx:
                o.cdeps[d.eng] = d

    def op(self, eng, fn, reads=(), writes=(), dma=False):
        reads = _unwrap(reads)
        writes = _unwrap(writes)
        o = Op(eng, fn, dma)
        o.idx = len(self.ops)
        for b in reads:
            self._adddep(o, b.writer)
        for b in writes:
            self._adddep(o, b.writer)
            for r in b.readers:
                self._adddep(o, r)
        for b in reads:
            b.readers.append(o)
        for b in writes:
            b.writer = o
            b.readers = []
        if dma:
            base, cnt = DMA_POOL[eng]
            k = self.n_dma_e.get(eng, 0)
            o.dsem = base + k % cnt
            o.dval = 16 * (k // cnt + 1)
            self.n_dma_e[eng] = k + 1
        self.ops.append(o)
        self.per_eng[eng].append(o)
        return o

    def alias(self, new_bufs, old_bufs):
        users = []
        for b in old_bufs:
            if b.writer is not None:
                users.append(b.writer)
            users.extend(b.readers)
        for b in new_bufs:
            b.writer = None
            b.readers = list(users)

    def emit(self, final_wait_ops):
        nc = self.nc
        for o in self.ops:
            for d in o.cdeps.values():
                d.flag = True
        for o in final_wait_ops:
            if not o.is_dma:
                o.flag = True
        for e in ENGINES:
            c = 0
            for o in self.per_eng[e]:
                if o.flag and not o.is_dma:
                    c += 1
                o.count = c
        last_on_sem = {}
        for o in self.ops:
            if o.is_dma:
                o.dprev = last_on_sem.get(o.dsem)
                last_on_sem[o.dsem] = o
        with contextlib.ExitStack() as es:
            esem = {e: es.enter_context(nc.semaphore("s_" + e)) for e in ENGINES}
            dsems = [es.enter_context(nc.semaphore("d_%d" % i)) for i in range(N_DMA_SEMS)]
            block = es.enter_context(nc.Block())

            def run_engine(e, eng):
                known = {x: 0 for x in ENGINES}
                dknown = [0] * N_DMA_SEMS
                for o in self.per_eng[e]:
                    dw = list(o.ddeps)
                    if o.is_dma and o.dprev is not None:
                        dw.append(o.dprev)
                    for d in dw:
                        if dknown[d.dsem] < d.dval:
                            eng.wait_ge(dsems[d.dsem], d.dval)
                            dknown[d.dsem] = d.dval
                    for d in o.cdeps.values():
                        if known[d.eng] < d.count:
                            eng.wait_ge(esem[d.eng], d.count)
                            known[d.eng] = d.count
                    ins = o.fn(eng)
                    if o.is_dma:
                        ins.then_inc(dsems[o.dsem], 16)
                    elif o.flag:
                        ins.then_inc(esem[e], 1)
                if e == "sync":
                    for d in final_wait_ops:
                        if d.is_dma:
                            eng.wait_ge(dsems[d.dsem], d.dval)
                        else:
                            eng.wait_ge(esem[d.eng], d.count)

            @block.tensor
            def _(eng):
                run_engine("tensor", eng)

            @block.vector
            def _(eng):
                run_engine("vector", eng)

            @block.scalar
            def _(eng):
                run_engine("scalar", eng)

            @block.gpsimd
            def _(eng):
                run_engine("gpsimd", eng)

            @block.sync
            def _(eng):
                run_engine("sync", eng)


def _rope_tables():
    t = np.arange(TL)
    row = (t // 64).astype(np.float32)
    col = (t % 64).astype(np.float32)
    inv = (np.float32(10000.0) ** (-np.arange(0, 32, 2, dtype=np.float32) / np.float32(32))).astype(np.float32)
    ar = (row[:, None] * inv).astype(np.float32)
    ac = (col[:, None] * inv).astype(np.float32)
    cr, sr, cc, sc = np.cos(ar), np.sin(ar), np.cos(ac), np.sin(ac)
    C = np.zeros((64, TL), np.float32)
    Sg = np.zeros((64, TL), np.float32)
    for d in range(64):
        i = d % 16
        first = (d % 32) < 16
        if d < 32:
            C[d] = cr[:, i]
            Sg[d] = -sr[:, i] if first else sr[:, i]
        else:
            C[d] = cc[:, i]
            Sg[d] = -sc[:, i] if first else sc[:, i]
    cs = np.zeros((128, 2, TL), np.float32)
    cs[:64, 0] = C
    cs[64:, 0] = C
    cs[:64, 1] = Sg
    cs[64:, 1] = Sg
    return cs


def _const_mats():
    m = np.zeros((128, 4, 128), np.float32)
    m[:, 0, :] = np.eye(128, dtype=np.float32)
    m[:, 1, :] = 1.0
    m[:64, 2, :64] = 1.0
    m[64:, 2, 64:] = 1.0
    for mm_ in range(128):
        d = mm_ % 64
        p = d + 16 if (d % 32) < 16 else d - 16
        m[(mm_ // 64) * 64 + p, 3, mm_] = 1.0
    return m


NAB_F = 36 * 64


def _na_bias_tables(rpb):
    kc = np.arange(64)[:, None]
    qc = np.arange(64)[None, :]
    cs = np.clip(qc - 8, 0, 48)
    colv = (kc >= cs) & (kc < cs + 16)
    dc = np.clip(kc - qc, -15, 15) + 15
    out = np.full((16, 128, 36, 64), NEG, np.float32)
    for kr2 in range(2):
        rows = slice(kr2 * 64, (kr2 + 1) * 64)
        for j in range(1, 15):
            dr = kr2 + 7 - j
            if -7 <= dr <= 7:
                out[:, rows, j - 1, :] = np.where(colv[None], rpb[:, dr + 7][:, dc], np.float32(NEG))
        for j in range(22):
            dr = kr2 + 10 - j
            if -4 <= dr <= 3:
                out[:, rows, 14 + j, :] = np.where(colv[None], rpb[:, dr + 7][:, dc], np.float32(NEG))
    return out.reshape(16, 128, NAB_F)


R_C = 0
R_DAQ, R_DAK, R_DAO, R_QA0, R_QA1, R_KVA, R_MQN, R_MQR, R_MK, R_MKR, R_NAQ, R_NAK = range(24, 36)
NPM2 = 64


def _pack_params(inp, b0):
    pm1 = np.zeros((128, 128), np.float32)
    pm1[0:96] = inp["mod_b"].reshape(96, 128)
    pm1[96:112] = inp["norm_mix_g"].reshape(16, 128)
    pm1[112:128] = inp["norm_ffn_g"].reshape(16, 128)
    pm2 = np.zeros((NPM2, 128), np.float32)
    pm2[0:8] = inp["c"][b0].reshape(8, 128)
    pm2[8:16] = inp["c"][b0 + 1].reshape(8, 128)
    pm2[16:24] = inp["c_ctx"].reshape(8, 128)
    rep = lambda v: np.concatenate([v, v])
    pm2[R_DAQ] = rep(inp["da_q_g"][0])
    pm2[R_DAK] = rep(inp["da_k_g"][0])
    pm2[R_DAO] = inp["da_out_g"][0]
    pm2[R_QA0] = inp["mla_q_a_g"][0][:128]
    pm2[R_QA1] = inp["mla_q_a_g"][0][128:]
    pm2[R_KVA] = inp["mla_kv_a_g"][0]
    pm2[R_MQN] = inp["mla_q_g"][0][:128]
    pm2[R_MQR] = rep(inp["mla_q_g"][0][128:])
    pm2[R_MK] = inp["mla_k_g"][0]
    pm2[R_MKR] = rep(inp["mla_kr_g"][0])
    pm2[R_NAQ] = rep(inp["na_q_g"][0])
    pm2[R_NAK] = rep(inp["na_k_g"][0])
    return pm1, pm2


def build_nc(nb=NB, nlayers=2, dbg=None):
    nc = bass.Bass("TRN2", target_bir_lowering=False)
    dt_in = lambda name, shape: nc.dram_tensor(name, shape, F32, kind="ExternalInput").ap()
    x_d = dt_in("x", [nb, TL, D])
    ctx_d = dt_in("ctx", [nb, 256, D])
    modw_d = dt_in("mod_w", [2, D, 6 * D])
    wout_d = dt_in("w_out", [2, D, D])
    fwi_d = dt_in("ffn_w_in", [2, D, 2 * DFF])
    fwo_d = dt_in("ffn_w_out", [2, DFF, D])
    ev_d = dt_in("ev_w_in", [D, 1984])
    uq_d = dt_in("mla_w_uq", [256, 768])
    ukv_d = dt_in("mla_w_ukv", [128, 1024])
    od_d = dt_in("od_w_in", [D, 3072])
    lam_d = dt_in("lamv", [4, 64])
    pm1_d = dt_in("pm1", [128, 128])
    pm2_d = dt_in("pm2", [NPM2, 128])
    cmat_d = dt_in("cmat", [128, 4, 128])
    rope_d = dt_in("ropecs", [128, 2, TL])
    nab_d = dt_in("nabias", [16, 128, NAB_F])
    out_d = nc.dram_tensor("out", [nb, TL, D], F32, kind="ExternalOutput").ap()
    dbg_d = None
    if dbg is not None:
        dbg_d = nc.dram_tensor("dbg", [128, 8, T], F32, kind="ExternalOutput").ap()

    sc = lambda name, shape: nc.dram_tensor(name, shape, BF16, kind="Internal").ap()
    ev_s = sc("ev_s", [16, 128, 8, 128])
    od_s = sc("od_s", [24, 128, 8, 128])
    wout_s = sc("wout_s", [2, D, D])
    wi_s = sc("wi_s", [2, 22, 128, 8, 256])
    wo_s = sc("wo_s", [2, 8, 128, 22, 128])

    S = Sched(nc)
    sb = nc.alloc_sbuf_tensor
    xT = sb("xT", [128, 8, T], F32)
    hT = sb("hT", [128, 8, T], BF16)
    ARENA = 36352
    arena = sb("arena", [128, ARENA], BF16)
    NSLOT = 8
    slots = arena[:, 0:NSLOT * T].rearrange("p (s t) -> p s t", s=NSLOT)
    off = NSLOT * T
    ropecs = arena[:, off:off + 2 * TL * 2].bitcast(F32).rearrange("p (a t) -> p a t", a=2)
    nabt = arena[:, 4 * T:4 * T + 2 * 2 * NAB_F].rearrange("p (u h f) -> p u h f", u=2, h=2)
    off += 2 * TL * 2
    ptiles = arena[:, off:off + 4 * 512].rearrange("p (s n) -> p s n", s=4)
    off += 4 * 512
    ytiles = arena[:, off:off + 2 * 512].rearrange("p (s n) -> p s n", s=2)
    off += 2 * 512
    uq_sb = arena[:, off:off + 2 * 768].rearrange("p (k n) -> p k n", k=2)
    off += 2 * 768
    ukv_sb = arena[:, off:off + 1024]
    off += 1024
    wproj = arena[:, off:off + 4 * 1024].rearrange("p (s n) -> p s n", s=4)
    off += 4 * 1024
    assert off <= ARENA, off
    a_sb = arena[:, 0:22 * 512].rearrange("p (j n) -> p j n", j=22)
    foff = 22 * 512
    wA = arena[:, foff:foff + 4 * 2048].rearrange("p (s n) -> p s n", s=4)
    foff += 4 * 2048
    wB = arena[:, foff:foff + 2 * 22 * 128].rearrange("p (s n) -> p s n", s=2)
    foff += 2 * 22 * 128
    assert foff <= ARENA, foff
    modw_t = arena[:, 0:3 * 4096].rearrange("p (s k n) -> p s k n", s=3, k=8)

    sqt = sb("sqt", [128, 4, 512], BF16)
    ftm = sb("ftm", [128, 6, 512], F32)
    ddt = sb("ddt", [128, 2, 512], F32)
    rst = sb("rst", [128, 2, 512], F32)
    stg = arena[:, 0:4096].bitcast(F32).rearrange("p (s n) -> p s n", s=2)
    cmf = sb("cmf", [128, 128], F32)
    cmb = sb("cmb", [128, 4, 128], BF16)
    pv1 = sb("pv1", [128, 128], F32)
    pv2 = sb("pv2", [128, NPM2], F32)
    modv = sb("modv", [128, 2, 48, 3], F32)
    Av = sb("Av", [128, 2, 2, 3, 8], F32)
    siluT = sb("siluT", [128, 3, 8], BF16)
    sm = sb("sm", [128, 16], F32)

    identf = cmf
    identb = cmb[:, 0, :]
    onesb = cmb[:, 1, :]
    blk64 = cmb[:, 2, :]
    permb = cmb[:, 3, :]
    epsb = sm[:, 0:1]
    neglam = sm[:, 1:2]
    og08 = sm[:, 2:3]
    naq8 = sm[:, 3:4]

    pst = [nc.alloc_psum_tensor("ps%d" % i, [128, 512], F32) for i in range(8)]
    psS = Ring(pst[0:4], "psS")
    psA = Ring(pst[4:8], "psA")
    sq_r = Ring([sqt[:, i, :] for i in range(4)], "sq")
    ft_r = Ring([ftm[:, i, :] for i in range(6)], "ft")
    dd_r = Ring([ddt[:, i, :] for i in range(2)], "dd")
    rs_r = Ring([rst[:, i, :] for i in range(2)], "rs")
    stg_r = Ring([stg[:, i, :] for i in range(2)], "stg")
    p_r = Ring([ptiles[:, i, :] for i in range(4)], "pt")
    y_r = Ring([ytiles[:, i, :] for i in range(2)], "yt")
    wproj_r = Ring([wproj[:, i, :] for i in range(4)], "wp")
    wA_r = Ring([wA[:, i, :] for i in range(4)], "wA")
    wB_r = Ring([wB[:, i, :] for i in range(2)], "wB")
    modw_r = Ring([modw_t[:, i] for i in range(3)], "mw")
    nab_r = Ring([nabt[:, i] for i in range(2)], "nab")

    xB = [[Buf("x%d_%d" % (c, tb)) for tb in range(5)] for c in range(8)]
    hB = [[Buf("h%d_%d" % (c, tb)) for tb in range(5)] for c in range(8)]
    slB = [[Buf("sl%d_%d" % (s, tb)) for tb in range(5)] for s in range(NSLOT)]
    aB = [Buf("a%d" % j) for j in range(22)]
    cB = Buf("consts")
    ropeB = Buf("rope")
    uqB = Buf("uq")
    ukvB = Buf("ukv")
    evB = [Buf("ev%d" % i) for i in range(16)]
    odB = [Buf("od%d" % i) for i in range(24)]
    woutB = [Buf("wout%d" % l) for l in range(2)]
    wiB = [[Buf("wi%d_%d" % (l, j)) for j in range(22)] for l in range(2)]
    woB = [[Buf("wo%d_%d" % (l, n)) for n in range(8)] for l in range(2)]
    modB = Buf("modv")
    outB = Buf("outd")
    arena_attn_bufs = [b for r in slB for b in r] + [ropeB, uqB, ukvB] + [it[1] for rr in (p_r, y_r, wproj_r, nab_r) for it in rr.items]
    arena_ffn_bufs = aB + [it[1] for rr in (wA_r, wB_r) for it in rr.items]
    arena_pro_bufs = [it[1] for it in modw_r.items]
    stg_bufs = [it[1] for it in stg_r.items]
    nab_bufs = [it[1] for it in nab_r.items]

    def mm(out, lhsT, rhs, start, stop, reads, writes):
        return S.op("tensor", lambda e: e.matmul(out, lhsT=lhsT, rhs=rhs, start=start, stop=stop), reads, writes)

    def act(out, in_, func, reads, writes, bias=None, scale=None):
        kw = {}
        if bias is not None:
            kw["bias"] = bias
        if scale is not None:
            kw["scale"] = scale
        return S.op("scalar", lambda e: e.activation(out=out, in_=in_, func=func, **kw), reads, writes)

    def dve(fn, reads, writes, eng="vector"):
        return S.op(eng, fn, reads, writes)

    def dma(eng, out, in_, reads, writes):
        return S.op(eng, lambda e: e.dma_start(out=out, in_=in_), reads, writes, dma=True)

    dma("sync", cmf[:], cmat_d[:, 0, :], [], [cB])
    dma("gpsimd", cmb[:], cmat_d, [], [cB])
    pmr_t = ft_r.next()
    pmr = pmr_t[:, 0:128]
    dma("sync", pmr, pm1_d, [], [pmr_t])
    lam_t = ft_r.next()
    lamt = lam_t[:, 0:256].rearrange("p (a n) -> p a n", a=4)
    dve(lambda e: e.memset(sm[:], 0.0), [], [cB])
    dve(lambda e: e.memset(epsb, EPS), [], [cB])
    for i in range(4):
        dma("sync", lamt[:, i, :], lam_d[i, :].partition_broadcast(128), [], [lam_t])
    t_ps = psA.next()
    mm(t_ps[:, 0:128], pmr, identf[:], True, True, [cB, pmr_t], [t_ps])
    dve(lambda e: e.tensor_copy(out=pv1[:], in_=t_ps[:, 0:128]), [t_ps], [cB])
    pm2r = ft_r.next()
    dma("sync", pm2r[0:NPM2, 0:128], pm2_d, [], [pm2r])
    t_ps2 = psA.next()
    mm(t_ps2[:, 0:NPM2], pm2r[0:NPM2, 0:128], identf[0:NPM2, 0:NPM2], True, True, [pm2r, cB], [t_ps2])
    dve(lambda e: e.tensor_copy(out=pv2[:], in_=t_ps2[:, 0:NPM2]), [t_ps2], [cB])
    act(siluT[:].rearrange("p s k -> p (s k)"), pv2[:, 0:24], AF.Silu, [cB], [cB])
    dve(lambda e: e.tensor_tensor(out=lamt[:, 0, :], in0=lamt[:, 0, :], in1=lamt[:, 1, :], op=ALU.mult), [cB, lam_t], [lam_t])
    dve(lambda e: e.tensor_tensor(out=lamt[:, 2, :], in0=lamt[:, 2, :], in1=lamt[:, 3, :], op=ALU.mult), [cB, lam_t], [lam_t])
    dve(lambda e: e.tensor_reduce(out=sm[:, 4:5], in_=lamt[:, 0, :], axis=AX.X, op=ALU.add), [cB, lam_t], [cB])
    dve(lambda e: e.tensor_reduce(out=sm[:, 5:6], in_=lamt[:, 2, :], axis=AX.X, op=ALU.add), [cB, lam_t], [cB])
    act(sm[:, 4:6], sm[:, 4:6], AF.Exp, [cB], [cB])
    dve(lambda e: e.tensor_tensor(out=sm[:, 6:7], in0=sm[:, 5:6], in1=sm[:, 4:5], op=ALU.subtract), [cB], [cB])
    dve(lambda e: e.tensor_scalar(out=neglam, in0=sm[:, 6:7], scalar1=-0.2, scalar2=None, op0=ALU.add), [cB], [cB])
    dve(lambda e: e.tensor_scalar(out=og08, in0=pv2[:, R_DAO:R_DAO + 1], scalar1=0.8, scalar2=None, op0=ALU.mult), [cB], [cB])
    dve(lambda e: e.tensor_scalar(out=naq8, in0=pv2[:, R_NAQ:R_NAQ + 1], scalar1=0.125, scalar2=None, op0=ALU.mult), [cB], [cB])

    for l in range(nlayers):
        mps = psA.next()
        mview = mps[:, 0:144].rearrange("p (n s) -> p n s", s=3)
        for g in range(12):
            wt = modw_r.next()
            dma("gpsimd", wt[:], modw_d[l].rearrange("(k p) n -> p k n", p=128)[:, :, g * 512:(g + 1) * 512], [], [wt])
            for nn in range(4):
                n = g * 4 + nn
                for kc in range(8):
                    mm(mview[:, n, :], wt[:, kc, nn * 128:(nn + 1) * 128], siluT[:, :, kc], kc == 0, kc == 7, [wt, cB], [mps])
        for s in range(3):
            dve(lambda e, s=s, l=l, mview=mview: e.tensor_tensor(out=modv[:, l, :, s], in0=mview[:, :, s], in1=pv1[:, l * 48:(l + 1) * 48], op=ALU.add), [mps, cB], [modB])
        for w_, part, grow in ((0, 1, 96), (1, 4, 112)):
            for s in range(3):
                dve(lambda e, l=l, w_=w_, part=part, grow=grow, s=s: e.scalar_tensor_tensor(
                    out=Av[:, l, w_, s, :], in0=modv[:, l, part * 8:(part + 1) * 8, s], scalar=1.0,
                    in1=pv1[:, grow + l * 8:grow + (l + 1) * 8], op0=ALU.add, op1=ALU.mult), [modB, cB], [modB])

    def Acol(l, w_, s, c):
        return Av[:, l, w_, s, c:c + 1]

    def Mcol(l, part, s, c):
        return modv[:, l, part * 8 + c, s:s + 1]

    S.alias(arena_attn_bufs, arena_pro_bufs)
    ev_src = ev_d.rearrange("(k p) n -> p k n", p=128)
    for nbk in range(16):
        w = 128 if nbk < 15 else 64
        dma("gpsimd", ev_s[nbk, :, :, 0:w], ev_src[:, :, nbk * 128:nbk * 128 + w], [], [evB[nbk]])
    dma("gpsimd", uq_sb, uq_d.rearrange("(k p) n -> p k n", p=128), [], [uqB])
    dma("gpsimd", ukv_sb, ukv_d, [], [ukvB])

    def precast_layer(l):
        dma("gpsimd", wout_s[l], wout_d[l], [], [woutB[l]])
        src = fwi_d[l].rearrange("(k p) n -> p k n", p=128)
        for j in range(22):
            dma("gpsimd", wi_s[l, j, :, :, 0:128], src[:, :, j * 128:(j + 1) * 128], [], [wiB[l][j]])
            dma("gpsimd", wi_s[l, j, :, :, 128:256], src[:, :, DFF + j * 128:DFF + (j + 1) * 128], [], [wiB[l][j]])
        srco = fwo_d[l].rearrange("(j p) n -> p j n", p=128)
        for n in range(8):
            dma("gpsimd", wo_s[l, n], srco[:, :, n * 128:(n + 1) * 128], [], [woB[l][n]])

    precast_layer(0)
    if nlayers > 1:
        od_src = od_d.rearrange("(k p) n -> p k n", p=128)
        for nbk in range(24):
            dma("gpsimd", od_s[nbk], od_src[:, :, nbk * 128:(nbk + 1) * 128], [], [odB[nbk]])
        precast_layer(1)

    def load_x(b):
        S.alias(stg_bufs, arena_attn_bufs)
        for t in range(18):
            st = stg_r.next()
            src = x_d[b, t * 128:(t + 1) * 128, :] if t < 16 else ctx_d[b, (t - 16) * 128:(t - 15) * 128, :]
            dma("sync", st[:], src, [], [st])
            tb = t // 4
            for half in range(2):
                ps = psA.next()
                for cc in range(4):
                    c = half * 4 + cc
                    mm(ps[:, cc * 128:(cc + 1) * 128], st[:, c * 128:(c + 1) * 128], identf[:], True, True, [st, cB], [ps])
                o = xT[:, half * 4:(half + 1) * 4, t * 128:(t + 1) * 128]
                i_ = ps[:, :].rearrange("p (c n) -> p c n", c=4)
                wr = [xB[half * 4 + cc][tb] for cc in range(4)]
                if half == 0:
                    dve(lambda e, o=o, i_=i_: e.tensor_copy(out=o, in_=i_), [ps], wr)
                else:
                    act(o, i_, AF.Copy, [ps], wr)
        S.alias(arena_attn_bufs, stg_bufs)

    def store_out(b):
        ops = []
        S.alias(stg_bufs, arena_attn_bufs)
        for t in range(16):
            st = stg_r.next()
            tb = t // 4
            for half in range(2):
                ps = psA.next()
                for cc in range(4):
                    c = half * 4 + cc
                    mm(ps[:, cc * 128:(cc + 1) * 128], xT[:, c, t * 128:(t + 1) * 128], identf[:], True, True, [xB[c][tb], cB], [ps])
                o = st[:, half * 512:(half + 1) * 512]
                if half == 0:
                    dve(lambda e, o=o, ps=ps: e.tensor_copy(out=o, in_=ps[:, :]), [ps], [st])
                else:
                    act(o, ps[:, :], AF.Copy, [ps], [st])
            ops.append(dma("sync", out_d[b, t * 128:(t + 1) * 128, :], st[:], [st], [outB]))
        S.alias(arena_attn_bufs, stg_bufs)
        return ops

    def norm_steps(l, w_, b, tb):
        t0, n = TBLK[tb]
        s = 2 if tb == 4 else b
        sqs = []
        for c in range(4):
            sq = sq_r.next()
            act(sq[:, 0:n], xT[:, c, t0:t0 + n], AF.Square, [xB[c][tb]], [sq])
            sqs.append(sq)
        yield
        ssP = psS.next()
        for c in range(4):
            mm(ssP[:, 0:n], onesb, sqs[c][:, 0:n], c == 0, False, [sqs[c], cB], [ssP])
        sqs = []
        for c in range(4, 8):
            sq = sq_r.next()
            act(sq[:, 0:n], xT[:, c, t0:t0 + n], AF.Square, [xB[c][tb]], [sq])
            sqs.append(sq)
        yield
        for i, c in enumerate(range(4, 8)):
            mm(ssP[:, 0:n], onesb, sqs[i][:, 0:n], False, c == 7, [sqs[i], cB], [ssP])
        sd = ft_r.next()
        act(sd[:, 0:n], ssP[:, 0:n], AF.Ln, [ssP, cB], [sd], bias=epsb, scale=1.0 / D)
        rs = rs_r.next()
        act(rs[:, 0:n], sd[:, 0:n], AF.Exp, [sd], [rs], scale=-0.5)
        for c in range(8):
            tmp = ft_r.next()
            dve(lambda e, tmp=tmp, c=c: e.scalar_tensor_tensor(
                out=tmp[:, 0:n], in0=xT[:, c, t0:t0 + n], scalar=Acol(l, w_, s, c), in1=rs[:, 0:n],
                op0=ALU.mult, op1=ALU.mult), [xB[c][tb], rs, modB], [tmp])
            act(hT[:, c, t0:t0 + n], tmp[:, 0:n], AF.Identity, [tmp, modB], [hB[c][tb]],
                bias=Mcol(l, 0 if w_ == 0 else 3, s, c))

    def norm_mod(l, w_, b, tbs):
        for tb in tbs:
            for _ in norm_steps(l, w_, b, tb):
                pass

    def group_norm(tiles, nfeat, tb, rope):
        t0, n = TBLK[tb]
        ssP = psS.next()
        for i, (ps, M, G, gain, dfn, dbufs) in enumerate(tiles):
            sq = sq_r.next()
            act(sq[0:M, 0:n], ps[0:M, 0:n], AF.Square, [ps], [sq])
            mm(ssP[:, 0:n], G, sq[0:M, 0:n], i == 0, i == len(tiles) - 1, [sq, cB], [ssP])
        yield
        sd = ft_r.next()
        act(sd[:, 0:n], ssP[:, 0:n], AF.Ln, [ssP, cB], [sd], bias=epsb, scale=1.0 / nfeat)
        yield
        rs = rs_r.next()
        act(rs[:, 0:n], sd[:, 0:n], AF.Exp, [sd], [rs], scale=-0.5)
        yield
        todo = []
        for (ps, M, G, gain, dfn, dbufs), rp in zip(tiles, rope):
            if not rp:
                dve(lambda e, ps=ps, M=M, gain=gain, dfn=dfn: e.scalar_tensor_tensor(
                    out=dfn(t0, n), in0=ps[0:M, 0:n], scalar=gain, in1=rs[0:M, 0:n], op0=ALU.mult, op1=ALU.mult),
                    [ps, rs, cB], dbufs)
                continue
            xn = ft_r.next()
            dve(lambda e, ps=ps, M=M, gain=gain, xn=xn: e.scalar_tensor_tensor(
                out=xn[0:M, 0:n], in0=ps[0:M, 0:n], scalar=gain, in1=rs[0:M, 0:n], op0=ALU.mult, op1=ALU.mult),
                [ps, rs, cB], [xn])
            todo.append((M, dfn, dbufs, xn))
        if not todo:
            return
        yield
        st2 = []
        for (M, dfn, dbufs, xn) in todo:
            xb = sq_r.next()
            act(xb[0:M, 0:n], xn[0:M, 0:n], AF.Copy, [xn], [xb])
            pp = psS.next()
            mm(pp[0:M, 0:n], permb[0:M, 0:M], xb[0:M, 0:n], True, True, [xb, cB], [pp])
            st2.append((M, dfn, dbufs, xn, pp))
        yield
        st3 = []
        for (M, dfn, dbufs, xn, pp) in st2:
            dve(lambda e, xn=xn, M=M: e.tensor_tensor(out=xn[0:M, 0:n], in0=xn[0:M, 0:n], in1=ropecs[0:M, 0, t0:t0 + n], op=ALU.mult),
                [xn, ropeB], [xn])
            t2 = ft_r.next()
            dve(lambda e, pp=pp, t2=t2, M=M: e.tensor_tensor(out=t2[0:M, 0:n], in0=pp[0:M, 0:n], in1=ropecs[0:M, 1, t0:t0 + n], op=ALU.mult),
                [pp, ropeB], [t2])
            st3.append((M, dfn, dbufs, xn, t2))
        yield
        for (M, dfn, dbufs, xn, t2) in st3:
            dve(lambda e, xn=xn, t2=t2, M=M, dfn=dfn: e.tensor_tensor(out=dfn(t0, n), in0=xn[0:M, 0:n], in1=t2[0:M, 0:n], op=ALU.add),
                [xn, t2], dbufs)

    def run_chains(*gens):
        gens = list(gens)
        while gens:
            for g in list(gens):
                try:
                    next(g)
                except StopIteration:
                    gens.remove(g)

    def load_wtile(scr, scrB, ring=None):
        wt = (ring or wproj_r).next()
        dma("sync", wt[:], scr.rearrange("p k n -> p (k n)"), [scrB], [wt])
        return wt

    def proj_fm(wt, M, tb, col0=0, wstride=128):
        t0, n = TBLK[tb]
        ps = psA.next()
        for kc in range(8):
            mm(ps[0:M, 0:n], wt[:, kc * wstride + col0:kc * wstride + col0 + M], hT[:, kc, t0:t0 + n], kc == 0, kc == 7,
               [wt, hB[kc][tb]], [ps])
        return ps

    def proj_v_tm(wt, slot, ntiles=18):
        for t in range(ntiles):
            tb = t // 4
            ps = psS.next()
            for kc in range(8):
                mm(ps[:, 0:128], hT[:, kc, t * 128:(t + 1) * 128], wt[:, kc * 128:(kc + 1) * 128], kc == 0, kc == 7,
                   [wt, hB[kc][tb]], [ps])
            o = slots[:, slot, t * 128:(t + 1) * 128]
            if t % 2 == 0:
                dve(lambda e, o=o, ps=ps: e.tensor_copy(out=o, in_=ps[:, 0:128]), [ps], [slB[slot][tb]])
            else:
                act(o, ps[:, 0:128], AF.Copy, [ps], [slB[slot][tb]])

    def attn_core(units, scale, group=2, need_sm=True):
        O = psA.next()
        Sm = psA.next() if need_sm else None
        groups = [units[i:i + group] for i in range(0, len(units), group)]
        sts = {}

        def emit_qk(gi):
            for ui, u in enumerate(groups[gi]):
                st = psS.next()
                u[1](st)
                sts[(gi, ui)] = st
        emit_qk(0)
        if len(groups) > 1:
            emit_qk(1)
        n_units = len(units)
        done = 0
        for gi, g in enumerate(groups):
            ps = []
            for ui, u in enumerate(g):
                st = sts.pop((gi, ui))
                p = p_r.next()
                act(p[:, 0:u[0]], st[:, 0:u[0]], AF.Exp, [st], [p], scale=scale)
                if len(u) > 3 and u[3] is not None:
                    u[3](p)
                ps.append(p)
            for ui in reversed(range(len(g))):
                first = done == 0
                done += 1
                g[ui][2](ps[ui], O, Sm, first, done == n_units)
            if gi + 2 < len(groups):
                emit_qk(gi + 2)
        return O, Sm

    def tbs_of(t0, n):
        return sorted(set([t0 // 512, (t0 + n - 1) // 512]))

    def wout_accum(l, wt, yb, tb, b, q0=None, nq=None):
        t0, n = TBLK[tb]
        if q0 is not None:
            t0, n = q0, nq
        s = 2 if tb == 4 else b
        tbl_ = tbs_of(t0, n)
        for nchk in range(8):
            ps = psS.next()
            mm(ps[:, 0:n], wt[:, nchk * 128:(nchk + 1) * 128], yb[:, 0:n], True, True, [wt, yb], [ps])
            xbs = [xB[nchk][t_] for t_ in tbl_]
            dve(lambda e, ps=ps, nchk=nchk, t0=t0, n=n, s=s: e.scalar_tensor_tensor(
                out=xT[:, nchk, t0:t0 + n], in0=ps[:, 0:n], scalar=Mcol(l, 2, s, nchk), in1=xT[:, nchk, t0:t0 + n],
                op0=ALU.mult, op1=ALU.add), [ps, modB] + xbs, xbs)

    def ffn(l, b, tbs):
        S.alias(arena_ffn_bufs, arena_attn_bufs)
        tbs = list(tbs)
        for _ in norm_steps(l, 1, b, tbs[0]):
            pass
        for ti, tb in enumerate(tbs):
            nxt = norm_steps(l, 1, b, tbs[ti + 1]) if ti + 1 < len(tbs) else None
            t0, n = TBLK[tb]
            s = 2 if tb == 4 else b
            for j in range(22):
                wt = load_wtile(wi_s[l, j], wiB[l][j], wA_r)
                G = psS.next()
                U = psS.next()
                for kc in range(8):
                    mm(G[:, 0:n], wt[:, kc * 256:kc * 256 + 128], hT[:, kc, t0:t0 + n], kc == 0, kc == 7, [wt, hB[kc][tb]], [G])
                for kc in range(8):
                    mm(U[:, 0:n], wt[:, kc * 256 + 128:kc * 256 + 256], hT[:, kc, t0:t0 + n], kc == 0, kc == 7, [wt, hB[kc][tb]], [U])
                sg = ft_r.next()
                act(sg[:, 0:n], G[:, 0:n], AF.Silu, [G], [sg])
                dve(lambda e, sg=sg, U=U, j=j, n=n: e.tensor_tensor(out=a_sb[:, j, 0:n], in0=sg[:, 0:n], in1=U[:, 0:n], op=ALU.mult),
                    [sg, U], [aB[j]])
            for nchk in range(8):
                if nxt is not None and nchk in (0, 2, 4):
                    next(nxt, None)
                wt = load_wtile(wo_s[l, nchk], woB[l][nchk], wB_r)
                ps = psA.next()
                for j in range(22):
                    mm(ps[:, 0:n], wt[:, j * 128:(j + 1) * 128], a_sb[:, j, 0:n], j == 0, j == 21, [wt, aB[j]], [ps])
                dve(lambda e, ps=ps, nchk=nchk, t0=t0, n=n, s=s: e.scalar_tensor_tensor(
                    out=xT[:, nchk, t0:t0 + n], in0=ps[:, 0:n], scalar=Mcol(l, 5, s, nchk), in1=xT[:, nchk, t0:t0 + n],
                    op0=ALU.mult, op1=ALU.add), [ps, modB, xB[nchk][tb]], [xB[nchk][tb]])
        S.alias(arena_attn_bufs, arena_ffn_bufs)

    ALLCH = list(range(18))
    CTXCH = [16, 17]

    def da_qblock(tb, wo_t, b):
        l = 0
        SQ, SK, SV = 0, 1, 2
        t0, nq = TBLK[tb]
        chunks = ALLCH if tb < 4 else CTXCH
        ons = []
        for i in range(2):
            def mk_unit(kc, i=i):
                def qk_fn(st):
                    mm(st[:, 0:nq], slots[64 * i:64 * i + 64, SK, kc * 128:(kc + 1) * 128], slots[64 * i:64 * i + 64, SQ, t0:t0 + nq],
                       True, True, [slB[SK][kc // 4], slB[SQ][tb]], [st])

                def av_fn(p, O, Sm, first, last):
                    mm(O[:, 0:nq], slots[:, SV, kc * 128:(kc + 1) * 128], p[:, 0:nq], first, last, [p, slB[SV][kc // 4]], [O])
                    mm(Sm[:, 0:nq], onesb, p[:, 0:nq], first, last, [p, cB], [Sm])
                return (nq, qk_fn, av_fn)
            O, Sm = attn_core([mk_unit(kc) for kc in chunks], 0.125)
            rs = rs_r.next()
            dve(lambda e, rs=rs, Sm=Sm: e.reciprocal(out=rs[:, 0:nq], in_=Sm[:, 0:nq]), [Sm], [rs])
            on = ft_r.next()
            dve(lambda e, on=on, O=O, rs=rs: e.tensor_tensor(out=on[:, 0:nq], in0=O[:, 0:nq], in1=rs[:, 0:nq], op=ALU.mult), [O, rs], [on])
            ons.append(on)
        dd = dd_r.next()
        dve(lambda e: e.scalar_tensor_tensor(out=dd[:, 0:nq], in0=ons[1][:, 0:nq], scalar=neglam, in1=ons[0][:, 0:nq],
                                             op0=ALU.mult, op1=ALU.add), [ons[0], ons[1], cB], [dd])

        def tail():
            sq = sq_r.next()
            act(sq[:, 0:nq], dd[:, 0:nq], AF.Square, [dd], [sq])
            ssP = psS.next()
            mm(ssP[:, 0:nq], onesb, sq[:, 0:nq], True, True, [sq, cB], [ssP])
            sd = ft_r.next()
            act(sd[:, 0:nq], ssP[:, 0:nq], AF.Ln, [ssP, cB], [sd], bias=epsb, scale=1.0 / 128)
            rs2 = rs_r.next()
            act(rs2[:, 0:nq], sd[:, 0:nq], AF.Exp, [sd], [rs2], scale=-0.5)
            yb = y_r.next()
            dve(lambda e: e.scalar_tensor_tensor(out=yb[:, 0:nq], in0=dd[:, 0:nq], scalar=og08, in1=rs2[:, 0:nq],
                                                 op0=ALU.mult, op1=ALU.mult), [dd, rs2, cB], [yb])
            wout_accum(l, wo_t, yb, tb, b)
        return tail

    def mla_qblock(tb, wo_t, b):
        l = 0
        SCQ0, SCQ1, SCKV, SKR, SQN, SQR, SKN, SMV = range(8)
        t0, nq = TBLK[tb]
        chunks = ALLCH if tb < 4 else CTXCH

        def mk_unit(kc):
            def qk_fn(st):
                mm(st[:, 0:nq], slots[:, SKN, kc * 128:(kc + 1) * 128], slots[:, SQN, t0:t0 + nq], True, False,
                   [slB[SKN][kc // 4], slB[SQN][tb]], [st])
                mm(st[:, 0:nq], slots[0:64, SKR, kc * 128:(kc + 1) * 128], slots[0:64, SQR, t0:t0 + nq], False, True,
                   [slB[SKR][kc // 4], slB[SQR][tb]], [st])

            def av_fn(p, O, Sm, first, last):
                mm(O[:, 0:nq], slots[:, SMV, kc * 128:(kc + 1) * 128], p[:, 0:nq], first, last, [p, slB[SMV][kc // 4]], [O])
                mm(Sm[:, 0:nq], onesb, p[:, 0:nq], first, last, [p, cB], [Sm])
            return (nq, qk_fn, av_fn)
        O, Sm = attn_core([mk_unit(kc) for kc in chunks], 192.0 ** -0.5)
        rs = rs_r.next()
        dve(lambda e: e.reciprocal(out=rs[:, 0:nq], in_=Sm[:, 0:nq]), [Sm], [rs])
        yb = y_r.next()
        dve(lambda e: e.tensor_tensor(out=yb[:, 0:nq], in0=O[:, 0:nq], in1=rs[:, 0:nq], op=ALU.mult), [O, rs], [yb])
        return lambda: wout_accum(l, wo_t, yb, tb, b)

    def mla_head_proj(h, tb):
        SCQ0, SCQ1, SCKV, SKR, SQN, SQR, SKN, SMV = range(8)
        t0, n = TBLK[tb]
        pn = psA.next()
        for kc in range(2):
            mm(pn[:, 0:n], uq_sb[:, kc, 192 * h:192 * h + 128], slots[:, SCQ0 + kc, t0:t0 + n], kc == 0, kc == 1, [uqB, slB[SCQ0 + kc][tb]], [pn])
        pr = psA.next()
        for kc in range(2):
            mm(pr[0:64, 0:n], uq_sb[:, kc, 192 * h + 128:192 * h + 192], slots[:, SCQ0 + kc, t0:t0 + n], kc == 0, kc == 1, [uqB, slB[SCQ0 + kc][tb]], [pr])
        pk = psA.next()
        mm(pk[:, 0:n], ukv_sb[:, 256 * h:256 * h + 128], slots[:, SCKV, t0:t0 + n], True, True, [ukvB, slB[SCKV][tb]], [pk])
        run_chains(
            group_norm([(pn, 128, onesb, pv2[:, R_MQN:R_MQN + 1], lambda t0, n: slots[:, SQN, t0:t0 + n], [slB[SQN][tb]]),
                        (pr, 64, onesb[0:64, :], pv2[0:64, R_MQR:R_MQR + 1], lambda t0, n: slots[0:64, SQR, t0:t0 + n], [slB[SQR][tb]])],
                       192, tb, [False, tb < 4]),
            group_norm([(pk, 128, onesb, pv2[:, R_MK:R_MK + 1], lambda t0, n: slots[:, SKN, t0:t0 + n], [slB[SKN][tb]])], 128, tb, [False]))

    import os as _os
    NA_OLDV = _os.environ.get("NA_OLDV", "1") == "1"
    NA_KEEPSUM = NA_OLDV or _os.environ.get("NA_KEEPSUM", "0") == "1"
    NA_BLOCKS = [(0, 4), (4, 8), (12, 8), (20, 8), (28, 4)]
    if _os.environ.get("NA_R4", "0") == "1":
        NA_BLOCKS = [(4 * i, 4) for i in range(8)]

    def na_qblock(m, blk, wo_t, nab, b):
        l = 1
        SQ, SK, SV0 = 0, 1, 2
        q0, R = NA_BLOCKS[blk]
        t0, nq = q0 * 64, R * 64
        qtb = tbs_of(t0, nq)
        qbufs = [slB[SQ][t_] for t_ in qtb]
        if q0 == 0 or q0 == 28:
            lrows = [0, 2, 4, 6] if q0 == 0 else [24, 26, 28, 30]
            nbase = lambda r: (7 - (r - q0) - 1) * 64
        else:
            lrows = [q0 - 4 + 2 * i for i in range(R // 2 + 4)]
            nbase = lambda r: (14 + 10 - (r - q0)) * 64
        chunks = [("l", r) for r in lrows] + [("c", 32), ("c", 34)]
        per_unit = 512 // nq
        yb = y_r.next()
        for e_ in range(2):
            pl, ph = 64 * e_, 64 * e_ + 64
            ol, oh = 64 - pl, 128 - pl

            def mk_unit(grp, pl=pl, ph=ph, e_=e_):
                def qk_fn(st):
                    for hf, (kind, r) in enumerate(grp):
                        kc = r // 2
                        mm(st[:, hf * nq:(hf + 1) * nq], slots[pl:ph, SK, kc * 128:(kc + 1) * 128], slots[pl:ph, SQ, t0:t0 + nq], True, True,
                           [slB[SK][kc // 4]] + qbufs, [st])

                def post_fn(p):
                    for hf, (kind, r) in enumerate(grp):
                        if kind == "l":
                            nb_ = nbase(r)
                            dve(lambda e, p=p, hf=hf, nb_=nb_: e.tensor_tensor(out=p[:, hf * nq:(hf + 1) * nq], in0=p[:, hf * nq:(hf + 1) * nq],
                                                                              in1=nab[:, e_, nb_:nb_ + nq], op=ALU.mult), [p, nab], [p])

                def av_fn(p, O, Sm, first, last):
                    for hf, (kind, r) in enumerate(grp):
                        kc = r // 2
                        vs = SV0 if NA_OLDV else SV0 + e_
                        mm(O[:, 0:nq], slots[:, vs, kc * 128:(kc + 1) * 128], p[:, hf * nq:(hf + 1) * nq],
                           first and hf == 0, last and hf == len(grp) - 1, [p, slB[vs][kc // 4]], [O])
                        if NA_KEEPSUM:
                            mm(Sm[:, 0:nq], onesb, p[:, hf * nq:(hf + 1) * nq], first and hf == 0, last and hf == len(grp) - 1, [p, cB], [Sm])
                return (nq * len(grp), qk_fn, av_fn, post_fn)
            O, Sm = attn_core([mk_unit(chunks[i:i + per_unit]) for i in range(0, len(chunks), per_unit)], 1.0, need_sm=NA_KEEPSUM)
            rs = rs_r.next()
            lnv = ft_r.next()
            if NA_KEEPSUM:
                act(lnv[pl:ph, 0:nq], Sm[pl:ph, 0:nq], AF.Ln, [Sm], [lnv])
            else:
                act(lnv[pl:ph, 0:nq], O[ol:oh, 0:nq], AF.Ln, [O], [lnv])
            act(rs[pl:ph, 0:nq], lnv[pl:ph, 0:nq], AF.Exp, [lnv], [rs], scale=-1.0)
            dve(lambda e, O=O, rs=rs, pl=pl, ph=ph: e.tensor_tensor(out=yb[pl:ph, 0:nq], in0=O[pl:ph, 0:nq], in1=rs[pl:ph, 0:nq], op=ALU.mult),
                [O, rs], [yb])
        return lambda: wout_accum(l, wo_t, yb, qtb[0], b, q0=t0, nq=nq)

    def proj_v_na(wt):
        for t in range(18):
            tb = t // 4
            ps = psS.next()
            for kc in range(8):
                mm(ps[:, 0:128], hT[:, kc, t * 128:(t + 1) * 128], wt[:, kc * 128:(kc + 1) * 128], kc == 0, kc == 7,
                   [wt, hB[kc][tb]], [ps])
            a0 = 2 * T + t * 128
            o = arena[:, a0:a0 + 2 * (T + 64)].rearrange("p (s c) -> p s c", s=2)[:, :, 0:64]
            i_ = ps[:, 0:128].rearrange("p (s c) -> p s c", s=2)
            if t % 2 == 0:
                dve(lambda e, o=o, i_=i_: e.tensor_copy(out=o, in_=i_), [ps], [slB[2][tb], slB[3][tb]])
            else:
                act(o, i_, AF.Copy, [ps], [slB[2][tb], slB[3][tb]])

    def qblocks(need_ctx):
        return [0, 1, 2, 3, 4] if need_ctx else [0, 1, 2, 3]

    class Pend:
        def __init__(self):
            self.t = None

        def push(self, tail):
            old = self.t
            self.t = tail
            if old is not None:
                old()

        def flush(self):
            if self.t is not None:
                self.t()
            self.t = None

    def layer0(b):
        l = 0
        S.alias([ropeB] + [bb for sl in slB[4:8] for bb in sl], nab_bufs)
        dma("sync", ropecs, rope_d, [], [ropeB])
        norm_mod(l, 0, b, range(5))
        gq = pv2[:, R_DAQ:R_DAQ + 1]
        gk = pv2[:, R_DAK:R_DAK + 1]
        SQ, SK, SV = 0, 1, 2
        pend = Pend()
        for h in range(4):
            wq = load_wtile(ev_s[h], evB[h])
            wk = load_wtile(ev_s[4 + h], evB[4 + h])
            wv = load_wtile(ev_s[8 + h], evB[8 + h])
            for tb in range(5):
                rp = tb < 4
                psq = proj_fm(wq, 128, tb)
                psk = proj_fm(wk, 128, tb)
                run_chains(group_norm([(psq, 128, blk64, gq, lambda t0, n: slots[:, SQ, t0:t0 + n], [slB[SQ][tb]])], 64, tb, [rp]),
                           group_norm([(psk, 128, blk64, gk, lambda t0, n: slots[:, SK, t0:t0 + n], [slB[SK][tb]])], 64, tb, [rp]))
            proj_v_tm(wv, SV)
            pend.flush()
            wo_t = wproj_r.next()
            dma("sync", wo_t[:], wout_s[l, h * 128:(h + 1) * 128, :], [woutB[l]], [wo_t])
            for tb in range(5):
                pend.push(da_qblock(tb, wo_t, b))
        pend.flush()
        SCQ0, SCQ1, SCKV, SKR, SQN, SQR, SKN, SMV = range(8)
        wcq0 = load_wtile(ev_s[12], evB[12])
        wcq1 = load_wtile(ev_s[13], evB[13])
        wckv = load_wtile(ev_s[14], evB[14])
        wkr = load_wtile(ev_s[15], evB[15])
        for tb in range(5):
            p0 = proj_fm(wcq0, 128, tb)
            p1 = proj_fm(wcq1, 128, tb)
            p2 = proj_fm(wckv, 128, tb)
            p3 = proj_fm(wkr, 64, tb)
            run_chains(
                group_norm([(p0, 128, onesb, pv2[:, R_QA0:R_QA0 + 1], lambda t0, n: slots[:, SCQ0, t0:t0 + n], [slB[SCQ0][tb]]),
                            (p1, 128, onesb, pv2[:, R_QA1:R_QA1 + 1], lambda t0, n: slots[:, SCQ1, t0:t0 + n], [slB[SCQ1][tb]])], 256, tb, [False, False]),
                group_norm([(p2, 128, onesb, pv2[:, R_KVA:R_KVA + 1], lambda t0, n: slots[:, SCKV, t0:t0 + n], [slB[SCKV][tb]])], 128, tb, [False]))
            run_chains(
                group_norm([(p3, 64, onesb[0:64, :], pv2[0:64, R_MKR:R_MKR + 1], lambda t0, n: slots[0:64, SKR, t0:t0 + n], [slB[SKR][tb]])], 64, tb, [tb < 4]))
        for h in range(4):
            for tb in range(5):
                mla_head_proj(h, tb)
            for t in range(18):
                tb = t // 4
                ps = psS.next()
                mm(ps[:, 0:128], slots[:, SCKV, t * 128:(t + 1) * 128], ukv_sb[:, 256 * h + 128:256 * h + 256], True, True, [ukvB, slB[SCKV][tb]], [ps])
                o = slots[:, SMV, t * 128:(t + 1) * 128]
                if t % 2 == 0:
                    dve(lambda e, o=o, ps=ps: e.tensor_copy(out=o, in_=ps[:, 0:128]), [ps], [slB[SMV][tb]])
                else:
                    act(o, ps[:, 0:128], AF.Copy, [ps], [slB[SMV][tb]])
            pend.flush()
            wo_t = wproj_r.next()
            dma("sync", wo_t[:], wout_s[l, (4 + h) * 128:(5 + h) * 128, :], [woutB[l]], [wo_t])
            for tb in range(5):
                pend.push(mla_qblock(tb, wo_t, b))
        pend.flush()
        ffn(l, b, range(5))

    def layer1(b):
        l = 1
        import os as _os
        S.alias(nab_bufs, [ropeB] + [bb for sl in slB[4:8] for bb in sl])
        norm_mod(l, 0, b, range(5))
        SQ, SK, SV = 0, 1, 2
        if (not NA_OLDV) or _os.environ.get('NA_T1', '0') == '1':
            for t in range(18):
                o0 = slots[:, 2, t * 128 + 64:(t + 1) * 128]
                o1 = slots[:, 3, t * 128:t * 128 + 64]
                dve(lambda e, o0=o0: e.tensor_copy(out=o0, in_=onesb[:, 0:64]), [cB], [slB[2][t // 4]])
                act(o1, onesb[:, 0:64], AF.Copy, [cB], [slB[3][t // 4]])
        gq = naq8
        gk = pv2[:, R_NAK:R_NAK + 1]
        pend = Pend()
        for m in range(8):
            wq = load_wtile(od_s[m], odB[m])
            wk = load_wtile(od_s[8 + m], odB[8 + m])
            wv = load_wtile(od_s[16 + m], odB[16 + m])
            nab = nab_r.next()
            dma("gpsimd", nab[:], nab_d[2 * m:2 * m + 2].rearrange("h p f -> p h f"), [], [nab])
            for e_ in range(2):
                act(nab[:, e_, :], nab[:, e_, :], AF.Exp, [nab], [nab])
            for tb in range(5):
                psk = proj_fm(wk, 128, tb)
                ch = [group_norm([(psk, 128, blk64, gk, lambda t0, n: slots[:, SK, t0:t0 + n], [slB[SK][tb]])], 64, tb, [False])]
                if tb < 4:
                    psq = proj_fm(wq, 128, tb)
                    ch.append(group_norm([(psq, 128, blk64, gq, lambda t0, n: slots[:, SQ, t0:t0 + n], [slB[SQ][tb]])], 64, tb, [False]))
                run_chains(*ch)
            if NA_OLDV and _os.environ.get('NA_T2', '0') != '1':
                proj_v_tm(wv, 2)
            else:
                proj_v_na(wv)
            pend.flush()
            wo_t = wproj_r.next()
            dma("sync", wo_t[:], wout_s[l, m * 128:(m + 1) * 128, :], [woutB[l]], [wo_t])
            import os as _os
            for blk in [int(v) for v in _os.environ.get("NA_DEBUG_BLOCKS", ",".join(str(i) for i in range(len(NA_BLOCKS)))).split(",")]:
                pend.push(na_qblock(m, blk, wo_t, nab, b))
        pend.flush()
        ffn(l, b, range(4))

    final_ops = []
    for b in range(nb):
        load_x(b)
        if dbg == "x0":
            break
        layer0(b)
        if dbg == "l0" or nlayers == 1:
            break
        layer1(b)
        final_ops += store_out(b)
    if dbg is not None:
        for c in range(8):
            final_ops.append(dma("sync", dbg_d[:, c, :], xT[:, c, :], [xB[c][tb] for tb in range(5)], [outB]))
    S.emit(final_ops)
    return nc


_NC_CACHE = {}


def kernel(**inputs):
    inp = {k: np.ascontiguousarray(np.asarray(v)) for k, v in inputs.items()}
    if "nc" not in _NC_CACHE:
        _NC_CACHE["nc"] = build_nc()
    nc = _NC_CACHE["nc"]
    cmat = _const_mats()
    rope = _rope_tables()
    nab = _na_bias_tables(inp["na_rpb"][0])
    lamv = np.stack([inp["da_lq1"][0], inp["da_lk1"][0], inp["da_lq2"][0], inp["da_lk2"][0]]).astype(np.float32)
    in_maps = []
    for core in range(NCORES):
        b0 = core * NB
        pm1, pm2 = _pack_params(inp, b0)
        in_maps.append({
            "x": inp["x"][b0:b0 + NB], "ctx": inp["ctx"][b0:b0 + NB],
            "mod_w": inp["mod_w"], "w_out": inp["w_out"], "ffn_w_in": inp["ffn_w_in"], "ffn_w_out": inp["ffn_w_out"],
            "ev_w_in": inp["ev_w_in"][0], "mla_w_uq": inp["mla_w_uq"][0], "mla_w_ukv": inp["mla_w_ukv"][0],
            "od_w_in": inp["od_w_in"][0], "lamv": lamv, "pm1": pm1, "pm2": pm2, "cmat": cmat, "ropecs": rope, "nabias": nab,
        })
    res = run_bass_kernel_spmd(nc, in_maps, core_ids=list(range(NCORES)))
    out = np.concatenate([np.asarray(r["out"]) for r in res.results], axis=0)
    return out.astype(np.float32)
```
